# Optimizing a Trainium2 kernel written in Bass

```python
import math
import jax, jax.numpy as jnp
from jax import lax
import numpy as np

D_MODEL = 1024
BATCH = 8
SEQ = 2048
DEPTH = 4
DEC_BATCH = 128
DEC_SEQ = 4
PAST_LEN = 8192
PAGE_SIZE = 128

N_MIXERS = 2
N_DELTA_LAYERS = (DEPTH + 1) // 2
N_SWA_LAYERS = DEPTH // 2

DN_QK_HEADS = 8
DN_V_HEADS = 16
DN_V_PER_QK = DN_V_HEADS // DN_QK_HEADS
DN_HEAD_K = 128
DN_HEAD_V = 128
DN_KEY_DIM = DN_QK_HEADS * DN_HEAD_K
DN_VAL_DIM = DN_V_HEADS * DN_HEAD_V
DN_CONV_DIM = 2 * DN_KEY_DIM + DN_VAL_DIM
DN_CONV = 4
DN_CHUNK = 64
DN_IN_DIM = DN_CONV_DIM + DN_VAL_DIM + 2 * DN_V_HEADS

SWA_HEADS = 16
SWA_KV_HEADS = 4
SWA_GROUPS = SWA_HEADS // SWA_KV_HEADS
SWA_HEAD_DIM = 64
SWA_QKV_DIM = (SWA_HEADS + 2 * SWA_KV_HEADS) * SWA_HEAD_DIM
SWA_SCALE = SWA_HEAD_DIM ** -0.5
WINDOW = 128

D_FF = -(-8 * D_MODEL // (3 * 256)) * 256

RMS_EPS = 1e-6
L2_EPS = 1e-6

kernel_name = "hybrid_gdn_swa_sink_alibi_step"


def rmsnorm(x, w):
    xf = x.astype(jnp.float32)
    y = xf * lax.rsqrt(jnp.mean(xf * xf, -1, keepdims=True) + RMS_EPS)
    return (y * w.astype(jnp.float32)).astype(x.dtype)


def l2norm(x):
    return x * lax.rsqrt(jnp.sum(x * x, -1, keepdims=True) + L2_EPS)


def swiglu(h, w_gu, w_down):
    gu = h @ w_gu
    return (jax.nn.silu(gu[..., :D_FF]) * gu[..., D_FF:]) @ w_down


def short_conv(u, buf, w):
    L = u.shape[1]
    full = jnp.concatenate([buf.astype(u.dtype), u], 1)
    y = sum(full[:, k:k + L] * w[k] for k in range(DN_CONV))
    return jax.nn.silu(y), full[:, L:]


def gated_delta_chunked(q, k, v, g, beta, s0):
    B, L, H, DK = q.shape
    C = min(DN_CHUNK, L)
    pad = (-L) % C
    if pad:
        pw = ((0, 0), (0, pad), (0, 0), (0, 0))
        q, k, v = jnp.pad(q, pw), jnp.pad(k, pw), jnp.pad(v, pw)
        g, beta = jnp.pad(g, pw[:3]), jnp.pad(beta, pw[:3])
    N = (L + pad) // C

    def blk(t):
        return jnp.moveaxis(t.reshape((B, N, C) + t.shape[2:]), 3, 1)

    q, k, v, g, beta = blk(q), blk(k), blk(v), blk(g), blk(beta)
    G = jnp.cumsum(g, -1)
    idx = jnp.arange(C)
    causal = idx[:, None] >= idx[None, :]
    strict = idx[:, None] > idx[None, :]
    diff = G[..., :, None] - G[..., None, :]
    decay = jnp.where(causal, jnp.exp(jnp.where(causal, diff, 0.0)), 0.0)
    kb = k * beta[..., None]
    Bm = jnp.where(strict, jnp.einsum('bhncd,bhnsd->bhncs', kb, k) * decay, 0.0)
    M = -Bm
    T = jnp.eye(C, dtype=jnp.float32) + M
    Mp = M
    for _ in range(1, max(1, (C - 1).bit_length())):
        Mp = Mp @ Mp
        T = T + T @ Mp
    u = jnp.einsum('bhncs,bhnse->bhnce', T, v * beta[..., None])
    w = jnp.einsum('bhncs,bhnsd->bhncd', T, kb * jnp.exp(G)[..., None])
    aqk = jnp.einsum('bhncd,bhnsd->bhncs', q, k) * decay

    def step(S, xs):
        qc, kc, uc, wc, Gc, ac = xs
        unew = uc - jnp.einsum('bhcd,bhde->bhce', wc, S)
        o = (jnp.einsum('bhcd,bhde->bhce', qc * jnp.exp(Gc)[..., None], S)
             + jnp.einsum('bhcs,bhse->bhce', ac, unew))
        Gl = Gc[..., -1:]
        S = S * jnp.exp(Gl)[..., None] + jnp.einsum('bhcd,bhce->bhde', kc * jnp.exp(Gl - Gc)[..., None], unew)
        return S, o

    xs = tuple(jnp.moveaxis(t, 2, 0) for t in (q, k, u, w, G, aqk))
    s_new, o = lax.scan(step, s0, xs)
    o = jnp.transpose(o, (1, 0, 3, 2, 4)).reshape(B, N * C, H, -1)[:, :L]
    return o, s_new


def deltanet_mixer(h, conv_buf, s0, w_in, conv_w, a_log, dt_bias, norm_w, w_out):
    B, L, _ = h.shape
    f32 = jnp.float32
    proj = h @ w_in
    o1 = DN_CONV_DIM
    o2 = o1 + DN_VAL_DIM
    o3 = o2 + DN_V_HEADS
    qkv, new_buf = short_conv(proj[..., :o1], conv_buf, conv_w)
    z, b, a = proj[..., o1:o2], proj[..., o2:o3], proj[..., o3:]
    qkv = qkv.astype(f32)
    q = qkv[..., :DN_KEY_DIM].reshape(B, L, DN_QK_HEADS, DN_HEAD_K)
    k = qkv[..., DN_KEY_DIM:2 * DN_KEY_DIM].reshape(B, L, DN_QK_HEADS, DN_HEAD_K)
    v = qkv[..., 2 * DN_KEY_DIM:].reshape(B, L, DN_V_HEADS, DN_HEAD_V)
    q = jnp.repeat(l2norm(q) * DN_HEAD_K ** -0.5, DN_V_PER_QK, axis=2)
    k = jnp.repeat(l2norm(k), DN_V_PER_QK, axis=2)
    beta = jax.nn.sigmoid(b.astype(f32))
    g = -jnp.exp(a_log.astype(f32)) * jax.nn.softplus(a.astype(f32) + dt_bias.astype(f32))
    o, s_new = gated_delta_chunked(q, k, v, g, beta, s0.astype(f32))
    o = o * lax.rsqrt(jnp.mean(o * o, -1, keepdims=True) + RMS_EPS) * norm_w.astype(f32)
    o = o * jax.nn.silu(z.astype(f32).reshape(B, L, DN_V_HEADS, DN_HEAD_V))
    out = o.reshape(B, L, DN_VAL_DIM).astype(h.dtype) @ w_out
    return out, new_buf, s_new


def alibi_slopes():
    s = 2.0 ** (-8.0 * np.arange(1, SWA_HEADS + 1) / SWA_HEADS)
    return jnp.asarray(s, dtype=jnp.float32).reshape(SWA_KV_HEADS, SWA_GROUPS, 1, 1)


def swa_project(h, w_qkv, b_qkv):
    B, L, _ = h.shape
    qkv = h @ w_qkv + b_qkv
    nq = SWA_HEADS * SWA_HEAD_DIM
    nk = SWA_KV_HEADS * SWA_HEAD_DIM
    q = qkv[..., :nq].reshape(B, L, SWA_KV_HEADS, SWA_GROUPS, SWA_HEAD_DIM)
    k = qkv[..., nq:nq + nk].reshape(B, L, SWA_KV_HEADS, SWA_HEAD_DIM)
    v = qkv[..., nq + nk:].reshape(B, L, SWA_KV_HEADS, SWA_HEAD_DIM)
    return q, k, v


def sink_probs(s, rel, mask, sinks):
    s = s.astype(jnp.float32) * SWA_SCALE - alibi_slopes() * rel.astype(jnp.float32)
    s = jnp.where(mask, s, -jnp.inf)
    sk = sinks.astype(jnp.float32).reshape(SWA_KV_HEADS, SWA_GROUPS, 1, 1)
    m = jnp.maximum(jnp.max(s, -1, keepdims=True), sk)
    p = jnp.exp(s - m)
    return p / (jnp.sum(p, -1, keepdims=True) + jnp.exp(sk - m))


def swa_prompt(h, w_qkv, b_qkv, sinks, w_o, b_o):
    B, L, _ = h.shape
    q, k, v = swa_project(h, w_qkv, b_qkv)
    W = WINDOW
    pad = (-L) % W
    qp, kp, vp = q, k, v
    if pad:
        qp = jnp.pad(q, ((0, 0), (0, pad), (0, 0), (0, 0), (0, 0)))
        kp = jnp.pad(k, ((0, 0), (0, pad), (0, 0), (0, 0)))
        vp = jnp.pad(v, ((0, 0), (0, pad), (0, 0), (0, 0)))
    nb = (L + pad) // W
    qb = qp.reshape(B, nb, W, SWA_KV_HEADS, SWA_GROUPS, SWA_HEAD_DIM)
    kb = kp.reshape(B, nb, W, SWA_KV_HEADS, SWA_HEAD_DIM)
    vb = vp.reshape(B, nb, W, SWA_KV_HEADS, SWA_HEAD_DIM)
    prev = lambda t: jnp.concatenate([jnp.zeros_like(t[:, :1]), t[:, :-1]], 1)
    kk = jnp.concatenate([prev(kb), kb], 2)
    vv = jnp.concatenate([prev(vb), vb], 2)
    s = jnp.einsum('bnqkgd,bnskd->bnkgqs', qb, kk)
    i = jnp.arange(W)[:, None]
    j = jnp.arange(2 * W)[None, :]
    rel = W + i - j
    band = (rel >= 0) & (rel <= WINDOW)
    blkid = jnp.arange(nb)[:, None, None]
    mask = band[None] & ((blkid > 0) | (j[None] >= W))
    p = sink_probs(s, rel, mask[None, :, None, None], sinks).astype(vv.dtype)
    o = jnp.einsum('bnkgqs,bnskd->bnqkgd', p, vv).reshape(B, nb * W, SWA_HEADS * SWA_HEAD_DIM)[:, :L]
    R = min(WINDOW, L)
    return o @ w_o + b_o, k[:, L - R:], v[:, L - R:]


def swa_sample(h, cache_k, cache_v, w_qkv, b_qkv, sinks, w_o, b_o):
    B, L, _ = h.shape
    q, k, v = swa_project(h, w_qkv, b_qkv)
    R = cache_k.shape[1]
    kk = jnp.concatenate([cache_k.astype(k.dtype), k], 1)
    vv = jnp.concatenate([cache_v.astype(v.dtype), v], 1)
    s = jnp.einsum('bqkgd,bskd->bkgqs', q, kk)
    rel = (R + jnp.arange(L))[:, None] - jnp.arange(R + L)[None, :]
    mask = (rel >= 0) & (rel <= WINDOW)
    p = sink_probs(s, rel, mask, sinks).astype(vv.dtype)
    o = jnp.einsum('bkgqs,bskd->bqkgd', p, vv).reshape(B, L, SWA_HEADS * SWA_HEAD_DIM)
    return o @ w_o + b_o, k, v


def setup_inputs(seed: int = 0) -> dict:
    key = jax.random.key(seed)
    ks = jax.random.split(key, 26)
    f32 = jnp.float32
    nrm = lambda k, shape, scale: jax.random.normal(k, shape, f32) * scale
    NA, NB = N_DELTA_LAYERS, N_SWA_LAYERS
    cache_rows = min(WINDOW, PAST_LEN)
    dt = jnp.exp(jax.random.uniform(ks[13], (NA, DN_V_HEADS), f32, math.log(1e-3), math.log(1e-1)))
    return {
        "x_prompt": nrm(ks[0], (BATCH, SEQ, D_MODEL), 1.0),
        "x_sample": nrm(ks[1], (DEC_BATCH, DEC_SEQ, D_MODEL), 1.0),
        "state_conv": nrm(ks[2], (NA, DEC_BATCH, DN_CONV - 1, DN_CONV_DIM), 1.0),
        "state_delta": nrm(ks[3], (NA, DEC_BATCH, DN_V_HEADS, DN_HEAD_K, DN_HEAD_V), 0.5),
        "cache_k": nrm(ks[4], (NB, DEC_BATCH, cache_rows, SWA_KV_HEADS, SWA_HEAD_DIM), 1.0),
        "cache_v": nrm(ks[5], (NB, DEC_BATCH, cache_rows, SWA_KV_HEADS, SWA_HEAD_DIM), 1.0),
        "norm_mix": 1.0 + nrm(ks[6], (DEPTH, D_MODEL), 0.05),
        "norm_ffn": 1.0 + nrm(ks[7], (DEPTH, D_MODEL), 0.05),
        "norm_final": 1.0 + nrm(ks[8], (D_MODEL,), 0.05),
        "dn_w_in": nrm(ks[9], (NA, D_MODEL, DN_IN_DIM), D_MODEL ** -0.5),
        "dn_conv_w": nrm(ks[10], (NA, DN_CONV, DN_CONV_DIM), DN_CONV ** -0.5),
        "dn_a_log": jnp.log(jax.random.uniform(ks[11], (NA, DN_V_HEADS), f32, 1.0, 16.0)),
        "dn_dt_bias": dt + jnp.log(-jnp.expm1(-dt)),
        "dn_norm_w": 1.0 + nrm(ks[12], (NA, DN_HEAD_V), 0.05),
        "dn_w_out": nrm(ks[14], (NA, DN_VAL_DIM, D_MODEL), DN_VAL_DIM ** -0.5),
        "swa_w_qkv": nrm(ks[15], (NB, D_MODEL, SWA_QKV_DIM), D_MODEL ** -0.5),
        "swa_b_qkv": nrm(ks[16], (NB, SWA_QKV_DIM), 0.02),
        "swa_sinks": nrm(ks[17], (NB, SWA_HEADS), 1.0),
        "swa_w_o": nrm(ks[18], (NB, SWA_HEADS * SWA_HEAD_DIM, D_MODEL), (SWA_HEADS * SWA_HEAD_DIM) ** -0.5),
        "swa_b_o": nrm(ks[19], (NB, D_MODEL), 0.02),
        "ffn_w_gu": nrm(ks[20], (DEPTH, D_MODEL, 2 * D_FF), D_MODEL ** -0.5),
        "ffn_w_down": nrm(ks[21], (DEPTH, D_FF, D_MODEL), D_FF ** -0.5),
    }


def reference(x_prompt, x_sample, state_conv, state_delta, cache_k, cache_v,
              norm_mix, norm_ffn, norm_final,
              dn_w_in, dn_conv_w, dn_a_log, dn_dt_bias, dn_norm_w, dn_w_out,
              swa_w_qkv, swa_b_qkv, swa_sinks, swa_w_o, swa_b_o,
              ffn_w_gu, ffn_w_down):
    yp, ys = x_prompt, x_sample
    conv_p, delta_p, kp_list, vp_list = [], [], [], []
    conv_s, delta_s, ks_list, vs_list = [], [], [], []
    Bp = x_prompt.shape[0]
    for i in range(DEPTH):
        j = i // N_MIXERS
        hp = rmsnorm(yp, norm_mix[i])
        hs = rmsnorm(ys, norm_mix[i])
        if i % N_MIXERS == 0:
            wts = (dn_w_in[j], dn_conv_w[j], dn_a_log[j], dn_dt_bias[j], dn_norm_w[j], dn_w_out[j])
            buf0 = jnp.zeros((Bp, DN_CONV - 1, DN_CONV_DIM), hp.dtype)
            s00 = jnp.zeros((Bp, DN_V_HEADS, DN_HEAD_K, DN_HEAD_V), jnp.float32)
            op, cbp, sp = deltanet_mixer(hp, buf0, s00, *wts)
            os_, cbs, ss = deltanet_mixer(hs, state_conv[j], state_delta[j], *wts)
            conv_p.append(cbp.astype(state_conv.dtype))
            delta_p.append(sp.astype(state_delta.dtype))
            conv_s.append(cbs.astype(state_conv.dtype))
            delta_s.append(ss.astype(state_delta.dtype))
        else:
            wts = (swa_w_qkv[j], swa_b_qkv[j], swa_sinks[j], swa_w_o[j], swa_b_o[j])
            op, kpn, vpn = swa_prompt(hp, *wts)
            os_, ksn, vsn = swa_sample(hs, cache_k[j], cache_v[j], *wts)
            kp_list.append(kpn.astype(cache_k.dtype))
            vp_list.append(vpn.astype(cache_v.dtype))
            ks_list.append(ksn.astype(cache_k.dtype))
            vs_list.append(vsn.astype(cache_v.dtype))
        yp = yp + op
        ys = ys + os_
        yp = yp + swiglu(rmsnorm(yp, norm_ffn[i]), ffn_w_gu[i], ffn_w_down[i])
        ys = ys + swiglu(rmsnorm(ys, norm_ffn[i]), ffn_w_gu[i], ffn_w_down[i])
    y_prompt = rmsnorm(yp, norm_final)
    y_sample = rmsnorm(ys, norm_final)
    return (y_prompt, y_sample,
            jnp.stack(conv_p), jnp.stack(delta_p), jnp.stack(kp_list), jnp.stack(vp_list),
            jnp.stack(conv_s), jnp.stack(delta_s), jnp.stack(ks_list), jnp.stack(vs_list))
```

```python
import contextlib
import numpy as np
import concourse.bass as bass
import concourse.mybir as mybir
from concourse.bass_utils import run_bass_kernel_spmd

F32 = mybir.dt.float32
BF16 = mybir.dt.bfloat16
AF = mybir.ActivationFunctionType
ALU = mybir.AluOpType
AX = mybir.AxisListType

ENGS = ("tensor", "vector", "scalar", "gpsimd", "sync")
SAME_ENG_SYNC = {"tensor": False, "vector": True, "scalar": True, "gpsimd": True, "sync": False}
N_DMA_SEMS = 48


class Buf:
    __slots__ = ("name", "last_w", "readers")

    def __init__(self, name="b"):
        self.name = name
        self.last_w = None
        self.readers = []


class Prog:
    def __init__(self, nc):
        self.nc = nc
        self.ops = {e: [] for e in ENGS}
        self.count = {e: 0 for e in ENGS}
        self.known = {e: {} for e in ENGS}
        self.clock = {}
        self.dma_n = 0
        self.dma_cnt = [0] * N_DMA_SEMS
        self.nwaits = 0
        self.waited = {("e", e): set() for e in ENGS}

    def _need(self, eng, tok, waits):
        semkey, val, teng = tok
        if teng == eng and not SAME_ENG_SYNC[eng]:
            return
        k = self.known[eng]
        if k.get(semkey, 0) >= val:
            return
        waits.append((semkey, val))
        self.nwaits += 1
        if semkey[0] == "e":
            self.waited[semkey].add(val)
        c = self.clock.get((semkey, val))
        if c is not None:
            for s, v in c.items():
                if k.get(s, 0) < v:
                    k[s] = v
        k[semkey] = val

    def _deps(self, eng, reads, writes, waits):
        for b in reads:
            if b.last_w is not None:
                self._need(eng, b.last_w, waits)
        for b in writes:
            if b.last_w is not None:
                self._need(eng, b.last_w, waits)
            for r in b.readers:
                self._need(eng, r, waits)

    def _commit(self, tok, reads, writes):
        for b in reads:
            b.readers.append(tok)
            if len(b.readers) > 16:
                d = {}
                for t in b.readers:
                    if t[0] not in d or d[t[0]][1] < t[1]:
                        d[t[0]] = t
                b.readers = list(d.values())
        for b in writes:
            b.last_w = tok
            b.readers = []

    def op(self, eng, fn, reads=(), writes=()):
        waits = []
        self._deps(eng, reads, writes, waits)
        self.count[eng] += 1
        semkey = ("e", eng)
        val = self.count[eng]
        tok = (semkey, val, eng)
        c = dict(self.known[eng])
        c[semkey] = val
        self.clock[(semkey, val)] = c
        self.ops[eng].append((waits, fn, semkey, 1))
        self._commit(tok, reads, writes)
        return tok

    def dma(self, queue, out, in_, reads=(), writes=(), **kw):
        waits = []
        self._deps(queue, reads, writes, waits)
        j = self.dma_n % N_DMA_SEMS
        self.dma_n += 1
        semkey = ("d", j)
        prev = self.dma_cnt[j]
        if prev:
            self._need(queue, (semkey, prev * 16, "dma"), waits)
        self.dma_cnt[j] += 1
        val = self.dma_cnt[j] * 16
        tok = (semkey, val, "dma")
        self.clock[(semkey, val)] = dict(self.known[queue])

        def fn(e, out=out, in_=in_, kw=kw):
            return e.dma_start(out=out, in_=in_, **kw)

        self.ops[queue].append((waits, fn, semkey, 16))
        self._commit(tok, reads, writes)
        return tok

    def fence(self):
        toks = [(("e", x), self.count[x], x) for x in ENGS if self.count[x] > 0]
        toks += [(("d", j), self.dma_cnt[j] * 16, "dma") for j in range(N_DMA_SEMS) if self.dma_cnt[j]]
        for e in ENGS:
            waits = []
            for t in toks:
                if t[2] != e:
                    self._need(e, t, waits)
            if waits:
                self.ops[e].append((waits, None, None, 0))

    def emit(self):
        nc = self.nc
        with contextlib.ExitStack() as st:
            st.enter_context(nc.allow_non_contiguous_dma(reason="small strided layout DMAs"))
            sems = {}
            for e in ENGS:
                sems[("e", e)] = st.enter_context(nc.semaphore(f"s_{e}"))
            for j in range(N_DMA_SEMS):
                sems[("d", j)] = st.enter_context(nc.semaphore(f"s_d{j}"))
            fin = []
            for j in range(N_DMA_SEMS):
                if self.dma_cnt[j]:
                    fin.append((("d", j), self.dma_cnt[j] * 16))
            for e in ENGS:
                if e != "sync" and self.count[e]:
                    fin.append((("e", e), self.count[e]))
                    self.waited[("e", e)].add(self.count[e])
            rank = {}
            for e in ENGS:
                rank[("e", e)] = {v: i + 1 for i, v in enumerate(sorted(self.waited[("e", e)]))}

            def rv(sk, v):
                return rank[sk][v] if sk[0] == "e" else v

            block = st.enter_context(nc.Block())
            for e in ENGS:
                ops = self.ops[e]
                extra = fin if e == "sync" else []

                def body(eng, ops=ops, extra=extra, e=e):
                    idx = 0
                    rk = rank[("e", e)]
                    for waits, fn, semkey, inc in ops:
                        for sk, v in waits:
                            eng.wait_ge(sems[sk], rv(sk, v))
                        if fn is not None:
                            inst = fn(eng)
                            if semkey[0] == "e":
                                idx += 1
                                if idx in rk:
                                    inst.then_inc(sems[semkey], 1)
                            else:
                                inst.then_inc(sems[semkey], inc)
                    for sk, v in extra:
                        eng.wait_ge(sems[sk], rv(sk, v))

                getattr(block, e)(body)


D = 1024
TP = 2048
NB = 16
TS = 64
NT = TP + TS
DEPTH = 4
DFF = 2816
DN_IN = 6176
RMS_EPS = 1e-6
TILES = [(0, 512), (512, 512), (1024, 512), (1536, 512), (2048, 64)]
NCORES = 8


class Rot:
    def __init__(self, alloc, name, shape, dtype, n):
        self.t = [alloc(f"{name}{i}", shape, dtype) for i in range(n)]
        self.b = [Buf(f"{name}{i}") for i in range(n)]
        self.i = 0

    def get(self):
        i = self.i
        self.i = (i + 1) % len(self.t)
        return self.t[i], self.b[i]


def build_program(stage=99, debug=False):
    nc = bass.Bass("TRN2", target_bir_lowering=False)
    dbg_state = {"n": 0}

    def dump(P, name, ap, reads, once=True):
        if not debug:
            return
        key = "dbg_" + name
        if once and key in dbg_state:
            return
        dbg_state[key] = 1
        shp = list(ap.shape)
        dt_ = ap.dtype
        t = nc.dram_tensor(key, shp, dt_, kind="ExternalOutput").ap()
        P.dma("sync", t, ap, reads=reads)

    def din(name, shape):
        return nc.dram_tensor(name, list(shape), F32, kind="ExternalInput").ap()

    def dout(name, shape):
        return nc.dram_tensor(name, list(shape), F32, kind="ExternalOutput").ap()

    xp = din("xp", [TP, D])
    xs = din("xs", [TS, D])
    norm_mix = din("norm_mix", [DEPTH, D])
    norm_ffn = din("norm_ffn", [DEPTH, D])
    norm_final = din("norm_final", [1, D])
    ffn_w_gu = din("ffn_w_gu", [DEPTH, D, 2 * DFF])
    ffn_w_down = din("ffn_w_down", [DEPTH, DFF, D])
    swa_w_qkv = din("swa_w_qkv", [2, D, 1536])
    swa_b_qkv = din("swa_b_qkv", [2, 1536])
    swa_sinks = din("swa_sinks", [2, 16])
    swa_w_o = din("swa_w_o", [2, D, D])
    swa_b_o = din("swa_b_o", [2, D])
    ck_in = din("cache_k", [2, NB, 128, 256])
    cv_in = din("cache_v", [2, NB, 128, 256])
    dn_w_in = din("dn_w_in", [2, D, DN_IN])
    dn_conv_w = din("dn_conv_w", [2, 4, 4096])
    dn_a_log = din("dn_a_log", [2, 16])
    dn_dt_bias = din("dn_dt_bias", [2, 16])
    dn_norm_w = din("dn_norm_w", [2, 128])
    dn_w_out = din("dn_w_out", [2, 2048, D])
    sconv_in = din("state_conv", [2, NB * 3, 4096])
    sdelta_in = din("state_delta", [2, NB, 16, 128, 128])
    yp = dout("yp", [TP, D])
    ys = dout("ys", [TS, D])
    convp_out = dout("convp", [2, 3, 4096])
    deltap_out = dout("deltap", [2, 16, 128, 128])
    convs_out = dout("convs", [2, NB, 3, 4096])
    deltas_out = dout("deltas", [2, NB, 16, 128, 128])
    kp_out = dout("kp", [2, 128, 256])
    vp_out = dout("vp", [2, 128, 256])
    ks_out = dout("ks", [2, TS, 256])
    vs_out = dout("vs", [2, TS, 256])

    with contextlib.ExitStack() as st:
        def sb(name, shape, dt):
            return st.enter_context(nc.sbuf_tensor(name, list(shape), dt))[:]

        arena_state = {"off": 0, "ap": None, "size": 0}

        def carve(name, shape, dt):
            shape = list(shape)
            n = 1
            for x in shape[1:]:
                n *= x
            isz = 2 if dt == BF16 else 4
            n32 = (n * isz + 3) // 4
            n32 = (n32 + 7) // 8 * 8
            off = arena_state["off"]
            assert off + n32 <= arena_state["size"], (name, off, n32, arena_state["size"])
            arena_state["off"] = off + n32
            ap = arena_state["ap"][:, off:off + n32]
            if dt != F32:
                ap = ap.bitcast(dt)
            ap = ap[:, 0:n]
            if len(shape) == 3:
                ap = ap.rearrange("p (a b) -> p a b", b=shape[2])
            elif len(shape) == 4:
                ap = ap.rearrange("p (a b c) -> p a b c", b=shape[2], c=shape[3])
            return ap

        def arena_reset():
            P.fence()
            arena_state["off"] = 0

        P = Prog(nc)
        V = lambda fn, r=(), w=(): P.op("vector", fn, r, w)
        A = lambda fn, r=(), w=(): P.op("scalar", fn, r, w)
        T = lambda fn, r=(), w=(): P.op("tensor", fn, r, w)
        G = lambda fn, r=(), w=(): P.op("gpsimd", fn, r, w)

        xT = sb("xT", [128, 8, NT], F32)
        hT = sb("hT", [128, 8, NT], BF16)
        bx = [[Buf(f"x{t}_{c}") for c in range(8)] for t in range(5)]
        bh = [Buf(f"h{t}") for t in range(5)]

        ident = sb("ident", [128, 128], F32)
        ones_bf = sb("ones_bf", [128, 128], BF16)
        eps_rms = sb("eps_rms", [128, 1], F32)
        nw = sb("nw", [128, 9, 8], F32)
        bconst = Buf("const")
        bnw = Buf("nw")
        bident = Buf("ident")
        G(lambda e: e.memset(ident[:], 1.0), w=[bident])
        G(lambda e: e.affine_select(out=ident[:], in_=ident[:], pattern=[[-1, 128]],
                                    compare_op=ALU.is_equal, fill=0.0, base=0, channel_multiplier=1),
          r=[bident], w=[bident])
        V(lambda e: e.memset(ones_bf[:], 1.0), w=[bconst])
        V(lambda e: e.memset(eps_rms[:], RMS_EPS), r=[bconst], w=[bconst])
        P.dma("sync", nw[:, 0:4, :], norm_mix.rearrange("l (c p) -> p l c", p=128), writes=[bnw])
        P.dma("sync", nw[:, 4:8, :], norm_ffn.rearrange("l (c p) -> p l c", p=128), reads=[bnw], writes=[bnw])
        P.dma("sync", nw[:, 8:9, :], norm_final.rearrange("l (c p) -> p l c", p=128), reads=[bnw], writes=[bnw])
        P.fence()

        psb = [st.enter_context(nc.psum_tensor(f"ps{i}", [128, 512], F32)) for i in range(8)]
        psB = [Buf(f"ps{i}") for i in range(8)]
        pstate = {"i": 0}

        def ps_get():
            i = pstate["i"]
            pstate["i"] = (i + 1) % 6
            return psb[i], psB[i]

        wrot = Rot(sb, "wslot", [128, 8, 512], BF16, 4)
        asz = (nc.sbuf_bytes_remaining - 256) // 4 // 8 * 8
        arena_state["ap"] = sb("arena", [128, asz], F32)
        arena_state["size"] = asz

        def wload(dram2d, r0, nrow_chunks, c0, ncols):
            t, b = wrot.get()
            src = dram2d[r0:r0 + nrow_chunks * 128, c0:c0 + ncols].rearrange("(c p) n -> p c n", p=128)
            P.dma("gpsimd", t[:, 0:nrow_chunks, 0:ncols], src, writes=[b])
            return t, b


        def load_x():
            xin = Rot(carve, "xin", [128, D], F32, 3)
            blocks = [(xp, t0, 128, t0) for t0 in range(0, TP, 128)] + [(xs, 0, 64, TP)]
            for src, r0, n, tok0 in blocks:
                t, b = xin.get()
                P.dma("sync", t[0:n, :], src[r0:r0 + n, :], writes=[b])
                tt = min(tok0 // 512, 4)
                for half in range(2):
                    ps, pb = ps_get()
                    for cc in range(4):
                        c = half * 4 + cc
                        T(lambda e, ps=ps, t=t, c=c, cc=cc, n=n: e.transpose(
                            out=ps[:, cc * 128:cc * 128 + n], in_=t[0:n, c * 128:(c + 1) * 128],
                            identity=ident[0:n, 0:n]), r=[b, bident], w=[pb])
                    V(lambda e, ps=ps, half=half, tok0=tok0, n=n: e.tensor_copy(
                        out=xT[:, half * 4:half * 4 + 4, tok0:tok0 + n],
                        in_=ps[:].rearrange("p (c t) -> p c t", t=128)[:, :, 0:n]),
                      r=[pb], w=[bx[tt][c] for c in range(half * 4, half * 4 + 4)])

        nrm = {}

        def norm_alloc():
            nrm["sq"] = Rot(carve, "sq", [128, 8, 512], BF16, 2)
            nrm["rs"] = Rot(carve, "rs", [128, 512], F32, 2)

        def rmsnorm(widx, out_fn, out_bufs_fn):
            for ti, (t0, n) in enumerate(TILES):
                sq, sqb = nrm["sq"].get()
                A(lambda e, sq=sq, t0=t0, n=n: e.activation(out=sq[:, :, 0:n], in_=xT[:, :, t0:t0 + n], func=AF.Square),
                  r=bx[ti], w=[sqb])
                ps, pb = ps_get()
                for c in range(8):
                    T(lambda e, ps=ps, sq=sq, c=c, n=n: e.matmul(ps[:, 0:n], lhsT=ones_bf[:], rhs=sq[:, c, 0:n],
                                                               start=(c == 0), stop=(c == 7)),
                      r=[sqb, bconst], w=[pb])
                rs, rsb = nrm["rs"].get()
                A(lambda e, rs=rs, ps=ps, n=n: e.activation(out=rs[:, 0:n], in_=ps[:, 0:n], func=AF.Sqrt,
                                                           bias=eps_rms[:], scale=1.0 / D),
                  r=[pb, bconst], w=[rsb])
                V(lambda e, rs=rs, n=n: e.reciprocal(out=rs[:, 0:n], in_=rs[:, 0:n]), r=[rsb], w=[rsb])
                for c in range(8):
                    o_ap, o_bufs = out_fn(ti, c, t0, n), out_bufs_fn(ti, c)
                    V(lambda e, o_ap=o_ap, c=c, t0=t0, n=n, rs=rs: e.scalar_tensor_tensor(
                        out=o_ap, in0=xT[:, c, t0:t0 + n], scalar=nw[:, widx, c:c + 1], in1=rs[:, 0:n],
                        op0=ALU.mult, op1=ALU.mult),
                      r=[bx[ti][c], rsb, bnw], w=o_bufs)

        def rmsnorm_h(widx):
            rmsnorm(widx, lambda ti, c, t0, n: hT[:, c, t0:t0 + n], lambda ti, c: [bh[ti]])

        def ffn(layer):
            act = carve("act", [128, 8, NT], BF16)
            bact = [Buf(f"act{t}") for t in range(5)]
            sgr = Rot(carve, "sg", [128, 512], F32, 3)
            wgu = ffn_w_gu[layer]
            wdn = ffn_w_down[layer]
            for (j0, Gn) in [(0, 8), (8, 8), (16, 6)]:
                for s in range((Gn + 3) // 4):
                    nch = min(4, Gn - 4 * s)
                    wg, wgb = wload(wgu, 0, 8, (j0 + 4 * s) * 128, nch * 128)
                    wu, wub = wload(wgu, 0, 8, DFF + (j0 + 4 * s) * 128, nch * 128)
                    for jj in range(nch):
                        j = 4 * s + jj
                        for ti, (t0, n) in enumerate(TILES):
                            psg, pgb = ps_get()
                            for k in range(8):
                                T(lambda e, psg=psg, wg=wg, k=k, jj=jj, t0=t0, n=n: e.matmul(
                                    psg[:, 0:n], lhsT=wg[:, k, jj * 128:(jj + 1) * 128], rhs=hT[:, k, t0:t0 + n],
                                    start=(k == 0), stop=(k == 7)), r=[wgb, bh[ti]], w=[pgb])
                            psu, pub = ps_get()
                            for k in range(8):
                                T(lambda e, psu=psu, wu=wu, k=k, jj=jj, t0=t0, n=n: e.matmul(
                                    psu[:, 0:n], lhsT=wu[:, k, jj * 128:(jj + 1) * 128], rhs=hT[:, k, t0:t0 + n],
                                    start=(k == 0), stop=(k == 7)), r=[wub, bh[ti]], w=[pub])
                            sg, sgb = sgr.get()
                            A(lambda e, sg=sg, psg=psg, n=n: e.activation(out=sg[:, 0:n], in_=psg[:, 0:n], func=AF.Silu),
                              r=[pgb], w=[sgb])
                            V(lambda e, sg=sg, psu=psu, j=j, t0=t0, n=n: e.tensor_tensor(
                                out=act[:, j, t0:t0 + n], in0=sg[:, 0:n], in1=psu[:, 0:n], op=ALU.mult),
                              r=[sgb, pub], w=[bact[ti]])
                for half in range(2):
                    wd, wdb = wload(wdn, j0 * 128, Gn, half * 512, 512)
                    for ti, (t0, n) in enumerate(TILES):
                        for nn in range(4):
                            c = half * 4 + nn
                            ps, pb = ps_get()
                            for kk in range(Gn):
                                T(lambda e, ps=ps, wd=wd, kk=kk, nn=nn, t0=t0, n=n, Gn=Gn: e.matmul(
                                    ps[:, 0:n], lhsT=wd[:, kk, nn * 128:(nn + 1) * 128], rhs=act[:, kk, t0:t0 + n],
                                    start=(kk == 0), stop=(kk == Gn - 1)), r=[wdb, bact[ti]], w=[pb])
                            V(lambda e, ps=ps, c=c, t0=t0, n=n: e.tensor_tensor(
                                out=xT[:, c, t0:t0 + n], in0=ps[:, 0:n], in1=xT[:, c, t0:t0 + n], op=ALU.add),
                              r=[pb, bx[ti][c]], w=[bx[ti][c]])

        def final_out():
            yfr = Rot(carve, "yf", [128, 8, 512], F32, 1)
            your = Rot(carve, "yout", [128, D], F32, 2)
            yf, yfb = yfr.get()
            for ti, (t0, n) in enumerate(TILES):
                sq, sqb = nrm["sq"].get()
                A(lambda e, sq=sq, t0=t0, n=n: e.activation(out=sq[:, :, 0:n], in_=xT[:, :, t0:t0 + n], func=AF.Square),
                  r=bx[ti], w=[sqb])
                ps, pb = ps_get()
                for c in range(8):
                    T(lambda e, ps=ps, sq=sq, c=c, n=n: e.matmul(ps[:, 0:n], lhsT=ones_bf[:], rhs=sq[:, c, 0:n],
                                                               start=(c == 0), stop=(c == 7)),
                      r=[sqb, bconst], w=[pb])
                rs, rsb = nrm["rs"].get()
                A(lambda e, rs=rs, ps=ps, n=n: e.activation(out=rs[:, 0:n], in_=ps[:, 0:n], func=AF.Sqrt,
                                                           bias=eps_rms[:], scale=1.0 / D),
                  r=[pb, bconst], w=[rsb])
                V(lambda e, rs=rs, n=n: e.reciprocal(out=rs[:, 0:n], in_=rs[:, 0:n]), r=[rsb], w=[rsb])
                for c in range(8):
                    V(lambda e, c=c, t0=t0, n=n, rs=rs, yf=yf: e.scalar_tensor_tensor(
                        out=yf[:, c, 0:n], in0=xT[:, c, t0:t0 + n], scalar=nw[:, 8, c:c + 1], in1=rs[:, 0:n],
                        op0=ALU.mult, op1=ALU.mult),
                      r=[bx[ti][c], rsb, bnw], w=[yfb])
                for s0 in range(0, n, 128):
                    m = min(128, n - s0)
                    yo, yob = your.get()
                    for half in range(2):
                        ps, pb = ps_get()
                        for cc in range(4):
                            c = half * 4 + cc
                            T(lambda e, ps=ps, yf=yf, c=c, cc=cc, s0=s0, m=m: e.transpose(
                                out=ps[0:m, cc * 128:(cc + 1) * 128], in_=yf[:, c, s0:s0 + m], identity=ident[:]),
                              r=[yfb, bident], w=[pb])
                        A(lambda e, ps=ps, yo=yo, half=half, m=m: e.copy(out=yo[0:m, half * 512:(half + 1) * 512], in_=ps[0:m, :]),
                          r=[pb], w=[yob])
                    dst = yp[t0 + s0:t0 + s0 + m, :] if t0 < TP else ys[s0:s0 + m, :]
                    P.dma("sync", dst, yo[0:m, :], reads=[yob])


        SLOPES = [2.0 ** (-8.0 * (h + 1) / 16) for h in range(16)]
        SCALE = 0.125
        NEG = -30000.0

        def pair_heads(i):
            kc, g = divmod(i, 4)
            return (2 * kc) * 4 + g, (2 * kc + 1) * 4 + g

        def swa(j):
            wqkv = swa_w_qkv[j]
            qT = carve("qT", [128, 8, NT], BF16)
            kT = carve("kT", [128, 2, NT], BF16)
            vtok = carve("vtok", [128, 17, 4, 65], BF16)
            maskT = carve("maskT", [128, 256], F32)
            relT = carve("relT", [128, 256], F32)
            reli = carve("reli", [128, 256], mybir.dt.int32) if False else None
            bq = carve("bq", [128, 8], F32)
            bk = carve("bk", [128, 2], F32)
            bo = carve("bo", [128, 8], F32)
            bkv = carve("bkv", [128, 512], F32)
            sinkEB = carve("sinkEB", [128, 16], F32)
            sink16 = carve("sink16", [128, 4], F32)
            biasC = carve("biasC", [128, 16, 4], F32)
            relC = carve("relC", [128, 4], F32)
            maskC = carve("maskC", [128, 4], F32)
            relN = carve("relN", [128, 16, 4], F32)
            maskN = carve("maskN", [128, 16, 4], F32)
            biasN = carve("biasN", [128, 16, 16, 4], F32)
            bq_b, bk_b, bo_b, bkv_b, bsink, bmask, bvt = [Buf(x) for x in "bq bk bo bkv sink mask vtok".split()]
            bqT = [Buf(f"qT{t}") for t in range(5)]
            bkT = [Buf(f"kT{t}") for t in range(5)]
            bvtok = [Buf(f"vt{b}") for b in range(17)]
            e_r = Rot(carve, "e", [128, 256], F32, 2)
            p_r = Rot(carve, "p", [128, 256], BF16, 2)
            otok_r = Rot(carve, "otok", [128, 8, 128], F32, 1)
            den_r = Rot(carve, "den", [128, 4], F32, 2)
            kvtm_r = Rot(carve, "kvtm", [128, 256], F32, 1)
            ckst_r = Rot(carve, "ckst", [128, 256], F32, 1)
            cvst_r = Rot(carve, "cvst", [128, 256], F32, 1)
            kct_r = Rot(carve, "kct", [128, 2, 128], BF16, 2)
            vc_r = Rot(carve, "vc", [128, 4, 65], BF16, 2)
            es_r = Rot(carve, "es", [128, 64], F32, 2)
            psb_r = Rot(carve, "psb", [128, 64], BF16, 2)
            esn_r = Rot(carve, "esn", [128, 64], F32, 2)
            pn_r = Rot(carve, "pn", [128, 64], BF16, 2)
            os_r = Rot(carve, "os", [128, 4, 64], F32, 1)

            for i in range(8):
                ha, hb = pair_heads(i)
                P.dma("sync", bq[0:64, i:i + 1], swa_b_qkv[j, ha * 64:(ha + 1) * 64].rearrange("(p o) -> p o", o=1), writes=[bq_b], reads=[bq_b])
                P.dma("sync", bq[64:128, i:i + 1], swa_b_qkv[j, hb * 64:(hb + 1) * 64].rearrange("(p o) -> p o", o=1), writes=[bq_b], reads=[bq_b])
            P.dma("sync", bk[:, :], swa_b_qkv[j, 1024:1280].rearrange("(c p) -> p c", p=128), writes=[bk_b])
            P.dma("sync", bo[:, :], swa_b_o[j].rearrange("(c p) -> p c", p=128), writes=[bo_b])
            P.dma("sync", bkv[:, :], swa_b_qkv[j:j + 1, 1024:1536].partition_broadcast(128).rearrange("p o n -> p (o n)"), writes=[bkv_b])
            P.dma("sync", sinkEB[:, :], swa_sinks[j:j + 1, :].partition_broadcast(128).rearrange("p o n -> p (o n)"), writes=[bsink])
            A(lambda e: e.activation(out=sinkEB[:, :], in_=sinkEB[:, :], func=AF.Exp), r=[bsink], w=[bsink])
            for g in range(4):
                P.dma("sync", sink16[4 * g:4 * g + 4, 0:4],
                      swa_sinks[j:j + 1, :].rearrange("o (kv g) -> o kv g", g=4)[:, :, g].partition_broadcast(4).rearrange("p o n -> p (o n)"),
                      writes=[bsink], reads=[bsink])
            A(lambda e: e.activation(out=sink16[0:16, :], in_=sink16[0:16, :], func=AF.Exp), r=[bsink], w=[bsink])
            P.fence()
            I32 = mybir.dt.int32
            ri = carve("ri", [128, 256], I32)
            G(lambda e: e.iota(ri[:, 0:128], pattern=[[1, 128]], base=128, channel_multiplier=-1), w=[bmask])
            G(lambda e: e.iota(ri[:, 128:256], pattern=[[1, 128]], base=0, channel_multiplier=-1), r=[bmask], w=[bmask])
            G(lambda e: e.tensor_copy(out=relT[:, :], in_=ri[:, :]), r=[bmask], w=[bmask])
            G(lambda e: e.memset(maskT[:, :], 0.0), r=[bmask], w=[bmask])
            G(lambda e: e.affine_select(out=maskT[:, 0:128], in_=maskT[:, 0:128], pattern=[[-1, 128]],
                                        compare_op=ALU.is_ge, fill=NEG, base=0, channel_multiplier=1), r=[bmask], w=[bmask])
            G(lambda e: e.affine_select(out=maskT[:, 128:256], in_=maskT[:, 128:256], pattern=[[1, 128]],
                                        compare_op=ALU.is_ge, fill=NEG, base=0, channel_multiplier=-1), r=[bmask], w=[bmask])
            G(lambda e: e.iota(ri[:, 0:4], pattern=[[1, 4]], base=128, channel_multiplier=-1), r=[bmask], w=[bmask])
            G(lambda e: e.tensor_copy(out=relC[:, :], in_=ri[:, 0:4]), r=[bmask], w=[bmask])
            G(lambda e: e.memset(maskC[:, :], 0.0), r=[bmask], w=[bmask])
            G(lambda e: e.affine_select(out=maskC[:, :], in_=maskC[:, :], pattern=[[-1, 4]],
                                        compare_op=ALU.is_ge, fill=NEG, base=0, channel_multiplier=1), r=[bmask], w=[bmask])
            G(lambda e: e.iota(ri[:, 0:64], pattern=[[4, 16], [1, 4]], base=0, channel_multiplier=-1), r=[bmask], w=[bmask])
            G(lambda e: e.tensor_copy(out=relN[:, :, :], in_=ri[:, 0:64].rearrange("p (b t) -> p b t", t=4)), r=[bmask], w=[bmask])
            G(lambda e: e.memset(maskN[:, :, :], 0.0), r=[bmask], w=[bmask])
            G(lambda e: e.affine_select(out=maskN[:, :, :], in_=maskN[:, :, :], pattern=[[4, 16], [1, 4]],
                                        compare_op=ALU.is_ge, fill=NEG, base=0, channel_multiplier=-1), r=[bmask], w=[bmask])
            G(lambda e: e.affine_select(out=maskN[:, :, :], in_=maskN[:, :, :], pattern=[[-4, 16], [0, 4]],
                                        compare_op=ALU.is_ge, fill=NEG, base=0, channel_multiplier=1), r=[bmask], w=[bmask])
            for h in range(16):
                V(lambda e, h=h: e.scalar_tensor_tensor(out=biasC[:, h, :], in0=relC[:, :], scalar=-SLOPES[h], in1=maskC[:, :],
                                                        op0=ALU.mult, op1=ALU.add), r=[bmask], w=[bmask])
                V(lambda e, h=h: e.scalar_tensor_tensor(out=biasN[:, :, h, :], in0=relN[:, :, :], scalar=-SLOPES[h], in1=maskN[:, :, :],
                                                        op0=ALU.mult, op1=ALU.add), r=[bmask], w=[bmask])
            V(lambda e: e.memset(vtok[:, :, :, 64:65], 1.0), w=[bvt])
            P.fence()

            for s_ in range(2):
                wq_, wqb = wrot.get()
                for ii in range(4):
                    ha, hb = pair_heads(s_ * 4 + ii)
                    for hf, hh in ((0, ha), (1, hb)):
                        P.dma("gpsimd", wq_[:, :, ii * 128 + hf * 64:ii * 128 + hf * 64 + 64],
                              wqkv[:, hh * 64:(hh + 1) * 64].rearrange("(c p) n -> p c n", p=128), writes=[wqb], reads=[wqb])
                for ii in range(4):
                    i = s_ * 4 + ii
                    for ti, (t0, n) in enumerate(TILES):
                        ps, pb = ps_get()
                        for k in range(8):
                            T(lambda e, ps=ps, wq_=wq_, ii=ii, k=k, t0=t0, n=n: e.matmul(
                                ps[:, 0:n], lhsT=wq_[:, k, ii * 128:(ii + 1) * 128], rhs=hT[:, k, t0:t0 + n], start=(k == 0), stop=(k == 7)),
                              r=[wqb, bh[ti]], w=[pb])
                        A(lambda e, ps=ps, i=i, t0=t0, n=n: e.activation(out=qT[:, i, t0:t0 + n], in_=ps[:, 0:n], func=AF.Identity,
                                                                        bias=bq[:, i:i + 1], scale=1.0),
                          r=[pb, bq_b], w=[bqT[ti]])
            wkv, wkvb = wload(wqkv, 0, 8, 1024, 512)
            for c in range(2):
                for ti, (t0, n) in enumerate(TILES):
                    ps, pb = ps_get()
                    for k in range(8):
                        T(lambda e, ps=ps, c=c, k=k, t0=t0, n=n: e.matmul(
                            ps[:, 0:n], lhsT=wkv[:, k, c * 128:(c + 1) * 128], rhs=hT[:, k, t0:t0 + n], start=(k == 0), stop=(k == 7)),
                          r=[wkvb, bh[ti]], w=[pb])
                    A(lambda e, ps=ps, c=c, t0=t0, n=n: e.activation(out=kT[:, c, t0:t0 + n], in_=ps[:, 0:n], func=AF.Identity,
                                                                    bias=bk[:, c:c + 1], scale=1.0),
                      r=[pb, bk_b], w=[bkT[ti]])
            for blk in range(17):
                t0, n = (blk * 128, 128) if blk < 16 else (TP, TS)
                ti = min(blk // 4, 4)
                need_k = blk >= 15
                ps, pb = ps_get()
                c0 = 0 if need_k else 256
                for k in range(8):
                    T(lambda e, ps=ps, k=k, t0=t0, n=n, c0=c0: e.matmul(
                        ps[0:n, c0:512], lhsT=hT[:, k, t0:t0 + n], rhs=wkv[:, k, c0:512], start=(k == 0), stop=(k == 7)),
                      r=[wkvb, bh[ti]], w=[pb])
                if need_k:
                    for which, c1, dst in ((0, 0, (kp_out[j] if blk == 15 else ks_out[j])), (1, 256, (vp_out[j] if blk == 15 else vs_out[j]))):
                        kv_, kvb_ = kvtm_r.get()
                        V(lambda e, ps=ps, kv_=kv_, c1=c1, n=n: e.tensor_tensor(out=kv_[0:n, :], in0=ps[0:n, c1:c1 + 256], in1=bkv[0:n, c1:c1 + 256], op=ALU.add),
                          r=[pb, bkv_b], w=[kvb_])
                        P.dma("sync", dst[0:n, :], kv_[0:n, :], reads=[kvb_])
                        if which == 1:
                            V(lambda e, kv_=kv_, blk=blk, n=n: e.tensor_copy(out=vtok[0:n, blk, :, 0:64], in_=kv_[0:n, :].rearrange("p (k d) -> p k d", d=64)),
                              r=[kvb_, bvt], w=[bvtok[blk]])
                else:
                    V(lambda e, ps=ps, blk=blk, n=n: e.tensor_tensor(out=vtok[0:n, blk, :, 0:64], in0=ps[0:n, 256:512].rearrange("p (k d) -> p k d", d=64),
                                                                    in1=bkv[0:n, 256:512].rearrange("p (k d) -> p k d", d=64), op=ALU.add),
                      r=[pb, bkv_b, bvt], w=[bvtok[blk]])

            for qb in range(16):
                ti = qb // 4
                q0 = qb * 128
                ot, otb = otok_r.get()
                for kv in range(4):
                    half, kc = kv % 2, kv // 2
                    pso, psob = ps_get()
                    for g in range(4):
                        h = kv * 4 + g
                        i = kc * 4 + g
                        ps, pb = ps_get()
                        rb = [bkT[ti], bqT[ti]] + ([bkT[(qb - 1) // 4]] if qb > 0 else [])
                        T(lambda e, ps=ps, half=half, kc=kc, i=i, q0=q0: e.matmul(
                            ps[:, 128:256], lhsT=kT[half * 64:(half + 1) * 64, kc, q0:q0 + 128],
                            rhs=qT[half * 64:(half + 1) * 64, i, q0:q0 + 128], start=True, stop=True), r=rb, w=[pb])
                        if qb > 0:
                            T(lambda e, ps=ps, half=half, kc=kc, i=i, q0=q0: e.matmul(
                                ps[:, 0:128], lhsT=kT[half * 64:(half + 1) * 64, kc, q0 - 128:q0],
                                rhs=qT[half * 64:(half + 1) * 64, i, q0:q0 + 128], start=True, stop=True), r=rb, w=[pb])
                        lo = 0 if qb > 0 else 128
                        e_, eb_ = e_r.get()
                        V(lambda e, ps=ps, e_=e_, lo=lo: e.scalar_tensor_tensor(out=e_[:, lo:256], in0=ps[:, lo:256], scalar=SCALE, in1=maskT[:, lo:256],
                                                                             op0=ALU.mult, op1=ALU.add), r=[pb, bmask], w=[eb_])
                        V(lambda e, e_=e_, lo=lo, h=h: e.scalar_tensor_tensor(out=e_[:, lo:256], in0=relT[:, lo:256], scalar=-SLOPES[h], in1=e_[:, lo:256],
                                                                            op0=ALU.mult, op1=ALU.add), r=[eb_, bmask], w=[eb_])
                        p_, pb_ = p_r.get()
                        A(lambda e, e_=e_, p_=p_, lo=lo: e.activation(out=p_[:, lo:256], in_=e_[:, lo:256], func=AF.Exp), r=[eb_], w=[pb_])
                        T(lambda e, pso=pso, p_=p_, qb=qb, kv=kv, g=g: e.matmul(
                            pso[:, g * 65:(g + 1) * 65], lhsT=p_[:, 128:256], rhs=vtok[:, qb, kv, :], start=True, stop=(qb == 0)),
                          r=[pb_, bvtok[qb]], w=[psob])
                        if qb > 0:
                            T(lambda e, pso=pso, p_=p_, qb=qb, kv=kv, g=g: e.matmul(
                                pso[:, g * 65:(g + 1) * 65], lhsT=p_[:, 0:128], rhs=vtok[:, qb - 1, kv, :], start=False, stop=True),
                              r=[pb_, bvtok[qb - 1]], w=[psob])
                    dn, dnb = den_r.get()
                    pso3 = pso[:, 0:260].rearrange("p (g d) -> p g d", d=65)
                    V(lambda e, dn=dn, pso3=pso3, kv=kv: e.tensor_tensor(out=dn[:, :], in0=pso3[:, :, 64], in1=sinkEB[:, kv * 4:kv * 4 + 4], op=ALU.add),
                      r=[psob, bsink], w=[dnb])
                    V(lambda e, dn=dn: e.reciprocal(out=dn[:, :], in_=dn[:, :]), r=[dnb], w=[dnb])
                    V(lambda e, ot=ot, pso3=pso3, dn=dn, kc=kc, half=half: e.tensor_tensor(
                        out=ot[:, kc * 4:kc * 4 + 4, half * 64:(half + 1) * 64], in0=pso3[:, :, 0:64],
                        in1=dn[:, :].to_broadcast([128, 4, 64]) if False else bass.AP(dn.tensor, dn.offset, [list(dn.ap[0]), [1, 4], [0, 64]]),
                        op=ALU.mult), r=[psob, dnb], w=[otb])
                for hf in range(2):
                    ps, pb = ps_get()
                    for cc in range(4):
                        i = hf * 4 + cc
                        T(lambda e, ps=ps, ot=ot, i=i, cc=cc: e.transpose(out=ps[:, cc * 128:(cc + 1) * 128], in_=ot[:, i, :], identity=ident[:, :]),
                          r=[otb, bident], w=[pb])
                    A(lambda e, ps=ps, hf=hf, q0=q0: e.copy(out=hT[:, hf * 4:hf * 4 + 4, q0:q0 + 128], in_=ps[:, :].rearrange("p (c t) -> p c t", t=128)),
                      r=[pb], w=[bh[ti]])

            for b in range(NB):
                tk0 = TP + 4 * b
                ckst, ckb = ckst_r.get()
                cvst, cvb = cvst_r.get()
                P.dma("sync", ckst[:, :], ck_in[j, b], writes=[ckb])
                P.dma("sync", cvst[:, :], cv_in[j, b], writes=[cvb])
                vc, vcb = vc_r.get()
                V(lambda e, vc=vc: e.memset(vc[:, :, 64:65], 1.0), w=[vcb])
                V(lambda e, vc=vc, cvst=cvst: e.tensor_copy(out=vc[:, :, 0:64], in_=cvst[:, :].rearrange("p (k d) -> p k d", d=64)), r=[cvb], w=[vcb])
                ps, pb = ps_get()
                for c in range(2):
                    T(lambda e, ps=ps, ckst=ckst, c=c: e.transpose(out=ps[:, c * 128:(c + 1) * 128], in_=ckst[:, c * 128:(c + 1) * 128], identity=ident[:, :]),
                      r=[ckb, bident], w=[pb])
                kct, kctb = kct_r.get()
                A(lambda e, ps=ps, kct=kct: e.copy(out=kct[:, :, :], in_=ps[:, 0:256].rearrange("p (c t) -> p c t", t=128)), r=[pb], w=[kctb])
                psc, pscb = ps_get()
                for kv in range(4):
                    half, kc = kv % 2, kv // 2
                    for g in range(4):
                        T(lambda e, psc=psc, kct=kct, half=half, kc=kc, kv=kv, g=g, tk0=tk0: e.matmul(
                            psc[:, kv * 16 + g * 4:kv * 16 + g * 4 + 4], lhsT=kct[half * 64:(half + 1) * 64, kc, :],
                            rhs=qT[half * 64:(half + 1) * 64, kc * 4 + g, tk0:tk0 + 4], start=True, stop=True),
                          r=[kctb, bqT[4]], w=[pscb])
                        T(lambda e, psc=psc, half=half, kc=kc, kv=kv, g=g, tk0=tk0: e.matmul(
                            psc[0:64, 64 + kv * 16 + g * 4:64 + kv * 16 + g * 4 + 4], lhsT=kT[half * 64:(half + 1) * 64, kc, TP:TP + 64],
                            rhs=qT[half * 64:(half + 1) * 64, kc * 4 + g, tk0:tk0 + 4], start=True, stop=True),
                          r=[bkT[4], bqT[4]], w=[pscb])
                es, esb = es_r.get()
                V(lambda e, es=es, psc=psc: e.scalar_tensor_tensor(out=es[:, :], in0=psc[:, 0:64], scalar=SCALE, in1=biasC[:, :, :].rearrange("p h t -> p (h t)"),
                                                                 op0=ALU.mult, op1=ALU.add), r=[pscb, bmask], w=[esb])
                pc_, pcb = psb_r.get()
                A(lambda e, es=es, pc_=pc_: e.activation(out=pc_[:, :], in_=es[:, :], func=AF.Exp), r=[esb], w=[pcb])
                esn, esnb = esn_r.get()
                V(lambda e, esn=esn, psc=psc, b=b: e.scalar_tensor_tensor(out=esn[0:64, :], in0=psc[0:64, 64:128], scalar=SCALE,
                                                                        in1=biasN[0:64, b, :, :].rearrange("p h t -> p (h t)"),
                                                                        op0=ALU.mult, op1=ALU.add), r=[pscb, bmask], w=[esnb])
                pn, pnb = pn_r.get()
                A(lambda e, esn=esn, pn=pn: e.activation(out=pn[0:64, :], in_=esn[0:64, :], func=AF.Exp), r=[esnb], w=[pnb])
                pso, psob = ps_get()
                for kv in range(4):
                    T(lambda e, pso=pso, pc_=pc_, vc=vc, kv=kv: e.matmul(pso[0:16, kv * 65:(kv + 1) * 65], lhsT=pc_[:, kv * 16:(kv + 1) * 16], rhs=vc[:, kv, :],
                                                                       start=True, stop=False), r=[pcb, vcb], w=[psob])
                    T(lambda e, pso=pso, pn=pn, kv=kv: e.matmul(pso[0:16, kv * 65:(kv + 1) * 65], lhsT=pn[0:64, kv * 16:(kv + 1) * 16], rhs=vtok[0:64, 16, kv, :],
                                                              start=False, stop=True), r=[pnb, bvtok[16]], w=[psob])
                dn, dnb = den_r.get()
                pso3 = pso[0:16, 0:260].rearrange("p (k d) -> p k d", d=65)
                V(lambda e, dn=dn, pso3=pso3: e.tensor_tensor(out=dn[0:16, :], in0=pso3[:, :, 64], in1=sink16[0:16, :], op=ALU.add), r=[psob, bsink], w=[dnb])
                V(lambda e, dn=dn: e.reciprocal(out=dn[0:16, :], in_=dn[0:16, :]), r=[dnb], w=[dnb])
                os_, osb = os_r.get()
                V(lambda e, os_=os_, pso3=pso3, dn=dn: e.tensor_tensor(out=os_[0:16, :, :], in0=pso3[:, :, 0:64],
                                                                     in1=bass.AP(dn.tensor, dn.offset, [[dn.ap[0][0], 16], [1, 4], [0, 64]]), op=ALU.mult),
                  r=[psob, dnb], w=[osb])
                pst, pstb = ps_get()
                for kc in range(2):
                    T(lambda e, pst=pst, os_=os_, kc=kc: e.transpose(out=pst[:, kc * 16:(kc + 1) * 16],
                                                                    in_=os_[0:16, 2 * kc:2 * kc + 2, :].rearrange("p k d -> p (k d)"), identity=ident[0:16, 0:16]),
                      r=[osb, bident], w=[pstb])
                A(lambda e, pst=pst, tk0=tk0: e.copy(out=hT[:, :, tk0:tk0 + 4], in_=pst[:, 0:32].rearrange("p (i t) -> p i t", t=4)), r=[pstb], w=[bh[4]])

            wo = swa_w_o[j]
            for half in range(2):
                t_, b_ = wrot.get()
                for i in range(8):
                    ha, hb = pair_heads(i)
                    P.dma("gpsimd", t_[0:64, i, 0:512], wo[ha * 64:(ha + 1) * 64, half * 512:(half + 1) * 512], writes=[b_], reads=[b_])
                    P.dma("gpsimd", t_[64:128, i, 0:512], wo[hb * 64:(hb + 1) * 64, half * 512:(half + 1) * 512], writes=[b_], reads=[b_])
                for ti, (t0, n) in enumerate(TILES):
                    for nn in range(4):
                        c = half * 4 + nn
                        ps, pb = ps_get()
                        for i in range(8):
                            T(lambda e, ps=ps, t_=t_, i=i, nn=nn, t0=t0, n=n: e.matmul(
                                ps[:, 0:n], lhsT=t_[:, i, nn * 128:(nn + 1) * 128], rhs=hT[:, i, t0:t0 + n], start=(i == 0), stop=(i == 7)),
                              r=[b_, bh[ti]], w=[pb])
                        V(lambda e, ps=ps, c=c, t0=t0, n=n: e.scalar_tensor_tensor(
                            out=xT[:, c, t0:t0 + n], in0=ps[:, 0:n], scalar=bo[:, c:c + 1], in1=xT[:, c, t0:t0 + n], op0=ALU.add, op1=ALU.add),
                          r=[pb, bo_b, bx[ti][c]], w=[bx[ti][c]])


        def deltanet(j):
            w_in = dn_w_in[j]
            I32 = mybir.dt.int32
            cw = carve("cw", [128, 32, 4], F32)
            dtb = carve("dtb", [128, 16], F32)
            negA = carve("negA", [128, 16], F32)
            nwB = carve("nwB", [128, 128], F32)
            one1 = carve("one1", [128, 1], F32)
            eps2 = carve("eps2", [128, 1], F32)
            ones32 = carve("ones32", [128, 128], F32)
            Ltri = [carve(f"Ltri{i}", [128, 128], F32) for i in range(2)]
            Lsel = [carve(f"Lsel{i}", [128, 128], F32) for i in range(2)]
            mk1 = [carve(f"mk1{i}", [128, 128], F32) for i in range(2)]
            mk2 = [carve(f"mk2{i}", [128, 128], F32) for i in range(2)]
            bmask16 = carve("bmask16", [128, 16], F32)
            tmpi = carve("tmpi", [128, 128], I32)
            beta_a = carve("beta_a", [128, 17, 16], F32)
            G_a = carve("G_a", [128, 17, 16], F32)
            bexpG_a = carve("bexpG_a", [128, 17, 16], F32)
            kdecs_a = carve("kdecs_a", [128, 17, 16], F32)
            bset = Buf("dnset")
            btok = Buf("dntok")
            for k_ in range(4):
                P.dma("sync", cw[:, :, k_], dn_conv_w[j, k_].rearrange("(c p) -> p c", p=128), writes=[bset], reads=[bset])
            P.dma("sync", dtb[:, :], dn_dt_bias[j:j + 1, :].partition_broadcast(128).rearrange("p o n -> p (o n)"), writes=[bset], reads=[bset])
            P.dma("sync", negA[:, :], dn_a_log[j:j + 1, :].partition_broadcast(128).rearrange("p o n -> p (o n)"), writes=[bset], reads=[bset])
            P.dma("sync", nwB[:, :], dn_norm_w[j:j + 1, :].partition_broadcast(128).rearrange("p o n -> p (o n)"), writes=[bset], reads=[bset])
            P.fence()
            A(lambda e: e.activation(out=negA[:, :], in_=negA[:, :], func=AF.Exp), r=[bset], w=[bset])
            V(lambda e: e.tensor_scalar(out=negA[:, :], in0=negA[:, :], scalar1=-1.0, scalar2=None, op0=ALU.mult), r=[bset], w=[bset])
            V(lambda e: e.memset(one1[:, :], 1.0), r=[bset], w=[bset])
            V(lambda e: e.memset(eps2[:, :], 1e-6), r=[bset], w=[bset])
            V(lambda e: e.memset(ones32[:, :], 1.0), r=[bset], w=[bset])
            for i, C in enumerate((64, 4)):
                sh = 6 if C == 64 else 2
                G(lambda e, i=i: e.memset(Ltri[i][:, :], 0.0), r=[bset], w=[bset])
                G(lambda e, i=i: e.memset(Lsel[i][:, :], 0.0), r=[bset], w=[bset])
                G(lambda e, i=i: e.memset(mk1[i][:, :], NEG), r=[bset], w=[bset])
                G(lambda e, i=i: e.memset(mk2[i][:, :], NEG), r=[bset], w=[bset])
            P.fence()
            mark_tmp = arena_state["off"]
            pidx = carve("pidx", [128, 128], F32)
            cidx = carve("cidx", [128, 128], F32)
            G(lambda e: e.iota(tmpi[:, :], pattern=[[0, 128]], base=0, channel_multiplier=1), r=[bset], w=[bset])
            G(lambda e: e.tensor_copy(out=pidx[:, :], in_=tmpi[:, :]), r=[bset], w=[bset])
            G(lambda e: e.iota(tmpi[:, :], pattern=[[1, 128]], base=0, channel_multiplier=0), r=[bset], w=[bset])
            G(lambda e: e.tensor_copy(out=cidx[:, :], in_=tmpi[:, :]), r=[bset], w=[bset])
            pch = carve("pch", [128, 128], F32)
            cch = carve("cch", [128, 128], F32)
            same = carve("same", [128, 128], F32)
            tmpf = carve("tmpf", [128, 128], F32)
            for i, C in enumerate((64, 4)):
                sh = 6 if C == 64 else 2
                for src, dst in ((pidx, pch), (cidx, cch)):
                    V(lambda e, src=src: e.tensor_copy(out=tmpi[:, :], in_=src[:, :]), r=[bset], w=[bset])
                    V(lambda e, sh=sh: e.tensor_scalar(out=tmpi[:, :], in0=tmpi[:, :], scalar1=sh, scalar2=None, op0=ALU.arith_shift_right), r=[bset], w=[bset])
                    V(lambda e, dst=dst: e.tensor_copy(out=dst[:, :], in_=tmpi[:, :]), r=[bset], w=[bset])
                V(lambda e: e.tensor_tensor(out=same[:, :], in0=pch[:, :], in1=cch[:, :], op=ALU.is_equal), r=[bset], w=[bset])
                V(lambda e: e.tensor_tensor(out=tmpf[:, :], in0=pidx[:, :], in1=cidx[:, :], op=ALU.is_le), r=[bset], w=[bset])
                V(lambda e, i=i: e.tensor_tensor(out=Ltri[i][:, :], in0=tmpf[:, :], in1=same[:, :], op=ALU.mult), r=[bset], w=[bset])
                V(lambda e, i=i: e.tensor_scalar(out=mk1[i][:, :], in0=Ltri[i][:, :], scalar1=-1.0, scalar2=-NEG, op0=ALU.add, op1=ALU.mult), r=[bset], w=[bset])
                V(lambda e: e.tensor_tensor(out=tmpf[:, :], in0=pidx[:, :], in1=cidx[:, :], op=ALU.is_gt), r=[bset], w=[bset])
                V(lambda e: e.tensor_tensor(out=tmpf[:, :], in0=tmpf[:, :], in1=same[:, :], op=ALU.mult), r=[bset], w=[bset])
                V(lambda e, i=i: e.tensor_scalar(out=mk2[i][:, :], in0=tmpf[:, :], scalar1=-1.0, scalar2=-NEG, op0=ALU.add, op1=ALU.mult), r=[bset], w=[bset])
                V(lambda e, C=C: e.tensor_scalar(out=tmpf[:, :], in0=cch[:, :], scalar1=float(C), scalar2=float(C - 1), op0=ALU.mult, op1=ALU.add), r=[bset], w=[bset])
                V(lambda e, i=i: e.tensor_tensor(out=Lsel[i][:, :], in0=tmpf[:, :], in1=pidx[:, :], op=ALU.is_equal), r=[bset], w=[bset])
            V(lambda e: e.tensor_tensor(out=bmask16[:, :], in0=pch[:, 0:16], in1=cidx[:, 0:16], op=ALU.is_equal), r=[bset], w=[bset])
            P.fence()
            arena_state["off"] = mark_tmp

            wba, wbab = wload(w_in, 0, 8, 6144, 32)
            for blk in range(17):
                t0, n = (blk * 128, 128) if blk < 16 else (TP, TS)
                mi = 0 if blk < 16 else 1
                ti = min(blk // 4, 4)
                ps, pb = ps_get()
                for k in range(8):
                    T(lambda e, ps=ps, k=k, t0=t0, n=n: e.matmul(ps[0:n, 0:32], lhsT=hT[:, k, t0:t0 + n], rhs=wba[:, k, 0:32], start=(k == 0), stop=(k == 7)),
                      r=[wbab, bh[ti]], w=[pb])
                A(lambda e, ps=ps, blk=blk, n=n: e.activation(out=beta_a[0:n, blk, :], in_=ps[0:n, 0:16], func=AF.Sigmoid), r=[pb], w=[btok])
                V(lambda e, ps=ps, blk=blk, n=n: e.tensor_tensor(out=G_a[0:n, blk, :], in0=ps[0:n, 16:32], in1=dtb[0:n, :], op=ALU.add), r=[pb, bset, btok], w=[btok])
                A(lambda e, blk=blk, n=n: e.activation(out=G_a[0:n, blk, :], in_=G_a[0:n, blk, :], func=AF.Exp), r=[btok], w=[btok])
                A(lambda e, blk=blk, n=n: e.activation(out=G_a[0:n, blk, :], in_=G_a[0:n, blk, :], func=AF.Ln, bias=one1[0:n, :], scale=1.0), r=[btok, bset], w=[btok])
                V(lambda e, blk=blk, n=n: e.tensor_tensor(out=kdecs_a[0:n, blk, :], in0=G_a[0:n, blk, :], in1=negA[0:n, :], op=ALU.mult), r=[btok, bset], w=[btok])
                ps2, pb2 = ps_get()
                T(lambda e, ps2=ps2, blk=blk, n=n, mi=mi: e.matmul(ps2[0:n, 0:16], lhsT=Ltri[mi][0:n, 0:n], rhs=kdecs_a[0:n, blk, :], start=True, stop=True), r=[btok, bset], w=[pb2])
                V(lambda e, ps2=ps2, blk=blk, n=n: e.tensor_copy(out=G_a[0:n, blk, :], in_=ps2[0:n, 0:16]), r=[pb2, btok], w=[btok])
                ps3, pb3 = ps_get()
                T(lambda e, ps3=ps3, blk=blk, n=n, mi=mi: e.matmul(ps3[0:n, 0:16], lhsT=Lsel[mi][0:n, 0:n], rhs=G_a[0:n, blk, :], start=True, stop=True), r=[btok, bset], w=[pb3])
                V(lambda e, ps3=ps3, blk=blk, n=n: e.tensor_tensor(out=kdecs_a[0:n, blk, :], in0=ps3[0:n, 0:16], in1=G_a[0:n, blk, :], op=ALU.subtract), r=[pb3, btok], w=[btok])
                A(lambda e, blk=blk, n=n: e.activation(out=kdecs_a[0:n, blk, :], in_=kdecs_a[0:n, blk, :], func=AF.Exp), r=[btok], w=[btok])
                A(lambda e, blk=blk, n=n: e.activation(out=bexpG_a[0:n, blk, :], in_=G_a[0:n, blk, :], func=AF.Exp), r=[btok], w=[btok])
                V(lambda e, blk=blk, n=n: e.tensor_tensor(out=bexpG_a[0:n, blk, :], in0=bexpG_a[0:n, blk, :], in1=beta_a[0:n, blk, :], op=ALU.mult), r=[btok], w=[btok])
            P.fence()

            def OP(eng, name, r, w, **kw):
                return P.op(eng, lambda e, kw=kw, name=name: getattr(e, name)(**kw), r, w)

            def ap4(t, p0, np_, off_ap, dims):
                ta = t if hasattr(t, "tensor") else t[:]
                return bass.AP(ta.tensor, off_ap.offset, [[ta.ap[0][0], np_]] + [list(d) for d in dims])

            TW = 256
            blk_pc = carve("pc", [128, 4 * (TW + 4)], F32)
            pc = blk_pc.rearrange("p (c t) -> p c t", t=TW + 4)
            pcs = carve("pcs", [128, 4, 16, 7], F32)
            blk_cv = carve("cv", [128, 4 * TW], F32)
            cv = blk_cv.rearrange("p (c t) -> p c t", t=TW)
            sqb = carve("sqb", [128, TW], BF16)
            rn = carve("rn", [128, TW], F32)
            kTf = carve("kTf", [128, TW], F32)
            tailtm = carve("tailtm", [128, 512], F32)
            scst = carve("scst", [128, 512], F32)
            qTn = [carve("qTn", [128, TW], BF16)] * 2
            kTn = [carve("kTn", [128, TW], BF16)] * 2
            ktok = [carve("ktok", [128, 2, 128], F32)] * 2
            vtok = [carve("vtokd", [128, 2, 2, 128], F32)] * 2
            zs = [carve(f"zs{i}", [128, 2, 256], F32) for i in range(2)]
            bA = [Buf("dnA")] * 2
            bZ = [Buf(f"dnZ{i}") for i in range(2)]
            dG = carve("dG", [128, 4, 128], F32)
            dd = carve("dd", [128, 4, 128], F32)
            Mp = [carve(f"Mp{i}", [128, 4, 128], BF16) for i in range(2)]
            Np = [carve(f"Np{i}", [128, 4, 128], BF16) for i in range(2)]
            TT = carve("TT", [128, 4, 128], BF16)
            vb = carve("vb", [128, 4, 128], BF16)
            kbg = carve("kbg", [128, 4, 128], BF16)
            bBi = Buf("dnBi")
            bdG = Buf("dG"); bdd = Buf("dd"); bTT = bBi; bvb = Buf("vb"); bkbg = Buf("kbg")
            bMp = [bBi, bBi]
            bNp = [bBi, bBi]
            eGB = [carve(f"eGB{i}", [128, 4, 128], F32) for i in range(2)]
            u32 = [carve(f"u32{i}", [128, 4, 128], F32) for i in range(2)]
            wTb = [carve(f"wTb{i}", [128, 4, 128], BF16) for i in range(2)]
            qgT = [carve(f"qgT{i}", [128, 4, 128], BF16) for i in range(2)]
            aqkT = [carve(f"aqkT{i}", [128, 4, 128], BF16) for i in range(2)]
            kdec = [carve(f"kdec{i}", [128, 4, 128], BF16) for i in range(2)]
            bB = [Buf(f"dnB{i}") for i in range(2)]
            S32 = carve("S32", [128, 2, 128], F32)
            Sbf = carve("Sbf", [128, 2, 128], BF16)
            un_r = Rot(carve, "unew", [128, 2, 128], BF16, 2)
            otok = [carve("otokd", [128, 2, 2, 128], F32)] * 2
            bO = [Buf("dnO")] * 2
            bS = Buf("S")
            on = carve("on", [128, 4, 128], F32)
            ssq = carve("ssq", [128, 4], F32)
            oTb = carve("oTb", [128, 2, TW], BF16)
            bD = Buf("dnD")
            s0_r = Rot(carve, "s0", [128, 128], F32, 2)
            unb_r = Rot(carve, "unb", [128, 128], F32, 2)
            tmp_r = Rot(carve, "tmpu", [128, 128], F32, 1)
            oacc = carve("oacc", [128, 128], F32)
            kdec32 = carve("kdec32", [128, 128], F32)
            aq32 = carve("aq32", [128, 128], F32)
            wT32 = carve("wT32", [128, 2, 64], F32)
            qgT32 = carve("qgT32", [128, 64], F32)
            sn_r = Rot(carve, "sn", [128, 128], F32, 2)
            bSm = Buf("dnSm")
            bAi = Buf("dnAi")
            bAc = [Buf(f"dnAc{c}") for c in range(4)]
            bTail = Buf("dnTail")
            DN_TILES = [(t0, TW) for t0 in range(0, TP, TW)] + [(TP, TS)]
            tcount = 0

            for hg in range(8):
                wA, wAb = wrot.get()
                P.dma("gpsimd", wA[:, :, 0:128], w_in[:, hg * 128:(hg + 1) * 128].rearrange("(c p) n -> p c n", p=128), writes=[wAb], reads=[wAb])
                P.dma("gpsimd", wA[:, :, 128:256], w_in[:, 1024 + hg * 128:1024 + (hg + 1) * 128].rearrange("(c p) n -> p c n", p=128), writes=[wAb], reads=[wAb])
                P.dma("gpsimd", wA[:, :, 256:512], w_in[:, 2048 + hg * 256:2048 + (hg + 1) * 256].rearrange("(c p) n -> p c n", p=128), writes=[wAb], reads=[wAb])
                wZ, wZb = wload(w_in, 0, 8, 4096 + hg * 256, 256)
                wO, wOb = wrot.get()
                for half in range(2):
                    P.dma("gpsimd", wO[:, half * 2:half * 2 + 2, 0:512],
                          dn_w_out[j, hg * 256:(hg + 1) * 256, half * 512:(half + 1) * 512].rearrange("(c p) n -> p c n", p=128), writes=[wOb], reads=[wOb])
                chan = [hg, 8 + hg, 16 + 2 * hg, 17 + 2 * hg]
                for ch in range(4):
                    OP("vector", "memset", [bAc[ch]], [bAc[ch]], ap=pc[:, ch, 0:3], constant=0.0)
                OP("vector", "memset", [bS], [bS], ap=S32[:, :, :], constant=0.0)
                OP("vector", "memset", [bS], [bS], ap=Sbf[:, :, :], constant=0.0)
                for (t0, n) in DN_TILES:
                    samp = t0 >= TP
                    ti = min(t0 // 512, 4)
                    mi = 1 if samp else 0
                    nb, bs = (1, 64) if samp else (2, 128)
                    NP_ = nb * 2
                    par = tcount % 2
                    tcount += 1
                    gblk0 = (t0 // 128) if not samp else 16
                    A_ = bA[par]
                    B_ = bB[par]
                    O_ = bO[par]
                    if samp:
                        for q_, (c0, wd_) in enumerate(((hg * 128, 128), (1024 + hg * 128, 128), (2048 + hg * 256, 256))):
                            so = (0, 128, 256)[q_]
                            P.dma("sync", scst[0:48, so:so + wd_], sconv_in[j, :, c0:c0 + wd_], reads=[bAi], writes=[bAi])
                        ps, pb = ps_get()
                        for ch in range(4):
                            OP("tensor", "transpose", [bAi, bident], [pb], out=ps[:, ch * 48:(ch + 1) * 48], in_=scst[0:48, ch * 128:(ch + 1) * 128], identity=ident[0:48, 0:48])
                        OP("vector", "tensor_copy", [pb] + bAc, bAc, out=pcs[:, :, :, 0:3], in_=ps[:, 0:192].rearrange("p (c b r) -> p c b r", b=16, r=3))
                    for ch in range(4):
                        ps, pb = ps_get()
                        for k in range(8):
                            OP("tensor", "matmul", [wAb, bh[ti]], [pb], out=ps[:, 0:n], lhsT=wA[:, k, ch * 128:(ch + 1) * 128], rhs=hT[:, k, t0:t0 + n],
                               start=(k == 0), stop=(k == 7))
                        if samp:
                            OP("scalar", "copy", [pb, bAc[ch]], [bAc[ch]], out=pcs[:, ch, :, 3:7], in_=ps[:, 0:64].rearrange("p (b t) -> p b t", t=4))
                        else:
                            OP("scalar", "copy", [pb, bAc[ch]], [bAc[ch]], out=pc[:, ch, 3:3 + TW], in_=ps[:, 0:TW])
                    if t0 == TP - TW or samp:
                        r0, nr = (TP - 32, 32) if not samp else (TP, 64)
                        ps, pb = ps_get()
                        for k in range(8):
                            OP("tensor", "matmul", [wAb, bh[ti]], [pb], out=ps[0:nr, :], lhsT=hT[:, k, r0:r0 + nr], rhs=wA[:, k, :], start=(k == 0), stop=(k == 7))
                        OP("vector", "tensor_copy", [pb, bTail], [bTail], out=tailtm[0:nr, :], in_=ps[0:nr, :])
                        for q_, (c0, wd_) in enumerate(((hg * 128, 128), (1024 + hg * 128, 128), (2048 + hg * 256, 256))):
                            so = (0, 128, 256)[q_]
                            if not samp:
                                P.dma("sync", convp_out[j, :, c0:c0 + wd_], tailtm[29:32, so:so + wd_], reads=[bTail])
                            else:
                                for t_ in range(1, 4):
                                    src = bass.AP(tailtm.tensor, tailtm[t_:t_ + 1, so:so + 1].offset, [[tailtm.ap[0][0] * 4, 16], [1, wd_]])
                                    P.dma("sync", convs_out[j, :, t_ - 1, c0:c0 + wd_], src, reads=[bTail])
                    for ch in range(4):
                        cc = chan[ch]
                        if samp:
                            srcs = [pcs[:, ch, :, k:k + 4] for k in range(4)]
                            dst = cv[:, ch, 0:64].rearrange("p (b t) -> p b t", t=4)
                        else:
                            srcs = [pc[:, ch, k:k + TW] for k in range(4)]
                            dst = cv[:, ch, 0:TW]
                        OP("vector", "tensor_scalar", [bAc[ch], bset], [bAc[ch]], out=dst, in0=srcs[0], scalar1=cw[:, cc, 0:1], scalar2=None, op0=ALU.mult)
                        for k in range(1, 4):
                            OP("vector", "scalar_tensor_tensor", [bAc[ch], bset], [bAc[ch]], out=dst, in0=srcs[k], scalar=cw[:, cc, k:k + 1], in1=dst, op0=ALU.mult, op1=ALU.add)
                        OP("scalar", "activation", [bAc[ch]], [bAc[ch]], out=cv[:, ch, 0:n], in_=cv[:, ch, 0:n], func=AF.Silu)
                        if not samp:
                            OP("scalar", "copy", [bAc[ch]], [bAc[ch]], out=pc[:, ch, 0:3], in_=pc[:, ch, TW:TW + 3])
                    for ch, dst, mul in ((0, qTn[par], 128.0 ** -0.5), (1, kTn[par], 1.0)):
                        OP("scalar", "activation", [bAc[ch], bAi], [bAi], out=sqb[:, 0:n], in_=cv[:, ch, 0:n], func=AF.Square)
                        ps, pb = ps_get()
                        OP("tensor", "matmul", [bAi, bconst], [pb], out=ps[:, 0:n], lhsT=ones_bf[:, :], rhs=sqb[:, 0:n], start=True, stop=True)
                        OP("scalar", "activation", [pb, bset, bAi], [bAi], out=rn[:, 0:n], in_=ps[:, 0:n], func=AF.Ln, bias=eps2[:, :], scale=1.0)
                        OP("scalar", "activation", [bAi], [bAi], out=rn[:, 0:n], in_=rn[:, 0:n], func=AF.Exp, scale=-0.5)
                        OP("vector", "scalar_tensor_tensor", [bAi, bAc[ch], A_], [A_], out=dst[:, 0:n], in0=cv[:, ch, 0:n], scalar=mul, in1=rn[:, 0:n], op0=ALU.mult, op1=ALU.mult)
                        if ch == 1:
                            OP("vector", "tensor_tensor", [bAi, bAc[1]], [bAi], out=kTf[:, 0:n], in0=cv[:, 1, 0:n], in1=rn[:, 0:n], op=ALU.mult)
                    for blk in range(nb):
                        ps, pb = ps_get()
                        OP("tensor", "transpose", [bAi, bident], [pb], out=ps[0:bs, 0:128], in_=kTf[:, blk * bs:(blk + 1) * bs], identity=ident[:, :])
                        for hl in range(2):
                            OP("tensor", "transpose", [bAc[2 + hl], bident], [pb], out=ps[0:bs, 128 + hl * 128:256 + hl * 128], in_=cv[:, 2 + hl, blk * bs:(blk + 1) * bs], identity=ident[:, :])
                        OP("vector", "tensor_copy", [pb, A_], [A_], out=ktok[par][0:bs, blk, :], in_=ps[0:bs, 0:128])
                        OP("scalar", "copy", [pb, A_], [A_], out=vtok[par][0:bs, blk, :, :], in_=ps[0:bs, 128:384].rearrange("p (h d) -> p h d", d=128))
                        psz, pzb = ps_get()
                        for k in range(8):
                            OP("tensor", "matmul", [wZb, bh[ti]], [pzb], out=psz[0:bs, 0:256], lhsT=hT[:, k, t0 + blk * bs:t0 + (blk + 1) * bs], rhs=wZ[:, k, 0:256],
                               start=(k == 0), stop=(k == 7))
                        OP("scalar", "activation", [pzb, bZ[par]], [bZ[par]], out=zs[par][0:bs, blk, :], in_=psz[0:bs, 0:256], func=AF.Silu)

                    def P4(t):
                        return t[0:bs, 0:NP_, 0:bs].rearrange("p (b h) c -> p b h c", h=2)

                    def scal4(arr, last):
                        return ap4(arr, 0, bs, arr[0:1, gblk0, 2 * hg:2 * hg + 1], [[16, nb], [1, 2], [0, last]])

                    def bc_tile(t, last):
                        return ap4(t, 0, bs, t[0:1, 0:1], [[0, nb], [0, 2], [1, last]])

                    psk, pkb = ps_get()
                    for blk in range(nb):
                        c0 = blk * bs
                        OP("tensor", "matmul", [A_], [pkb], out=psk[0:bs, blk * 128:blk * 128 + bs], lhsT=kTn[par][:, c0:c0 + bs], rhs=kTn[par][:, c0:c0 + bs], start=True, stop=True)
                        OP("tensor", "matmul", [A_], [pkb], out=psk[0:bs, 256 + blk * 128:256 + blk * 128 + bs], lhsT=kTn[par][:, c0:c0 + bs], rhs=qTn[par][:, c0:c0 + bs], start=True, stop=True)
                    OP("vector", "tensor_tensor", [bdG, btok, bident], [bdG], out=P4(dG), in0=bc_tile(ident, bs), in1=scal4(G_a, bs), op=ALU.mult)
                    psg, pgb = ps_get()
                    OP("tensor", "matmul", [bdG, bset], [pgb], out=psg[:, 0:NP_ * 128], lhsT=ones32[0:bs, :], rhs=dG[0:bs, 0:NP_, :].rearrange("p a c -> p (a c)"), start=True, stop=True)
                    psg4 = psg[0:bs, 0:NP_ * 128].rearrange("p (b h c) -> p b h c", h=2, c=128)[:, :, :, 0:bs]
                    OP("vector", "tensor_tensor", [pgb, btok, bdd], [bdd], out=P4(dd), in0=psg4, in1=scal4(G_a, bs), op=ALU.subtract)
                    OP("scalar", "activation", [pgb, B_], [B_], out=eGB[par][:, 0:NP_, :], in_=psg[:, 0:NP_ * 128].rearrange("p (a c) -> p a c", c=128), func=AF.Exp)
                    OP("vector", "tensor_tensor", [bdd, bdG, bset], [bdG], out=P4(dG), in0=P4(dd), in1=bc_tile(mk1[mi], bs), op=ALU.add)
                    OP("scalar", "activation", [bdG], [bdG], out=dG[0:bs, 0:NP_, 0:bs], in_=dG[0:bs, 0:NP_, 0:bs], func=AF.Exp)
                    OP("vector", "tensor_tensor", [bdd, bset], [bdd], out=P4(dd), in0=P4(dd), in1=bc_tile(mk2[mi], bs), op=ALU.subtract)
                    OP("scalar", "activation", [bdd], [bdd], out=dd[0:bs, 0:NP_, 0:bs], in_=dd[0:bs, 0:NP_, 0:bs], func=AF.Exp, scale=-1.0)
                    qk4 = ap4(psk, 0, bs, psk[0:1, 256:257], [[128, nb], [0, 2], [1, bs]])
                    kk4 = ap4(psk, 0, bs, psk[0:1, 0:1], [[128, nb], [0, 2], [1, bs]])
                    OP("vector", "tensor_tensor", [pkb, bdG, B_], [B_], out=P4(aqkT[par]), in0=qk4, in1=P4(dG), op=ALU.mult)
                    OP("vector", "tensor_tensor", [bdd, btok], [bdd], out=P4(dd), in0=P4(dd), in1=scal4(beta_a, bs), op=ALU.mult)
                    OP("vector", "tensor_tensor", [pkb, bdd], [bdd], out=P4(dd), in0=kk4, in1=P4(dd), op=ALU.mult)
                    OP("scalar", "mul", [bdd], [bdd], out=dd[0:bs, 0:NP_, 0:bs], in_=dd[0:bs, 0:NP_, 0:bs], mul=-1.0)
                    OP("scalar", "copy", [bdd, bMp[0]], [bMp[0]], out=Mp[0][0:bs, 0:NP_, 0:bs], in_=dd[0:bs, 0:NP_, 0:bs])
                    pst, ptb = ps_get()
                    for p in range(NP_):
                        OP("tensor", "transpose", [bdd, bident], [ptb], out=pst[0:bs, p * 128:p * 128 + bs], in_=dd[0:bs, p, 0:bs], identity=ident[0:bs, 0:bs])
                    pst3 = pst[0:bs, 0:NP_ * 128].rearrange("p (a c) -> p a c", c=128)[:, :, 0:bs]
                    OP("scalar", "copy", [ptb, bNp[0]], [bNp[0]], out=Np[0][0:bs, 0:NP_, 0:bs], in_=pst3)
                    OP("vector", "tensor_tensor", [ptb, bident, bTT], [bTT], out=TT[0:bs, 0:NP_, 0:bs], in0=pst3,
                       in1=ap4(ident, 0, bs, ident[0:1, 0:1], [[0, NP_], [1, bs]]), op=ALU.add)
                    cur = 0
                    for it in range(5):
                        nxt = 1 - cur
                        pa, pab = ps_get()
                        for p in range(NP_):
                            OP("tensor", "matmul", [bNp[cur], bMp[cur]], [pab], out=pa[0:bs, p * 128:p * 128 + bs], lhsT=Np[cur][0:bs, p, 0:bs], rhs=Mp[cur][0:bs, p, 0:bs], start=True, stop=True)
                        pa3 = pa[0:bs, 0:NP_ * 128].rearrange("p (a c) -> p a c", c=128)[:, :, 0:bs]
                        if it < 4:
                            pn_, pnb_ = ps_get()
                            for p in range(NP_):
                                OP("tensor", "matmul", [bNp[cur], bMp[cur]], [pnb_], out=pn_[0:bs, p * 128:p * 128 + bs], lhsT=Mp[cur][0:bs, p, 0:bs], rhs=Np[cur][0:bs, p, 0:bs], start=True, stop=True)
                            pn3 = pn_[0:bs, 0:NP_ * 128].rearrange("p (a c) -> p a c", c=128)[:, :, 0:bs]
                        OP("scalar", "copy", [pab, bMp[nxt]], [bMp[nxt]], out=Mp[nxt][0:bs, 0:NP_, 0:bs], in_=pa3)
                        if it < 4:
                            OP("scalar", "copy", [pnb_, bNp[nxt]], [bNp[nxt]], out=Np[nxt][0:bs, 0:NP_, 0:bs], in_=pn3)
                        pu, pub = ps_get()
                        for p in range(NP_):
                            OP("tensor", "matmul", [bMp[nxt], bTT], [pub], out=pu[0:bs, p * 128:p * 128 + bs], lhsT=Mp[nxt][0:bs, p, 0:bs], rhs=TT[0:bs, p, 0:bs], start=True, stop=True)
                        pu3 = pu[0:bs, 0:NP_ * 128].rearrange("p (a c) -> p a c", c=128)[:, :, 0:bs]
                        OP("vector", "tensor_tensor", [pub, bTT], [bTT], out=TT[0:bs, 0:NP_, 0:bs], in0=pu3, in1=TT[0:bs, 0:NP_, 0:bs], op=ALU.add)
                        cur = nxt
                    vt4 = vtok[par][0:bs, 0:nb, :, :]
                    kt4 = ap4(ktok[par], 0, bs, ktok[par][0:1, 0:1, 0:1], [[128, nb], [0, 2], [1, 128]])
                    V4 = lambda t: t[0:bs, 0:NP_, :].rearrange("p (b h) c -> p b h c", h=2)
                    OP("vector", "tensor_tensor", [A_, btok, bvb], [bvb], out=V4(vb), in0=vt4, in1=scal4(beta_a, 128), op=ALU.mult)
                    OP("vector", "tensor_tensor", [A_, btok, bkbg], [bkbg], out=V4(kbg), in0=kt4, in1=scal4(bexpG_a, 128), op=ALU.mult)
                    OP("vector", "tensor_tensor", [A_, btok, B_], [B_], out=V4(kdec[par]), in0=kt4, in1=scal4(kdecs_a, 128), op=ALU.mult)
                    pu, pub = ps_get()
                    pw, pwb = ps_get()
                    for p in range(NP_):
                        OP("tensor", "matmul", [bTT, bvb], [pub], out=pu[0:bs, p * 128:(p + 1) * 128], lhsT=TT[0:bs, p, 0:bs], rhs=vb[0:bs, p, :], start=True, stop=True)
                        OP("tensor", "matmul", [bTT, bkbg], [pwb], out=pw[:, p * 128:p * 128 + bs], lhsT=kbg[0:bs, p, :], rhs=TT[0:bs, p, 0:bs], start=True, stop=True)
                    OP("vector", "tensor_copy", [pub, B_], [B_], out=u32[par][0:bs, 0:NP_, :], in_=pu[0:bs, 0:NP_ * 128].rearrange("p (a c) -> p a c", c=128))
                    pw3 = pw[:, 0:NP_ * 128].rearrange("p (a c) -> p a c", c=128)[:, :, 0:bs]
                    qT4 = ap4(qTn[par], 0, 128, qTn[par][0:1, 0:1], [[bs, nb], [0, 2], [1, bs]])
                    eG4 = eGB[par][:, 0:NP_, 0:bs].rearrange("p (b h) c -> p b h c", h=2)
                    if not samp:
                        OP("scalar", "copy", [pwb, B_], [B_], out=wTb[par][:, 0:NP_, 0:bs], in_=pw3)
                        OP("vector", "tensor_tensor", [A_, B_], [B_], out=qgT[par][:, 0:NP_, 0:bs].rearrange("p (b h) c -> p b h c", h=2), in0=qT4, in1=eG4, op=ALU.mult)
                        for blk in range(nb):
                            for half in range(2):
                                hs = half * 64
                                un, unb = un_r.get()
                                p1, p1b = ps_get()
                                for hl in range(2):
                                    OP("tensor", "matmul", [B_, bS], [p1b], out=p1[hs:hs + 64, hl * 128:(hl + 1) * 128], lhsT=wTb[par][:, blk * 2 + hl, hs:hs + 64], rhs=Sbf[:, hl, :], start=True, stop=True)
                                OP("vector", "tensor_tensor", [p1b, B_, unb], [unb], out=un[hs:hs + 64, :, :],
                                   in0=u32[par][hs:hs + 64, blk * 2:blk * 2 + 2, :], in1=p1[hs:hs + 64, 0:256].rearrange("p (h d) -> p h d", d=128), op=ALU.subtract)
                                p2, p2b = ps_get()
                                for hl in range(2):
                                    OP("tensor", "matmul", [B_, bS], [p2b], out=p2[hs:hs + 64, hl * 128:(hl + 1) * 128], lhsT=qgT[par][:, blk * 2 + hl, hs:hs + 64], rhs=Sbf[:, hl, :], start=True, stop=False)
                                    OP("tensor", "matmul", [B_, unb], [p2b], out=p2[hs:hs + 64, hl * 128:(hl + 1) * 128], lhsT=aqkT[par][hs:hs + 64, blk * 2 + hl, hs:hs + 64], rhs=un[hs:hs + 64, hl, :], start=False, stop=True)
                                OP("scalar", "copy", [p2b, O_], [O_], out=otok[par][hs:hs + 64, blk, :, :], in_=p2[hs:hs + 64, 0:256].rearrange("p (h d) -> p h d", d=128))
                                p3, p3b = ps_get()
                                for hl in range(2):
                                    OP("tensor", "matmul", [B_, unb], [p3b], out=p3[:, hl * 128:(hl + 1) * 128], lhsT=kdec[par][hs:hs + 64, blk * 2 + hl, :], rhs=un[hs:hs + 64, hl, :], start=True, stop=True)
                                for hl in range(2):
                                    OP("vector", "scalar_tensor_tensor", [p3b, B_, bS], [bS], out=S32[:, hl, :], in0=S32[:, hl, :], scalar=eGB[par][:, blk * 2 + hl, hs + 63:hs + 64],
                                       in1=p3[:, hl * 128:(hl + 1) * 128], op0=ALU.mult, op1=ALU.add)
                                OP("scalar", "copy", [bS], [bS], out=Sbf[:, :, :], in_=S32[:, :, :])
                    else:
                        OP("scalar", "copy", [pwb, bSm], [bSm], out=wT32[:, :, :], in_=pw[:, 0:256].rearrange("p (h c) -> p h c", c=128)[:, :, 0:64])
                        for hl in range(2):
                            h = 2 * hg + hl
                            OP("vector", "tensor_tensor", [A_, B_, bSm], [bSm], out=qgT32[:, 0:64], in0=qTn[par][:, 0:64], in1=eGB[par][:, hl, 0:64], op=ALU.mult)
                            OP("vector", "tensor_scalar", [A_, btok, bSm], [bSm], out=kdec32[0:64, :], in0=ktok[par][0:64, 0, :], scalar1=kdecs_a[0:64, 16, h:h + 1], scalar2=None, op0=ALU.mult)
                            OP("vector", "tensor_copy", [B_, bSm], [bSm], out=aq32[0:64, 0:64], in_=aqkT[par][0:64, hl, 0:64])
                            OP("vector", "memset", [bSm], [bSm], ap=oacc[0:64, :], constant=0.0)
                            p2, p2b = psb[7], psB[7]
                            for b in range(NB):
                                s0t, s0b = s0_r.get()
                                P.dma("scalar", s0t[:, :], sdelta_in[j, b, h], writes=[s0b])
                                p1, p1b = ps_get()
                                OP("tensor", "matmul", [bSm, s0b], [p1b], out=p1[0:64, 0:128], lhsT=wT32[:, hl, :], rhs=s0t[:, :], start=True, stop=True)
                                OP("tensor", "matmul", [bSm, s0b], [p1b], out=p1[0:64, 128:256], lhsT=qgT32[:, 0:64], rhs=s0t[:, :], start=True, stop=True)
                                tm, tmb = tmp_r.get()
                                OP("vector", "tensor_tensor", [p1b, B_, tmb], [tmb], out=tm[0:64, :], in0=u32[par][0:64, hl, :], in1=p1[0:64, 0:128], op=ALU.subtract)
                                ub, ubb = unb_r.get()
                                OP("vector", "tensor_scalar", [tmb, bset, ubb], [ubb], out=ub[0:64, :], in0=tm[0:64, :], scalar1=bmask16[0:64, b:b + 1], scalar2=None, op0=ALU.mult)
                                OP("vector", "scalar_tensor_tensor", [p1b, bset, bSm], [bSm], out=oacc[0:64, :], in0=p1[0:64, 128:256], scalar=bmask16[0:64, b:b + 1], in1=oacc[0:64, :],
                                   op0=ALU.mult, op1=ALU.add)
                                OP("tensor", "matmul", [bSm, ubb], [p2b], out=p2[0:64, 0:128], lhsT=aq32[0:64, 0:64], rhs=ub[0:64, :], start=(b == 0), stop=(b == NB - 1))
                                p3, p3b = ps_get()
                                OP("tensor", "matmul", [bSm, ubb], [p3b], out=p3[:, 0:128], lhsT=kdec32[0:64, :], rhs=ub[0:64, :], start=True, stop=True)
                                sn, snb = sn_r.get()
                                OP("vector", "scalar_tensor_tensor", [p3b, B_, s0b, snb], [snb], out=sn[:, :], in0=s0t[:, :], scalar=eGB[par][:, hl, 4 * b + 3:4 * b + 4], in1=p3[:, 0:128],
                                   op0=ALU.mult, op1=ALU.add)
                                P.dma("sync", deltas_out[j, b, h], sn[:, :], reads=[snb])
                            OP("vector", "tensor_tensor", [p2b, bSm, O_], [O_], out=otok[par][0:64, 0, hl, :], in0=p2[0:64, 0:128], in1=oacc[0:64, :], op=ALU.add)
                    o8 = otok[par][0:bs, 0:nb, :, :].rearrange("p b h d -> p (b h) d")
                    nh = NP_
                    OP("scalar", "activation", [O_, bD], [bD], out=on[0:bs, 0:nh, :], in_=o8, func=AF.Square)
                    OP("vector", "tensor_reduce", [bD], [bD], out=ssq[0:bs, 0:nh], in_=on[0:bs, 0:nh, :], axis=AX.X, op=ALU.add)
                    OP("scalar", "activation", [bD, bset], [bD], out=ssq[0:bs, 0:nh], in_=ssq[0:bs, 0:nh], func=AF.Ln, bias=eps2[0:bs, :], scale=1.0 / 128)
                    OP("scalar", "activation", [bD], [bD], out=ssq[0:bs, 0:nh], in_=ssq[0:bs, 0:nh], func=AF.Exp, scale=-0.5)
                    OP("vector", "tensor_tensor", [O_, bD], [bD], out=on[0:bs, 0:nh, :], in0=o8, in1=ap4(ssq, 0, bs, ssq[0:1, 0:1], [[1, nh], [0, 128]]), op=ALU.mult)
                    OP("vector", "tensor_tensor", [bD, bset], [bD], out=on[0:bs, 0:nh, :], in0=on[0:bs, 0:nh, :], in1=ap4(nwB, 0, bs, nwB[0:1, 0:1], [[0, nh], [1, 128]]), op=ALU.mult)
                    OP("vector", "tensor_tensor", [bD, bZ[par]], [bD], out=on[0:bs, 0:nh, :], in0=on[0:bs, 0:nh, :], in1=zs[par][0:bs, 0:nb, :].rearrange("p b (h d) -> p (b h) d", d=128), op=ALU.mult)
                    pso, psob = ps_get()
                    for hl in range(2):
                        for blk in range(nb):
                            OP("tensor", "transpose", [bD, bident], [psob], out=pso[:, hl * 256 + blk * bs:hl * 256 + (blk + 1) * bs], in_=on[0:bs, blk * 2 + hl, :], identity=ident[0:bs, 0:bs])
                    OP("scalar", "copy", [psob, bD], [bD], out=oTb[:, :, 0:n], in_=pso[:, :].rearrange("p (h t) -> p h t", t=256)[:, :, 0:n])
                    for c in range(8):
                        ps, pb = ps_get()
                        for hl in range(2):
                            OP("tensor", "matmul", [wOb, bD], [pb], out=ps[:, 0:n], lhsT=wO[:, (c // 4) * 2 + hl, (c % 4) * 128:(c % 4 + 1) * 128], rhs=oTb[:, hl, 0:n],
                               start=(hl == 0), stop=(hl == 1))
                        OP("vector", "tensor_tensor", [pb, bx[ti][c]], [bx[ti][c]], out=xT[:, c, t0:t0 + n], in0=ps[:, 0:n], in1=xT[:, c, t0:t0 + n], op=ALU.add)
                for hl in range(2):
                    P.dma("sync", deltap_out[j, 2 * hg + hl], S32[:, hl, :], reads=[bS])

        load_x()
        for layer in range(DEPTH):
            if layer % 2 == 0 and stage >= 3:
                arena_reset()
                norm_alloc()
                rmsnorm_h(layer)
                arena_reset()
                deltanet(layer // 2)
            if layer % 2 == 1 and stage >= 2:
                arena_reset()
                norm_alloc()
                rmsnorm_h(layer)
                arena_reset()
                swa(layer // 2)
            arena_reset()
            norm_alloc()
            rmsnorm_h(4 + layer)
            arena_reset()
            ffn(layer)
        arena_reset()
        norm_alloc()
        final_out()
        P.emit()
    return nc


def core_inputs(inputs, c):
    f = lambda a: np.ascontiguousarray(np.asarray(a, dtype=np.float32))
    sl = slice(NB * c, NB * (c + 1))
    return {
        "xp": f(inputs["x_prompt"][c]),
        "xs": f(inputs["x_sample"][sl].reshape(TS, D)),
        "norm_mix": f(inputs["norm_mix"]),
        "norm_ffn": f(inputs["norm_ffn"]),
        "norm_final": f(inputs["norm_final"]).reshape(1, D),
        "ffn_w_gu": f(inputs["ffn_w_gu"]),
        "ffn_w_down": f(inputs["ffn_w_down"]),
        "swa_w_qkv": f(inputs["swa_w_qkv"]),
        "swa_b_qkv": f(inputs["swa_b_qkv"]),
        "swa_sinks": f(inputs["swa_sinks"]),
        "swa_w_o": f(inputs["swa_w_o"]),
        "swa_b_o": f(inputs["swa_b_o"]),
        "dn_w_in": f(inputs["dn_w_in"]),
        "dn_conv_w": f(inputs["dn_conv_w"]),
        "dn_a_log": f(inputs["dn_a_log"]),
        "dn_dt_bias": f(inputs["dn_dt_bias"]),
        "dn_norm_w": f(inputs["dn_norm_w"]),
        "dn_w_out": f(inputs["dn_w_out"]),
        "state_conv": f(inputs["state_conv"][:, sl].reshape(2, NB * 3, 4096)),
        "state_delta": f(inputs["state_delta"][:, sl]),
        "cache_k": f(inputs["cache_k"][:, sl].reshape(2, NB, 128, 256)),
        "cache_v": f(inputs["cache_v"][:, sl].reshape(2, NB, 128, 256)),
    }


def kernel(**inputs):
    nc = build_program()
    in_maps = [core_inputs(inputs, c) for c in range(NCORES)]
    res = run_bass_kernel_spmd(nc, in_maps, core_ids=list(range(NCORES)))
    R = res.results
    f = lambda a: np.asarray(a, dtype=np.float32)
    y_prompt = np.stack([f(R[c]["yp"]) for c in range(NCORES)], 0)
    y_sample = np.concatenate([f(R[c]["ys"]).reshape(NB, 4, D) for c in range(NCORES)], 0)
    conv_p = np.stack([f(R[c]["convp"]) for c in range(NCORES)], 1)
    delta_p = np.stack([f(R[c]["deltap"]) for c in range(NCORES)], 1)
    k_p = np.stack([f(R[c]["kp"]).reshape(2, 128, 4, 64) for c in range(NCORES)], 1)
    v_p = np.stack([f(R[c]["vp"]).reshape(2, 128, 4, 64) for c in range(NCORES)], 1)
    conv_s = np.concatenate([f(R[c]["convs"]) for c in range(NCORES)], 1)
    delta_s = np.concatenate([f(R[c]["deltas"]) for c in range(NCORES)], 1)
    k_s = np.concatenate([f(R[c]["ks"]).reshape(2, NB, 4, 4, 64) for c in range(NCORES)], 1)
    v_s = np.concatenate([f(R[c]["vs"]).reshape(2, NB, 4, 4, 64) for c in range(NCORES)], 1)
    return (y_prompt, y_sample, conv_p, delta_p, k_p, v_p, conv_s, delta_s, k_s, v_s)
```

```python
import contextlib
import numpy as np
import concourse.bass as bass
import concourse.mybir as mybir
from concourse.bass_utils import run_bass_kernel_spmd

F32 = mybir.dt.float32
BF16 = mybir.dt.bfloat16
AF = mybir.ActivationFunctionType
ALU = mybir.AluOpType
AX = mybir.AxisListType

ENGS = ("tensor", "vector", "scalar", "gpsimd", "sync")
SAME_ENG_SYNC = {"tensor": False, "vector": True, "scalar": True, "gpsimd": True, "sync": False}
N_DMA_SEMS = 48


class Buf:
    __slots__ = ("name", "last_w", "readers")

    def __init__(self, name="b"):
        self.name = name
        self.last_w = None
        self.readers = []


class Prog:
    def __init__(self, nc):
        self.nc = nc
        self.ops = {e: [] for e in ENGS}
        self.count = {e: 0 for e in ENGS}
        self.known = {e: {} for e in ENGS}
        self.clock = {}
        self.dma_n = 0
        self.dma_cnt = [0] * N_DMA_SEMS
        self.nwaits = 0

    def _need(self, eng, tok, waits):
        semkey, val, teng = tok
        if teng == eng and not SAME_ENG_SYNC[eng]:
            return
        k = self.known[eng]
        if k.get(semkey, 0) >= val:
            return
        waits.append((semkey, val))
        self.nwaits += 1
        c = self.clock.get((semkey, val))
        if c is not None:
            for s, v in c.items():
                if k.get(s, 0) < v:
                    k[s] = v
        k[semkey] = val

    def _deps(self, eng, reads, writes, waits):
        for b in reads:
            if b.last_w is not None:
                self._need(eng, b.last_w, waits)
        for b in writes:
            if b.last_w is not None:
                self._need(eng, b.last_w, waits)
            for r in b.readers:
                self._need(eng, r, waits)

    def _commit(self, tok, reads, writes):
        for b in reads:
            b.readers.append(tok)
            if len(b.readers) > 16:
                d = {}
                for t in b.readers:
                    if t[0] not in d or d[t[0]][1] < t[1]:
                        d[t[0]] = t
                b.readers = list(d.values())
        for b in writes:
            b.last_w = tok
            b.readers = []

    def op(self, eng, fn, reads=(), writes=()):
        waits = []
        self._deps(eng, reads, writes, waits)
        self.count[eng] += 1
        semkey = ("e", eng)
        val = self.count[eng]
        tok = (semkey, val, eng)
        c = dict(self.known[eng])
        c[semkey] = val
        self.clock[(semkey, val)] = c
        self.ops[eng].append((waits, fn, semkey, 1))
        self._commit(tok, reads, writes)
        return tok

    def dma(self, queue, out, in_, reads=(), writes=(), **kw):
        waits = []
        self._deps(queue, reads, writes, waits)
        j = self.dma_n % N_DMA_SEMS
        self.dma_n += 1
        semkey = ("d", j)
        prev = self.dma_cnt[j]
        if prev:
            self._need(queue, (semkey, prev * 16, "dma"), waits)
        self.dma_cnt[j] += 1
        val = self.dma_cnt[j] * 16
        tok = (semkey, val, "dma")
        self.clock[(semkey, val)] = dict(self.known[queue])

        def fn(e, out=out, in_=in_, kw=kw):
            return e.dma_start(out=out, in_=in_, **kw)

        self.ops[queue].append((waits, fn, semkey, 16))
        self._commit(tok, reads, writes)
        return tok

    def fence(self):
        toks = [(("e", x), self.count[x], x) for x in ENGS if self.count[x] > 0]
        toks += [(("d", j), self.dma_cnt[j] * 16, "dma") for j in range(N_DMA_SEMS) if self.dma_cnt[j]]
        for e in ENGS:
            waits = []
            for t in toks:
                if t[2] != e:
                    self._need(e, t, waits)
            if waits:
                self.ops[e].append((waits, None, None, 0))

    def emit(self):
        nc = self.nc
        with contextlib.ExitStack() as st:
            st.enter_context(nc.allow_non_contiguous_dma(reason="small strided layout DMAs"))
            sems = {}
            for e in ENGS:
                sems[("e", e)] = st.enter_context(nc.semaphore(f"s_{e}"))
            for j in range(N_DMA_SEMS):
                sems[("d", j)] = st.enter_context(nc.semaphore(f"s_d{j}"))
            fin = []
            for j in range(N_DMA_SEMS):
                if self.dma_cnt[j]:
                    fin.append((("d", j), self.dma_cnt[j] * 16))
            for e in ENGS:
                if e != "sync" and self.count[e]:
                    fin.append((("e", e), self.count[e]))
            block = st.enter_context(nc.Block())
            for e in ENGS:
                ops = self.ops[e]
                extra = fin if e == "sync" else []

                def body(eng, ops=ops, extra=extra):
                    for waits, fn, semkey, inc in ops:
                        for sk, v in waits:
                            eng.wait_ge(sems[sk], v)
                        if fn is not None:
                            inst = fn(eng)
                            inst.then_inc(sems[semkey], inc)
                    for sk, v in extra:
                        eng.wait_ge(sems[sk], v)

                getattr(block, e)(body)


D = 1024
TP = 2048
NB = 16
TS = 64
NT = TP + TS
DEPTH = 4
DFF = 2816
DN_IN = 6176
RMS_EPS = 1e-6
TILES = [(0, 512), (512, 512), (1024, 512), (1536, 512), (2048, 64)]
NCORES = 8


class Rot:
    def __init__(self, alloc, name, shape, dtype, n):
        self.t = [alloc(f"{name}{i}", shape, dtype) for i in range(n)]
        self.b = [Buf(f"{name}{i}") for i in range(n)]
        self.i = 0

    def get(self):
        i = self.i
        self.i = (i + 1) % len(self.t)
        return self.t[i], self.b[i]


def build_program(stage=99, debug=False):
    nc = bass.Bass("TRN2", target_bir_lowering=False)
    dbg_state = {"n": 0}

    def dump(P, name, ap, reads, once=True):
        if not debug:
            return
        key = "dbg_" + name
        if once and key in dbg_state:
            return
        dbg_state[key] = 1
        shp = list(ap.shape)
        dt_ = ap.dtype
        t = nc.dram_tensor(key, shp, dt_, kind="ExternalOutput").ap()
        P.dma("sync", t, ap, reads=reads)

    def din(name, shape):
        return nc.dram_tensor(name, list(shape), F32, kind="ExternalInput").ap()

    def dout(name, shape):
        return nc.dram_tensor(name, list(shape), F32, kind="ExternalOutput").ap()

    xp = din("xp", [TP, D])
    xs = din("xs", [TS, D])
    norm_mix = din("norm_mix", [DEPTH, D])
    norm_ffn = din("norm_ffn", [DEPTH, D])
    norm_final = din("norm_final", [1, D])
    ffn_w_gu = din("ffn_w_gu", [DEPTH, D, 2 * DFF])
    ffn_w_down = din("ffn_w_down", [DEPTH, DFF, D])
    swa_w_qkv = din("swa_w_qkv", [2, D, 1536])
    swa_b_qkv = din("swa_b_qkv", [2, 1536])
    swa_sinks = din("swa_sinks", [2, 16])
    swa_w_o = din("swa_w_o", [2, D, D])
    swa_b_o = din("swa_b_o", [2, D])
    ck_in = din("cache_k", [2, NB, 128, 256])
    cv_in = din("cache_v", [2, NB, 128, 256])
    dn_w_in = din("dn_w_in", [2, D, DN_IN])
    dn_conv_w = din("dn_conv_w", [2, 4, 4096])
    dn_a_log = din("dn_a_log", [2, 16])
    dn_dt_bias = din("dn_dt_bias", [2, 16])
    dn_norm_w = din("dn_norm_w", [2, 128])
    dn_w_out = din("dn_w_out", [2, 2048, D])
    sconv_in = din("state_conv", [2, NB * 3, 4096])
    sdelta_in = din("state_delta", [2, NB, 16, 128, 128])
    yp = dout("yp", [TP, D])
    ys = dout("ys", [TS, D])
    convp_out = dout("convp", [2, 3, 4096])
    deltap_out = dout("deltap", [2, 16, 128, 128])
    convs_out = dout("convs", [2, NB, 3, 4096])
    deltas_out = dout("deltas", [2, NB, 16, 128, 128])
    kp_out = dout("kp", [2, 128, 256])
    vp_out = dout("vp", [2, 128, 256])
    ks_out = dout("ks", [2, TS, 256])
    vs_out = dout("vs", [2, TS, 256])

    with contextlib.ExitStack() as st:
        def sb(name, shape, dt):
            return st.enter_context(nc.sbuf_tensor(name, list(shape), dt))[:]

        arena_state = {"off": 0, "ap": None, "size": 0}

        def carve(name, shape, dt):
            shape = list(shape)
            n = 1
            for x in shape[1:]:
                n *= x
            isz = 2 if dt == BF16 else 4
            n32 = (n * isz + 3) // 4
            n32 = (n32 + 7) // 8 * 8
            off = arena_state["off"]
            assert off + n32 <= arena_state["size"], (name, off, n32, arena_state["size"])
            arena_state["off"] = off + n32
            ap = arena_state["ap"][:, off:off + n32]
            if dt != F32:
                ap = ap.bitcast(dt)
            ap = ap[:, 0:n]
            if len(shape) == 3:
                ap = ap.rearrange("p (a b) -> p a b", b=shape[2])
            elif len(shape) == 4:
                ap = ap.rearrange("p (a b c) -> p a b c", b=shape[2], c=shape[3])
            return ap

        def arena_reset():
            P.fence()
            arena_state["off"] = 0

        P = Prog(nc)
        V = lambda fn, r=(), w=(): P.op("vector", fn, r, w)
        A = lambda fn, r=(), w=(): P.op("scalar", fn, r, w)
        T = lambda fn, r=(), w=(): P.op("tensor", fn, r, w)
        G = lambda fn, r=(), w=(): P.op("gpsimd", fn, r, w)

        xT = sb("xT", [128, 8, NT], F32)
        hT = sb("hT", [128, 8, NT], BF16)
        bx = [[Buf(f"x{t}_{c}") for c in range(8)] for t in range(5)]
        bh = [Buf(f"h{t}") for t in range(5)]

        ident = sb("ident", [128, 128], F32)
        ones_bf = sb("ones_bf", [128, 128], BF16)
        eps_rms = sb("eps_rms", [128, 1], F32)
        nw = sb("nw", [128, 9, 8], F32)
        bconst = Buf("const")
        bnw = Buf("nw")
        bident = Buf("ident")
        G(lambda e: e.memset(ident[:], 1.0), w=[bident])
        G(lambda e: e.affine_select(out=ident[:], in_=ident[:], pattern=[[-1, 128]],
                                    compare_op=ALU.is_equal, fill=0.0, base=0, channel_multiplier=1),
          r=[bident], w=[bident])
        V(lambda e: e.memset(ones_bf[:], 1.0), w=[bconst])
        V(lambda e: e.memset(eps_rms[:], RMS_EPS), r=[bconst], w=[bconst])
        P.dma("sync", nw[:, 0:4, :], norm_mix.rearrange("l (c p) -> p l c", p=128), writes=[bnw])
        P.dma("sync", nw[:, 4:8, :], norm_ffn.rearrange("l (c p) -> p l c", p=128), reads=[bnw], writes=[bnw])
        P.dma("sync", nw[:, 8:9, :], norm_final.rearrange("l (c p) -> p l c", p=128), reads=[bnw], writes=[bnw])
        P.fence()

        psb = [st.enter_context(nc.psum_tensor(f"ps{i}", [128, 512], F32)) for i in range(8)]
        psB = [Buf(f"ps{i}") for i in range(8)]
        pstate = {"i": 0}

        def ps_get():
            i = pstate["i"]
            pstate["i"] = (i + 1) % 6
            return psb[i], psB[i]

        wrot = Rot(sb, "wslot", [128, 8, 512], BF16, 4)
        asz = (nc.sbuf_bytes_remaining - 256) // 4 // 8 * 8
        arena_state["ap"] = sb("arena", [128, asz], F32)
        arena_state["size"] = asz

        def wload(dram2d, r0, nrow_chunks, c0, ncols):
            t, b = wrot.get()
            src = dram2d[r0:r0 + nrow_chunks * 128, c0:c0 + ncols].rearrange("(c p) n -> p c n", p=128)
            P.dma("gpsimd", t[:, 0:nrow_chunks, 0:ncols], src, writes=[b])
            return t, b


        def load_x():
            xin = Rot(carve, "xin", [128, D], F32, 3)
            blocks = [(xp, t0, 128, t0) for t0 in range(0, TP, 128)] + [(xs, 0, 64, TP)]
            for src, r0, n, tok0 in blocks:
                t, b = xin.get()
                P.dma("sync", t[0:n, :], src[r0:r0 + n, :], writes=[b])
                tt = min(tok0 // 512, 4)
                for half in range(2):
                    ps, pb = ps_get()
                    for cc in range(4):
                        c = half * 4 + cc
                        T(lambda e, ps=ps, t=t, c=c, cc=cc, n=n: e.transpose(
                            out=ps[:, cc * 128:cc * 128 + n], in_=t[0:n, c * 128:(c + 1) * 128],
                            identity=ident[0:n, 0:n]), r=[b, bident], w=[pb])
                    V(lambda e, ps=ps, half=half, tok0=tok0, n=n: e.tensor_copy(
                        out=xT[:, half * 4:half * 4 + 4, tok0:tok0 + n],
                        in_=ps[:].rearrange("p (c t) -> p c t", t=128)[:, :, 0:n]),
                      r=[pb], w=[bx[tt][c] for c in range(half * 4, half * 4 + 4)])

        nrm = {}

        def norm_alloc():
            nrm["sq"] = Rot(carve, "sq", [128, 8, 512], BF16, 2)
            nrm["rs"] = Rot(carve, "rs", [128, 512], F32, 2)

        def rmsnorm(widx, out_fn, out_bufs_fn):
            for ti, (t0, n) in enumerate(TILES):
                sq, sqb = nrm["sq"].get()
                A(lambda e, sq=sq, t0=t0, n=n: e.activation(out=sq[:, :, 0:n], in_=xT[:, :, t0:t0 + n], func=AF.Square),
                  r=bx[ti], w=[sqb])
                ps, pb = ps_get()
                for c in range(8):
                    T(lambda e, ps=ps, sq=sq, c=c, n=n: e.matmul(ps[:, 0:n], lhsT=ones_bf[:], rhs=sq[:, c, 0:n],
                                                               start=(c == 0), stop=(c == 7)),
                      r=[sqb, bconst], w=[pb])
                rs, rsb = nrm["rs"].get()
                A(lambda e, rs=rs, ps=ps, n=n: e.activation(out=rs[:, 0:n], in_=ps[:, 0:n], func=AF.Sqrt,
                                                           bias=eps_rms[:], scale=1.0 / D),
                  r=[pb, bconst], w=[rsb])
                V(lambda e, rs=rs, n=n: e.reciprocal(out=rs[:, 0:n], in_=rs[:, 0:n]), r=[rsb], w=[rsb])
                for c in range(8):
                    o_ap, o_bufs = out_fn(ti, c, t0, n), out_bufs_fn(ti, c)
                    V(lambda e, o_ap=o_ap, c=c, t0=t0, n=n, rs=rs: e.scalar_tensor_tensor(
                        out=o_ap, in0=xT[:, c, t0:t0 + n], scalar=nw[:, widx, c:c + 1], in1=rs[:, 0:n],
                        op0=ALU.mult, op1=ALU.mult),
                      r=[bx[ti][c], rsb, bnw], w=o_bufs)

        def rmsnorm_h(widx):
            rmsnorm(widx, lambda ti, c, t0, n: hT[:, c, t0:t0 + n], lambda ti, c: [bh[ti]])

        def ffn(layer):
            act = carve("act", [128, 8, NT], BF16)
            bact = [Buf(f"act{t}") for t in range(5)]
            sgr = Rot(carve, "sg", [128, 512], F32, 3)
            wgu = ffn_w_gu[layer]
            wdn = ffn_w_down[layer]
            for (j0, Gn) in [(0, 8), (8, 8), (16, 6)]:
                for s in range((Gn + 3) // 4):
                    nch = min(4, Gn - 4 * s)
                    wg, wgb = wload(wgu, 0, 8, (j0 + 4 * s) * 128, nch * 128)
                    wu, wub = wload(wgu, 0, 8, DFF + (j0 + 4 * s) * 128, nch * 128)
                    for jj in range(nch):
                        j = 4 * s + jj
                        for ti, (t0, n) in enumerate(TILES):
                            psg, pgb = ps_get()
                            for k in range(8):
                                T(lambda e, psg=psg, wg=wg, k=k, jj=jj, t0=t0, n=n: e.matmul(
                                    psg[:, 0:n], lhsT=wg[:, k, jj * 128:(jj + 1) * 128], rhs=hT[:, k, t0:t0 + n],
                                    start=(k == 0), stop=(k == 7)), r=[wgb, bh[ti]], w=[pgb])
                            psu, pub = ps_get()
                            for k in range(8):
                                T(lambda e, psu=psu, wu=wu, k=k, jj=jj, t0=t0, n=n: e.matmul(
                                    psu[:, 0:n], lhsT=wu[:, k, jj * 128:(jj + 1) * 128], rhs=hT[:, k, t0:t0 + n],
                                    start=(k == 0), stop=(k == 7)), r=[wub, bh[ti]], w=[pub])
                            sg, sgb = sgr.get()
                            A(lambda e, sg=sg, psg=psg, n=n: e.activation(out=sg[:, 0:n], in_=psg[:, 0:n], func=AF.Silu),
                              r=[pgb], w=[sgb])
                            V(lambda e, sg=sg, psu=psu, j=j, t0=t0, n=n: e.tensor_tensor(
                                out=act[:, j, t0:t0 + n], in0=sg[:, 0:n], in1=psu[:, 0:n], op=ALU.mult),
                              r=[sgb, pub], w=[bact[ti]])
                for half in range(2):
                    wd, wdb = wload(wdn, j0 * 128, Gn, half * 512, 512)
                    for ti, (t0, n) in enumerate(TILES):
                        for nn in range(4):
                            c = half * 4 + nn
                            ps, pb = ps_get()
                            for kk in range(Gn):
                                T(lambda e, ps=ps, wd=wd, kk=kk, nn=nn, t0=t0, n=n, Gn=Gn: e.matmul(
                                    ps[:, 0:n], lhsT=wd[:, kk, nn * 128:(nn + 1) * 128], rhs=act[:, kk, t0:t0 + n],
                                    start=(kk == 0), stop=(kk == Gn - 1)), r=[wdb, bact[ti]], w=[pb])
                            V(lambda e, ps=ps, c=c, t0=t0, n=n: e.tensor_tensor(
                                out=xT[:, c, t0:t0 + n], in0=ps[:, 0:n], in1=xT[:, c, t0:t0 + n], op=ALU.add),
                              r=[pb, bx[ti][c]], w=[bx[ti][c]])

        def final_out():
            yfr = Rot(carve, "yf", [128, 8, 512], F32, 1)
            your = Rot(carve, "yout", [128, D], F32, 2)
            yf, yfb = yfr.get()
            for ti, (t0, n) in enumerate(TILES):
                sq, sqb = nrm["sq"].get()
                A(lambda e, sq=sq, t0=t0, n=n: e.activation(out=sq[:, :, 0:n], in_=xT[:, :, t0:t0 + n], func=AF.Square),
                  r=bx[ti], w=[sqb])
                ps, pb = ps_get()
                for c in range(8):
                    T(lambda e, ps=ps, sq=sq, c=c, n=n: e.matmul(ps[:, 0:n], lhsT=ones_bf[:], rhs=sq[:, c, 0:n],
                                                               start=(c == 0), stop=(c == 7)),
                      r=[sqb, bconst], w=[pb])
                rs, rsb = nrm["rs"].get()
                A(lambda e, rs=rs, ps=ps, n=n: e.activation(out=rs[:, 0:n], in_=ps[:, 0:n], func=AF.Sqrt,
                                                           bias=eps_rms[:], scale=1.0 / D),
                  r=[pb, bconst], w=[rsb])
                V(lambda e, rs=rs, n=n: e.reciprocal(out=rs[:, 0:n], in_=rs[:, 0:n]), r=[rsb], w=[rsb])
                for c in range(8):
                    V(lambda e, c=c, t0=t0, n=n, rs=rs, yf=yf: e.scalar_tensor_tensor(
                        out=yf[:, c, 0:n], in0=xT[:, c, t0:t0 + n], scalar=nw[:, 8, c:c + 1], in1=rs[:, 0:n],
                        op0=ALU.mult, op1=ALU.mult),
                      r=[bx[ti][c], rsb, bnw], w=[yfb])
                for s0 in range(0, n, 128):
                    m = min(128, n - s0)
                    yo, yob = your.get()
                    for half in range(2):
                        ps, pb = ps_get()
                        for cc in range(4):
                            c = half * 4 + cc
                            T(lambda e, ps=ps, yf=yf, c=c, cc=cc, s0=s0, m=m: e.transpose(
                                out=ps[0:m, cc * 128:(cc + 1) * 128], in_=yf[:, c, s0:s0 + m], identity=ident[:]),
                              r=[yfb, bident], w=[pb])
                        A(lambda e, ps=ps, yo=yo, half=half, m=m: e.copy(out=yo[0:m, half * 512:(half + 1) * 512], in_=ps[0:m, :]),
                          r=[pb], w=[yob])
                    dst = yp[t0 + s0:t0 + s0 + m, :] if t0 < TP else ys[s0:s0 + m, :]
                    P.dma("sync", dst, yo[0:m, :], reads=[yob])


        SLOPES = [2.0 ** (-8.0 * (h + 1) / 16) for h in range(16)]
        SCALE = 0.125
        NEG = -30000.0

        def pair_heads(i):
            kc, g = divmod(i, 4)
            return (2 * kc) * 4 + g, (2 * kc + 1) * 4 + g

        def swa(j):
            wqkv = swa_w_qkv[j]
            qT = carve("qT", [128, 8, NT], BF16)
            kT = carve("kT", [128, 2, NT], BF16)
            vtok = carve("vtok", [128, 17, 4, 65], BF16)
            maskT = carve("maskT", [128, 256], F32)
            relT = carve("relT", [128, 256], F32)
            reli = carve("reli", [128, 256], mybir.dt.int32) if False else None
            bq = carve("bq", [128, 8], F32)
            bk = carve("bk", [128, 2], F32)
            bo = carve("bo", [128, 8], F32)
            bkv = carve("bkv", [128, 512], F32)
            sinkEB = carve("sinkEB", [128, 16], F32)
            sink16 = carve("sink16", [128, 4], F32)
            biasC = carve("biasC", [128, 16, 4], F32)
            relC = carve("relC", [128, 4], F32)
            maskC = carve("maskC", [128, 4], F32)
            relN = carve("relN", [128, 16, 4], F32)
            maskN = carve("maskN", [128, 16, 4], F32)
            biasN = carve("biasN", [128, 16, 16, 4], F32)
            bq_b, bk_b, bo_b, bkv_b, bsink, bmask, bvt = [Buf(x) for x in "bq bk bo bkv sink mask vtok".split()]
            bqT = [Buf(f"qT{t}") for t in range(5)]
            bkT = [Buf(f"kT{t}") for t in range(5)]
            bvtok = [Buf(f"vt{b}") for b in range(17)]
            e_r = Rot(carve, "e", [128, 256], F32, 2)
            p_r = Rot(carve, "p", [128, 256], BF16, 2)
            otok_r = Rot(carve, "otok", [128, 8, 128], F32, 1)
            den_r = Rot(carve, "den", [128, 4], F32, 2)
            kvtm_r = Rot(carve, "kvtm", [128, 256], F32, 1)
            ckst_r = Rot(carve, "ckst", [128, 256], F32, 1)
            cvst_r = Rot(carve, "cvst", [128, 256], F32, 1)
            kct_r = Rot(carve, "kct", [128, 2, 128], BF16, 2)
            vc_r = Rot(carve, "vc", [128, 4, 65], BF16, 2)
            es_r = Rot(carve, "es", [128, 64], F32, 2)
            psb_r = Rot(carve, "psb", [128, 64], BF16, 2)
            esn_r = Rot(carve, "esn", [128, 64], F32, 2)
            pn_r = Rot(carve, "pn", [128, 64], BF16, 2)
            os_r = Rot(carve, "os", [128, 4, 64], F32, 1)

            for i in range(8):
                ha, hb = pair_heads(i)
                P.dma("sync", bq[0:64, i:i + 1], swa_b_qkv[j, ha * 64:(ha + 1) * 64].rearrange("(p o) -> p o", o=1), writes=[bq_b], reads=[bq_b])
                P.dma("sync", bq[64:128, i:i + 1], swa_b_qkv[j, hb * 64:(hb + 1) * 64].rearrange("(p o) -> p o", o=1), writes=[bq_b], reads=[bq_b])
            P.dma("sync", bk[:, :], swa_b_qkv[j, 1024:1280].rearrange("(c p) -> p c", p=128), writes=[bk_b])
            P.dma("sync", bo[:, :], swa_b_o[j].rearrange("(c p) -> p c", p=128), writes=[bo_b])
            P.dma("sync", bkv[:, :], swa_b_qkv[j:j + 1, 1024:1536].partition_broadcast(128).rearrange("p o n -> p (o n)"), writes=[bkv_b])
            P.dma("sync", sinkEB[:, :], swa_sinks[j:j + 1, :].partition_broadcast(128).rearrange("p o n -> p (o n)"), writes=[bsink])
            A(lambda e: e.activation(out=sinkEB[:, :], in_=sinkEB[:, :], func=AF.Exp), r=[bsink], w=[bsink])
            for g in range(4):
                P.dma("sync", sink16[4 * g:4 * g + 4, 0:4],
                      swa_sinks[j:j + 1, :].rearrange("o (kv g) -> o kv g", g=4)[:, :, g].partition_broadcast(4).rearrange("p o n -> p (o n)"),
                      writes=[bsink], reads=[bsink])
            A(lambda e: e.activation(out=sink16[0:16, :], in_=sink16[0:16, :], func=AF.Exp), r=[bsink], w=[bsink])
            P.fence()
            I32 = mybir.dt.int32
            ri = carve("ri", [128, 256], I32)
            G(lambda e: e.iota(ri[:, 0:128], pattern=[[1, 128]], base=128, channel_multiplier=-1), w=[bmask])
            G(lambda e: e.iota(ri[:, 128:256], pattern=[[1, 128]], base=0, channel_multiplier=-1), r=[bmask], w=[bmask])
            G(lambda e: e.tensor_copy(out=relT[:, :], in_=ri[:, :]), r=[bmask], w=[bmask])
            G(lambda e: e.memset(maskT[:, :], 0.0), r=[bmask], w=[bmask])
            G(lambda e: e.affine_select(out=maskT[:, 0:128], in_=maskT[:, 0:128], pattern=[[-1, 128]],
                                        compare_op=ALU.is_ge, fill=NEG, base=0, channel_multiplier=1), r=[bmask], w=[bmask])
            G(lambda e: e.affine_select(out=maskT[:, 128:256], in_=maskT[:, 128:256], pattern=[[1, 128]],
                                        compare_op=ALU.is_ge, fill=NEG, base=0, channel_multiplier=-1), r=[bmask], w=[bmask])
            G(lambda e: e.iota(ri[:, 0:4], pattern=[[1, 4]], base=128, channel_multiplier=-1), r=[bmask], w=[bmask])
            G(lambda e: e.tensor_copy(out=relC[:, :], in_=ri[:, 0:4]), r=[bmask], w=[bmask])
            G(lambda e: e.memset(maskC[:, :], 0.0), r=[bmask], w=[bmask])
            G(lambda e: e.affine_select(out=maskC[:, :], in_=maskC[:, :], pattern=[[-1, 4]],
                                        compare_op=ALU.is_ge, fill=NEG, base=0, channel_multiplier=1), r=[bmask], w=[bmask])
            G(lambda e: e.iota(ri[:, 0:64], pattern=[[4, 16], [1, 4]], base=0, channel_multiplier=-1), r=[bmask], w=[bmask])
            G(lambda e: e.tensor_copy(out=relN[:, :, :], in_=ri[:, 0:64].rearrange("p (b t) -> p b t", t=4)), r=[bmask], w=[bmask])
            G(lambda e: e.memset(maskN[:, :, :], 0.0), r=[bmask], w=[bmask])
            G(lambda e: e.affine_select(out=maskN[:, :, :], in_=maskN[:, :, :], pattern=[[4, 16], [1, 4]],
                                        compare_op=ALU.is_ge, fill=NEG, base=0, channel_multiplier=-1), r=[bmask], w=[bmask])
            G(lambda e: e.affine_select(out=maskN[:, :, :], in_=maskN[:, :, :], pattern=[[-4, 16], [0, 4]],
                                        compare_op=ALU.is_ge, fill=NEG, base=0, channel_multiplier=1), r=[bmask], w=[bmask])
            for h in range(16):
                V(lambda e, h=h: e.scalar_tensor_tensor(out=biasC[:, h, :], in0=relC[:, :], scalar=-SLOPES[h], in1=maskC[:, :],
                                                        op0=ALU.mult, op1=ALU.add), r=[bmask], w=[bmask])
                V(lambda e, h=h: e.scalar_tensor_tensor(out=biasN[:, :, h, :], in0=relN[:, :, :], scalar=-SLOPES[h], in1=maskN[:, :, :],
                                                        op0=ALU.mult, op1=ALU.add), r=[bmask], w=[bmask])
            V(lambda e: e.memset(vtok[:, :, :, 64:65], 1.0), w=[bvt])
            P.fence()

            for s_ in range(2):
                wq_, wqb = wrot.get()
                for ii in range(4):
                    ha, hb = pair_heads(s_ * 4 + ii)
                    for hf, hh in ((0, ha), (1, hb)):
                        P.dma("gpsimd", wq_[:, :, ii * 128 + hf * 64:ii * 128 + hf * 64 + 64],
                              wqkv[:, hh * 64:(hh + 1) * 64].rearrange("(c p) n -> p c n", p=128), writes=[wqb], reads=[wqb])
                for ii in range(4):
                    i = s_ * 4 + ii
                    for ti, (t0, n) in enumerate(TILES):
                        ps, pb = ps_get()
                        for k in range(8):
                            T(lambda e, ps=ps, wq_=wq_, ii=ii, k=k, t0=t0, n=n: e.matmul(
                                ps[:, 0:n], lhsT=wq_[:, k, ii * 128:(ii + 1) * 128], rhs=hT[:, k, t0:t0 + n], start=(k == 0), stop=(k == 7)),
                              r=[wqb, bh[ti]], w=[pb])
                        A(lambda e, ps=ps, i=i, t0=t0, n=n: e.activation(out=qT[:, i, t0:t0 + n], in_=ps[:, 0:n], func=AF.Identity,
                                                                        bias=bq[:, i:i + 1], scale=1.0),
                          r=[pb, bq_b], w=[bqT[ti]])
            wkv, wkvb = wload(wqkv, 0, 8, 1024, 512)
            for c in range(2):
                for ti, (t0, n) in enumerate(TILES):
                    ps, pb = ps_get()
                    for k in range(8):
                        T(lambda e, ps=ps, c=c, k=k, t0=t0, n=n: e.matmul(
                            ps[:, 0:n], lhsT=wkv[:, k, c * 128:(c + 1) * 128], rhs=hT[:, k, t0:t0 + n], start=(k == 0), stop=(k == 7)),
                          r=[wkvb, bh[ti]], w=[pb])
                    A(lambda e, ps=ps, c=c, t0=t0, n=n: e.activation(out=kT[:, c, t0:t0 + n], in_=ps[:, 0:n], func=AF.Identity,
                                                                    bias=bk[:, c:c + 1], scale=1.0),
                      r=[pb, bk_b], w=[bkT[ti]])
            for blk in range(17):
                t0, n = (blk * 128, 128) if blk < 16 else (TP, TS)
                ti = min(blk // 4, 4)
                need_k = blk >= 15
                ps, pb = ps_get()
                c0 = 0 if need_k else 256
                for k in range(8):
                    T(lambda e, ps=ps, k=k, t0=t0, n=n, c0=c0: e.matmul(
                        ps[0:n, c0:512], lhsT=hT[:, k, t0:t0 + n], rhs=wkv[:, k, c0:512], start=(k == 0), stop=(k == 7)),
                      r=[wkvb, bh[ti]], w=[pb])
                if need_k:
                    for which, c1, dst in ((0, 0, (kp_out[j] if blk == 15 else ks_out[j])), (1, 256, (vp_out[j] if blk == 15 else vs_out[j]))):
                        kv_, kvb_ = kvtm_r.get()
                        V(lambda e, ps=ps, kv_=kv_, c1=c1, n=n: e.tensor_tensor(out=kv_[0:n, :], in0=ps[0:n, c1:c1 + 256], in1=bkv[0:n, c1:c1 + 256], op=ALU.add),
                          r=[pb, bkv_b], w=[kvb_])
                        P.dma("sync", dst[0:n, :], kv_[0:n, :], reads=[kvb_])
                        if which == 1:
                            V(lambda e, kv_=kv_, blk=blk, n=n: e.tensor_copy(out=vtok[0:n, blk, :, 0:64], in_=kv_[0:n, :].rearrange("p (k d) -> p k d", d=64)),
                              r=[kvb_, bvt], w=[bvtok[blk]])
                else:
                    V(lambda e, ps=ps, blk=blk, n=n: e.tensor_tensor(out=vtok[0:n, blk, :, 0:64], in0=ps[0:n, 256:512].rearrange("p (k d) -> p k d", d=64),
                                                                    in1=bkv[0:n, 256:512].rearrange("p (k d) -> p k d", d=64), op=ALU.add),
                      r=[pb, bkv_b, bvt], w=[bvtok[blk]])

            for qb in range(16):
                ti = qb // 4
                q0 = qb * 128
                ot, otb = otok_r.get()
                for kv in range(4):
                    half, kc = kv % 2, kv // 2
                    pso, psob = ps_get()
                    for g in range(4):
                        h = kv * 4 + g
                        i = kc * 4 + g
                        ps, pb = ps_get()
                        rb = [bkT[ti], bqT[ti]] + ([bkT[(qb - 1) // 4]] if qb > 0 else [])
                        T(lambda e, ps=ps, half=half, kc=kc, i=i, q0=q0: e.matmul(
                            ps[:, 128:256], lhsT=kT[half * 64:(half + 1) * 64, kc, q0:q0 + 128],
                            rhs=qT[half * 64:(half + 1) * 64, i, q0:q0 + 128], start=True, stop=True), r=rb, w=[pb])
                        if qb > 0:
                            T(lambda e, ps=ps, half=half, kc=kc, i=i, q0=q0: e.matmul(
                                ps[:, 0:128], lhsT=kT[half * 64:(half + 1) * 64, kc, q0 - 128:q0],
                                rhs=qT[half * 64:(half + 1) * 64, i, q0:q0 + 128], start=True, stop=True), r=rb, w=[pb])
                        lo = 0 if qb > 0 else 128
                        e_, eb_ = e_r.get()
                        V(lambda e, ps=ps, e_=e_, lo=lo: e.scalar_tensor_tensor(out=e_[:, lo:256], in0=ps[:, lo:256], scalar=SCALE, in1=maskT[:, lo:256],
                                                                             op0=ALU.mult, op1=ALU.add), r=[pb, bmask], w=[eb_])
                        V(lambda e, e_=e_, lo=lo, h=h: e.scalar_tensor_tensor(out=e_[:, lo:256], in0=relT[:, lo:256], scalar=-SLOPES[h], in1=e_[:, lo:256],
                                                                            op0=ALU.mult, op1=ALU.add), r=[eb_, bmask], w=[eb_])
                        p_, pb_ = p_r.get()
                        A(lambda e, e_=e_, p_=p_, lo=lo: e.activation(out=p_[:, lo:256], in_=e_[:, lo:256], func=AF.Exp), r=[eb_], w=[pb_])
                        T(lambda e, pso=pso, p_=p_, qb=qb, kv=kv, g=g: e.matmul(
                            pso[:, g * 65:(g + 1) * 65], lhsT=p_[:, 128:256], rhs=vtok[:, qb, kv, :], start=True, stop=(qb == 0)),
                          r=[pb_, bvtok[qb]], w=[psob])
                        if qb > 0:
                            T(lambda e, pso=pso, p_=p_, qb=qb, kv=kv, g=g: e.matmul(
                                pso[:, g * 65:(g + 1) * 65], lhsT=p_[:, 0:128], rhs=vtok[:, qb - 1, kv, :], start=False, stop=True),
                              r=[pb_, bvtok[qb - 1]], w=[psob])
                    dn, dnb = den_r.get()
                    pso3 = pso[:, 0:260].rearrange("p (g d) -> p g d", d=65)
                    V(lambda e, dn=dn, pso3=pso3, kv=kv: e.tensor_tensor(out=dn[:, :], in0=pso3[:, :, 64], in1=sinkEB[:, kv * 4:kv * 4 + 4], op=ALU.add),
                      r=[psob, bsink], w=[dnb])
                    V(lambda e, dn=dn: e.reciprocal(out=dn[:, :], in_=dn[:, :]), r=[dnb], w=[dnb])
                    V(lambda e, ot=ot, pso3=pso3, dn=dn, kc=kc, half=half: e.tensor_tensor(
                        out=ot[:, kc * 4:kc * 4 + 4, half * 64:(half + 1) * 64], in0=pso3[:, :, 0:64],
                        in1=dn[:, :].to_broadcast([128, 4, 64]) if False else bass.AP(dn.tensor, dn.offset, [list(dn.ap[0]), [1, 4], [0, 64]]),
                        op=ALU.mult), r=[psob, dnb], w=[otb])
                for hf in range(2):
                    ps, pb = ps_get()
                    for cc in range(4):
                        i = hf * 4 + cc
                        T(lambda e, ps=ps, ot=ot, i=i, cc=cc: e.transpose(out=ps[:, cc * 128:(cc + 1) * 128], in_=ot[:, i, :], identity=ident[:, :]),
                          r=[otb, bident], w=[pb])
                    A(lambda e, ps=ps, hf=hf, q0=q0: e.copy(out=hT[:, hf * 4:hf * 4 + 4, q0:q0 + 128], in_=ps[:, :].rearrange("p (c t) -> p c t", t=128)),
                      r=[pb], w=[bh[ti]])

            for b in range(NB):
                tk0 = TP + 4 * b
                ckst, ckb = ckst_r.get()
                cvst, cvb = cvst_r.get()
                P.dma("sync", ckst[:, :], ck_in[j, b], writes=[ckb])
                P.dma("sync", cvst[:, :], cv_in[j, b], writes=[cvb])
                vc, vcb = vc_r.get()
                V(lambda e, vc=vc: e.memset(vc[:, :, 64:65], 1.0), w=[vcb])
                V(lambda e, vc=vc, cvst=cvst: e.tensor_copy(out=vc[:, :, 0:64], in_=cvst[:, :].rearrange("p (k d) -> p k d", d=64)), r=[cvb], w=[vcb])
                ps, pb = ps_get()
                for c in range(2):
                    T(lambda e, ps=ps, ckst=ckst, c=c: e.transpose(out=ps[:, c * 128:(c + 1) * 128], in_=ckst[:, c * 128:(c + 1) * 128], identity=ident[:, :]),
                      r=[ckb, bident], w=[pb])
                kct, kctb = kct_r.get()
                A(lambda e, ps=ps, kct=kct: e.copy(out=kct[:, :, :], in_=ps[:, 0:256].rearrange("p (c t) -> p c t", t=128)), r=[pb], w=[kctb])
                psc, pscb = ps_get()
                for kv in range(4):
                    half, kc = kv % 2, kv // 2
                    for g in range(4):
                        T(lambda e, psc=psc, kct=kct, half=half, kc=kc, kv=kv, g=g, tk0=tk0: e.matmul(
                            psc[:, kv * 16 + g * 4:kv * 16 + g * 4 + 4], lhsT=kct[half * 64:(half + 1) * 64, kc, :],
                            rhs=qT[half * 64:(half + 1) * 64, kc * 4 + g, tk0:tk0 + 4], start=True, stop=True),
                          r=[kctb, bqT[4]], w=[pscb])
                        T(lambda e, psc=psc, half=half, kc=kc, kv=kv, g=g, tk0=tk0: e.matmul(
                            psc[0:64, 64 + kv * 16 + g * 4:64 + kv * 16 + g * 4 + 4], lhsT=kT[half * 64:(half + 1) * 64, kc, TP:TP + 64],
                            rhs=qT[half * 64:(half + 1) * 64, kc * 4 + g, tk0:tk0 + 4], start=True, stop=True),
                          r=[bkT[4], bqT[4]], w=[pscb])
                es, esb = es_r.get()
                V(lambda e, es=es, psc=psc: e.scalar_tensor_tensor(out=es[:, :], in0=psc[:, 0:64], scalar=SCALE, in1=biasC[:, :, :].rearrange("p h t -> p (h t)"),
                                                                 op0=ALU.mult, op1=ALU.add), r=[pscb, bmask], w=[esb])
                pc_, pcb = psb_r.get()
                A(lambda e, es=es, pc_=pc_: e.activation(out=pc_[:, :], in_=es[:, :], func=AF.Exp), r=[esb], w=[pcb])
                esn, esnb = esn_r.get()
                V(lambda e, esn=esn, psc=psc, b=b: e.scalar_tensor_tensor(out=esn[0:64, :], in0=psc[0:64, 64:128], scalar=SCALE,
                                                                        in1=biasN[0:64, b, :, :].rearrange("p h t -> p (h t)"),
                                                                        op0=ALU.mult, op1=ALU.add), r=[pscb, bmask], w=[esnb])
                pn, pnb = pn_r.get()
                A(lambda e, esn=esn, pn=pn: e.activation(out=pn[0:64, :], in_=esn[0:64, :], func=AF.Exp), r=[esnb], w=[pnb])
                pso, psob = ps_get()
                for kv in range(4):
                    T(lambda e, pso=pso, pc_=pc_, vc=vc, kv=kv: e.matmul(pso[0:16, kv * 65:(kv + 1) * 65], lhsT=pc_[:, kv * 16:(kv + 1) * 16], rhs=vc[:, kv, :],
                                                                       start=True, stop=False), r=[pcb, vcb], w=[psob])
                    T(lambda e, pso=pso, pn=pn, kv=kv: e.matmul(pso[0:16, kv * 65:(kv + 1) * 65], lhsT=pn[0:64, kv * 16:(kv + 1) * 16], rhs=vtok[0:64, 16, kv, :],
                                                              start=False, stop=True), r=[pnb, bvtok[16]], w=[psob])
                dn, dnb = den_r.get()
                pso3 = pso[0:16, 0:260].rearrange("p (k d) -> p k d", d=65)
                V(lambda e, dn=dn, pso3=pso3: e.tensor_tensor(out=dn[0:16, :], in0=pso3[:, :, 64], in1=sink16[0:16, :], op=ALU.add), r=[psob, bsink], w=[dnb])
                V(lambda e, dn=dn: e.reciprocal(out=dn[0:16, :], in_=dn[0:16, :]), r=[dnb], w=[dnb])
                os_, osb = os_r.get()
                V(lambda e, os_=os_, pso3=pso3, dn=dn: e.tensor_tensor(out=os_[0:16, :, :], in0=pso3[:, :, 0:64],
                                                                     in1=bass.AP(dn.tensor, dn.offset, [[dn.ap[0][0], 16], [1, 4], [0, 64]]), op=ALU.mult),
                  r=[psob, dnb], w=[osb])
                pst, pstb = ps_get()
                for kc in range(2):
                    T(lambda e, pst=pst, os_=os_, kc=kc: e.transpose(out=pst[:, kc * 16:(kc + 1) * 16],
                                                                    in_=os_[0:16, 2 * kc:2 * kc + 2, :].rearrange("p k d -> p (k d)"), identity=ident[0:16, 0:16]),
                      r=[osb, bident], w=[pstb])
                A(lambda e, pst=pst, tk0=tk0: e.copy(out=hT[:, :, tk0:tk0 + 4], in_=pst[:, 0:32].rearrange("p (i t) -> p i t", t=4)), r=[pstb], w=[bh[4]])

            wo = swa_w_o[j]
            for half in range(2):
                t_, b_ = wrot.get()
                for i in range(8):
                    ha, hb = pair_heads(i)
                    P.dma("gpsimd", t_[0:64, i, 0:512], wo[ha * 64:(ha + 1) * 64, half * 512:(half + 1) * 512], writes=[b_], reads=[b_])
                    P.dma("gpsimd", t_[64:128, i, 0:512], wo[hb * 64:(hb + 1) * 64, half * 512:(half + 1) * 512], writes=[b_], reads=[b_])
                for ti, (t0, n) in enumerate(TILES):
                    for nn in range(4):
                        c = half * 4 + nn
                        ps, pb = ps_get()
                        for i in range(8):
                            T(lambda e, ps=ps, t_=t_, i=i, nn=nn, t0=t0, n=n: e.matmul(
                                ps[:, 0:n], lhsT=t_[:, i, nn * 128:(nn + 1) * 128], rhs=hT[:, i, t0:t0 + n], start=(i == 0), stop=(i == 7)),
                              r=[b_, bh[ti]], w=[pb])
                        V(lambda e, ps=ps, c=c, t0=t0, n=n: e.scalar_tensor_tensor(
                            out=xT[:, c, t0:t0 + n], in0=ps[:, 0:n], scalar=bo[:, c:c + 1], in1=xT[:, c, t0:t0 + n], op0=ALU.add, op1=ALU.add),
                          r=[pb, bo_b, bx[ti][c]], w=[bx[ti][c]])


        def deltanet(j):
            w_in = dn_w_in[j]
            I32 = mybir.dt.int32
            cw = carve("cw", [128, 32, 4], F32)
            dtb = carve("dtb", [128, 16], F32)
            negA = carve("negA", [128, 16], F32)
            nwB = carve("nwB", [128, 128], F32)
            one1 = carve("one1", [128, 1], F32)
            eps2 = carve("eps2", [128, 1], F32)
            ones32 = carve("ones32", [128, 128], F32)
            Ltri = [carve(f"Ltri{i}", [128, 128], F32) for i in range(2)]
            Lsel = [carve(f"Lsel{i}", [128, 128], F32) for i in range(2)]
            mk1 = [carve(f"mk1{i}", [128, 128], F32) for i in range(2)]
            mk2 = [carve(f"mk2{i}", [128, 128], F32) for i in range(2)]
            bmask16 = carve("bmask16", [128, 16], F32)
            tmpi = carve("tmpi", [128, 128], I32)
            beta_a = carve("beta_a", [128, 17, 16], F32)
            G_a = carve("G_a", [128, 17, 16], F32)
            bexpG_a = carve("bexpG_a", [128, 17, 16], F32)
            kdecs_a = carve("kdecs_a", [128, 17, 16], F32)
            bset = Buf("dnset")
            btok = Buf("dntok")
            for k_ in range(4):
                P.dma("sync", cw[:, :, k_], dn_conv_w[j, k_].rearrange("(c p) -> p c", p=128), writes=[bset], reads=[bset])
            P.dma("sync", dtb[:, :], dn_dt_bias[j:j + 1, :].partition_broadcast(128).rearrange("p o n -> p (o n)"), writes=[bset], reads=[bset])
            P.dma("sync", negA[:, :], dn_a_log[j:j + 1, :].partition_broadcast(128).rearrange("p o n -> p (o n)"), writes=[bset], reads=[bset])
            P.dma("sync", nwB[:, :], dn_norm_w[j:j + 1, :].partition_broadcast(128).rearrange("p o n -> p (o n)"), writes=[bset], reads=[bset])
            P.fence()
            A(lambda e: e.activation(out=negA[:, :], in_=negA[:, :], func=AF.Exp), r=[bset], w=[bset])
            V(lambda e: e.tensor_scalar(out=negA[:, :], in0=negA[:, :], scalar1=-1.0, scalar2=None, op0=ALU.mult), r=[bset], w=[bset])
            V(lambda e: e.memset(one1[:, :], 1.0), r=[bset], w=[bset])
            V(lambda e: e.memset(eps2[:, :], 1e-6), r=[bset], w=[bset])
            V(lambda e: e.memset(ones32[:, :], 1.0), r=[bset], w=[bset])
            for i, C in enumerate((64, 4)):
                sh = 6 if C == 64 else 2
                G(lambda e, i=i: e.memset(Ltri[i][:, :], 0.0), r=[bset], w=[bset])
                G(lambda e, i=i: e.memset(Lsel[i][:, :], 0.0), r=[bset], w=[bset])
                G(lambda e, i=i: e.memset(mk1[i][:, :], NEG), r=[bset], w=[bset])
                G(lambda e, i=i: e.memset(mk2[i][:, :], NEG), r=[bset], w=[bset])
            P.fence()
            mark_tmp = arena_state["off"]
            pidx = carve("pidx", [128, 128], F32)
            cidx = carve("cidx", [128, 128], F32)
            G(lambda e: e.iota(tmpi[:, :], pattern=[[0, 128]], base=0, channel_multiplier=1), r=[bset], w=[bset])
            G(lambda e: e.tensor_copy(out=pidx[:, :], in_=tmpi[:, :]), r=[bset], w=[bset])
            G(lambda e: e.iota(tmpi[:, :], pattern=[[1, 128]], base=0, channel_multiplier=0), r=[bset], w=[bset])
            G(lambda e: e.tensor_copy(out=cidx[:, :], in_=tmpi[:, :]), r=[bset], w=[bset])
            pch = carve("pch", [128, 128], F32)
            cch = carve("cch", [128, 128], F32)
            same = carve("same", [128, 128], F32)
            tmpf = carve("tmpf", [128, 128], F32)
            for i, C in enumerate((64, 4)):
                sh = 6 if C == 64 else 2
                for src, dst in ((pidx, pch), (cidx, cch)):
                    V(lambda e, src=src: e.tensor_copy(out=tmpi[:, :], in_=src[:, :]), r=[bset], w=[bset])
                    V(lambda e, sh=sh: e.tensor_scalar(out=tmpi[:, :], in0=tmpi[:, :], scalar1=sh, scalar2=None, op0=ALU.arith_shift_right), r=[bset], w=[bset])
                    V(lambda e, dst=dst: e.tensor_copy(out=dst[:, :], in_=tmpi[:, :]), r=[bset], w=[bset])
                V(lambda e: e.tensor_tensor(out=same[:, :], in0=pch[:, :], in1=cch[:, :], op=ALU.is_equal), r=[bset], w=[bset])
                V(lambda e: e.tensor_tensor(out=tmpf[:, :], in0=pidx[:, :], in1=cidx[:, :], op=ALU.is_le), r=[bset], w=[bset])
                V(lambda e, i=i: e.tensor_tensor(out=Ltri[i][:, :], in0=tmpf[:, :], in1=same[:, :], op=ALU.mult), r=[bset], w=[bset])
                V(lambda e, i=i: e.tensor_scalar(out=mk1[i][:, :], in0=Ltri[i][:, :], scalar1=-1.0, scalar2=-NEG, op0=ALU.add, op1=ALU.mult), r=[bset], w=[bset])
                V(lambda e: e.tensor_tensor(out=tmpf[:, :], in0=pidx[:, :], in1=cidx[:, :], op=ALU.is_gt), r=[bset], w=[bset])
                V(lambda e: e.tensor_tensor(out=tmpf[:, :], in0=tmpf[:, :], in1=same[:, :], op=ALU.mult), r=[bset], w=[bset])
                V(lambda e, i=i: e.tensor_scalar(out=mk2[i][:, :], in0=tmpf[:, :], scalar1=-1.0, scalar2=-NEG, op0=ALU.add, op1=ALU.mult), r=[bset], w=[bset])
                V(lambda e, C=C: e.tensor_scalar(out=tmpf[:, :], in0=cch[:, :], scalar1=float(C), scalar2=float(C - 1), op0=ALU.mult, op1=ALU.add), r=[bset], w=[bset])
                V(lambda e, i=i: e.tensor_tensor(out=Lsel[i][:, :], in0=tmpf[:, :], in1=pidx[:, :], op=ALU.is_equal), r=[bset], w=[bset])
            V(lambda e: e.tensor_tensor(out=bmask16[:, :], in0=pch[:, 0:16], in1=cidx[:, 0:16], op=ALU.is_equal), r=[bset], w=[bset])
            P.fence()
            arena_state["off"] = mark_tmp

            wba, wbab = wload(w_in, 0, 8, 6144, 32)
            for blk in range(17):
                t0, n = (blk * 128, 128) if blk < 16 else (TP, TS)
                mi = 0 if blk < 16 else 1
                ti = min(blk // 4, 4)
                ps, pb = ps_get()
                for k in range(8):
                    T(lambda e, ps=ps, k=k, t0=t0, n=n: e.matmul(ps[0:n, 0:32], lhsT=hT[:, k, t0:t0 + n], rhs=wba[:, k, 0:32], start=(k == 0), stop=(k == 7)),
                      r=[wbab, bh[ti]], w=[pb])
                A(lambda e, ps=ps, blk=blk, n=n: e.activation(out=beta_a[0:n, blk, :], in_=ps[0:n, 0:16], func=AF.Sigmoid), r=[pb], w=[btok])
                V(lambda e, ps=ps, blk=blk, n=n: e.tensor_tensor(out=G_a[0:n, blk, :], in0=ps[0:n, 16:32], in1=dtb[0:n, :], op=ALU.add), r=[pb, bset, btok], w=[btok])
                A(lambda e, blk=blk, n=n: e.activation(out=G_a[0:n, blk, :], in_=G_a[0:n, blk, :], func=AF.Exp), r=[btok], w=[btok])
                A(lambda e, blk=blk, n=n: e.activation(out=G_a[0:n, blk, :], in_=G_a[0:n, blk, :], func=AF.Ln, bias=one1[0:n, :], scale=1.0), r=[btok, bset], w=[btok])
                V(lambda e, blk=blk, n=n: e.tensor_tensor(out=kdecs_a[0:n, blk, :], in0=G_a[0:n, blk, :], in1=negA[0:n, :], op=ALU.mult), r=[btok, bset], w=[btok])
                ps2, pb2 = ps_get()
                T(lambda e, ps2=ps2, blk=blk, n=n, mi=mi: e.matmul(ps2[0:n, 0:16], lhsT=Ltri[mi][0:n, 0:n], rhs=kdecs_a[0:n, blk, :], start=True, stop=True), r=[btok, bset], w=[pb2])
                V(lambda e, ps2=ps2, blk=blk, n=n: e.tensor_copy(out=G_a[0:n, blk, :], in_=ps2[0:n, 0:16]), r=[pb2, btok], w=[btok])
                ps3, pb3 = ps_get()
                T(lambda e, ps3=ps3, blk=blk, n=n, mi=mi: e.matmul(ps3[0:n, 0:16], lhsT=Lsel[mi][0:n, 0:n], rhs=G_a[0:n, blk, :], start=True, stop=True), r=[btok, bset], w=[pb3])
                V(lambda e, ps3=ps3, blk=blk, n=n: e.tensor_tensor(out=kdecs_a[0:n, blk, :], in0=ps3[0:n, 0:16], in1=G_a[0:n, blk, :], op=ALU.subtract), r=[pb3, btok], w=[btok])
                A(lambda e, blk=blk, n=n: e.activation(out=kdecs_a[0:n, blk, :], in_=kdecs_a[0:n, blk, :], func=AF.Exp), r=[btok], w=[btok])
                A(lambda e, blk=blk, n=n: e.activation(out=bexpG_a[0:n, blk, :], in_=G_a[0:n, blk, :], func=AF.Exp), r=[btok], w=[btok])
                V(lambda e, blk=blk, n=n: e.tensor_tensor(out=bexpG_a[0:n, blk, :], in0=bexpG_a[0:n, blk, :], in1=beta_a[0:n, blk, :], op=ALU.mult), r=[btok], w=[btok])
            P.fence()

            def OP(eng, name, r, w, **kw):
                return P.op(eng, lambda e, kw=kw, name=name: getattr(e, name)(**kw), r, w)

            def ap4(t, p0, np_, off_ap, dims):
                ta = t if hasattr(t, "tensor") else t[:]
                return bass.AP(ta.tensor, off_ap.offset, [[ta.ap[0][0], np_]] + [list(d) for d in dims])

            TW = 256
            blk_pc = carve("pc", [128, 4 * (TW + 4)], F32)
            pc = blk_pc.rearrange("p (c t) -> p c t", t=TW + 4)
            pcs = carve("pcs", [128, 4, 16, 7], F32)
            blk_cv = carve("cv", [128, 4 * TW], F32)
            cv = blk_cv.rearrange("p (c t) -> p c t", t=TW)
            sqb = carve("sqb", [128, TW], BF16)
            rn = carve("rn", [128, TW], F32)
            kTf = carve("kTf", [128, TW], F32)
            tailtm = carve("tailtm", [128, 512], F32)
            scst = carve("scst", [128, 512], F32)
            qTn = [carve("qTn", [128, TW], BF16)] * 2
            kTn = [carve("kTn", [128, TW], BF16)] * 2
            ktok = [carve("ktok", [128, 2, 128], F32)] * 2
            vtok = [carve("vtokd", [128, 2, 2, 128], F32)] * 2
            zs = [carve(f"zs{i}", [128, 2, 256], F32) for i in range(2)]
            bA = [Buf("dnA")] * 2
            bZ = [Buf(f"dnZ{i}") for i in range(2)]
            dG = carve("dG", [128, 4, 128], F32)
            dd = carve("dd", [128, 4, 128], F32)
            Mp = [carve(f"Mp{i}", [128, 4, 128], BF16) for i in range(2)]
            Np = [carve(f"Np{i}", [128, 4, 128], BF16) for i in range(2)]
            TT = carve("TT", [128, 4, 128], BF16)
            vb = carve("vb", [128, 4, 128], BF16)
            kbg = carve("kbg", [128, 4, 128], BF16)
            bBi = Buf("dnBi")
            bdG = Buf("dG"); bdd = Buf("dd"); bTT = bBi; bvb = Buf("vb"); bkbg = Buf("kbg")
            bMp = [bBi, bBi]
            bNp = [bBi, bBi]
            eGB = [carve(f"eGB{i}", [128, 4, 128], F32) for i in range(2)]
            u32 = [carve(f"u32{i}", [128, 4, 128], F32) for i in range(2)]
            wTb = [carve(f"wTb{i}", [128, 4, 128], BF16) for i in range(2)]
            qgT = [carve(f"qgT{i}", [128, 4, 128], BF16) for i in range(2)]
            aqkT = [carve(f"aqkT{i}", [128, 4, 128], BF16) for i in range(2)]
            kdec = [carve(f"kdec{i}", [128, 4, 128], BF16) for i in range(2)]
            bB = [Buf(f"dnB{i}") for i in range(2)]
            S32 = carve("S32", [128, 2, 128], F32)
            Sbf = carve("Sbf", [128, 2, 128], BF16)
            un_r = Rot(carve, "unew", [128, 2, 128], BF16, 2)
            otok = [carve("otokd", [128, 2, 2, 128], F32)] * 2
            bO = [Buf("dnO")] * 2
            bS = Buf("S")
            on = carve("on", [128, 4, 128], F32)
            ssq = carve("ssq", [128, 4], F32)
            oTb = carve("oTb", [128, 2, TW], BF16)
            bD = Buf("dnD")
            s0_r = Rot(carve, "s0", [128, 128], F32, 2)
            unb_r = Rot(carve, "unb", [128, 128], F32, 2)
            tmp_r = Rot(carve, "tmpu", [128, 128], F32, 1)
            oacc = carve("oacc", [128, 128], F32)
            boacc = Buf("oacc")
            kdec32 = carve("kdec32", [128, 128], F32)
            aq32 = carve("aq32", [128, 128], F32)
            wT32 = carve("wT32", [128, 2, 64], F32)
            qgT32 = carve("qgT32", [128, 64], F32)
            sn_r = Rot(carve, "sn", [128, 128], F32, 2)
            bSm = Buf("dnSm")
            bAi = Buf("dnAi")
            bAc = [Buf(f"dnAc{c}") for c in range(4)]
            bTail = Buf("dnTail")
            DN_TILES = [(t0, TW) for t0 in range(0, TP, TW)] + [(TP, TS)]
            tcount = 0

            for hg in range(8):
                wA, wAb = wrot.get()
                P.dma("gpsimd", wA[:, :, 0:128], w_in[:, hg * 128:(hg + 1) * 128].rearrange("(c p) n -> p c n", p=128), writes=[wAb], reads=[wAb])
                P.dma("gpsimd", wA[:, :, 128:256], w_in[:, 1024 + hg * 128:1024 + (hg + 1) * 128].rearrange("(c p) n -> p c n", p=128), writes=[wAb], reads=[wAb])
                P.dma("gpsimd", wA[:, :, 256:512], w_in[:, 2048 + hg * 256:2048 + (hg + 1) * 256].rearrange("(c p) n -> p c n", p=128), writes=[wAb], reads=[wAb])
                wZ, wZb = wload(w_in, 0, 8, 4096 + hg * 256, 256)
                wO, wOb = wrot.get()
                for half in range(2):
                    P.dma("gpsimd", wO[:, half * 2:half * 2 + 2, 0:512],
                          dn_w_out[j, hg * 256:(hg + 1) * 256, half * 512:(half + 1) * 512].rearrange("(c p) n -> p c n", p=128), writes=[wOb], reads=[wOb])
                chan = [hg, 8 + hg, 16 + 2 * hg, 17 + 2 * hg]
                for ch in range(4):
                    OP("vector", "memset", [bAc[ch]], [bAc[ch]], ap=pc[:, ch, 0:3], constant=0.0)
                OP("vector", "memset", [bS], [bS], ap=S32[:, :, :], constant=0.0)
                OP("vector", "memset", [bS], [bS], ap=Sbf[:, :, :], constant=0.0)
                for (t0, n) in DN_TILES:
                    samp = t0 >= TP
                    ti = min(t0 // 512, 4)
                    mi = 1 if samp else 0
                    nb, bs = (1, 64) if samp else (2, 128)
                    NP_ = nb * 2
                    par = tcount % 2
                    tcount += 1
                    gblk0 = (t0 // 128) if not samp else 16
                    A_ = bA[par]
                    B_ = bB[par]
                    O_ = bO[par]
                    if samp:
                        for q_, (c0, wd_) in enumerate(((hg * 128, 128), (1024 + hg * 128, 128), (2048 + hg * 256, 256))):
                            so = (0, 128, 256)[q_]
                            P.dma("sync", scst[0:48, so:so + wd_], sconv_in[j, :, c0:c0 + wd_], reads=[bAi], writes=[bAi])
                        ps, pb = ps_get()
                        for ch in range(4):
                            OP("tensor", "transpose", [bAi, bident], [pb], out=ps[:, ch * 48:(ch + 1) * 48], in_=scst[0:48, ch * 128:(ch + 1) * 128], identity=ident[0:48, 0:48])
                        OP("vector", "tensor_copy", [pb] + bAc, bAc, out=pcs[:, :, :, 0:3], in_=ps[:, 0:192].rearrange("p (c b r) -> p c b r", b=16, r=3))
                    for ch in range(4):
                        ps, pb = ps_get()
                        for k in range(8):
                            OP("tensor", "matmul", [wAb, bh[ti]], [pb], out=ps[:, 0:n], lhsT=wA[:, k, ch * 128:(ch + 1) * 128], rhs=hT[:, k, t0:t0 + n],
                               start=(k == 0), stop=(k == 7))
                        if samp:
                            OP("scalar", "copy", [pb, bAc[ch]], [bAc[ch]], out=pcs[:, ch, :, 3:7], in_=ps[:, 0:64].rearrange("p (b t) -> p b t", t=4))
                        else:
                            OP("scalar", "copy", [pb, bAc[ch]], [bAc[ch]], out=pc[:, ch, 3:3 + TW], in_=ps[:, 0:TW])
                    if t0 == TP - TW or samp:
                        r0, nr = (TP - 32, 32) if not samp else (TP, 64)
                        ps, pb = ps_get()
                        for k in range(8):
                            OP("tensor", "matmul", [wAb, bh[ti]], [pb], out=ps[0:nr, :], lhsT=hT[:, k, r0:r0 + nr], rhs=wA[:, k, :], start=(k == 0), stop=(k == 7))
                        OP("vector", "tensor_copy", [pb, bTail], [bTail], out=tailtm[0:nr, :], in_=ps[0:nr, :])
                        for q_, (c0, wd_) in enumerate(((hg * 128, 128), (1024 + hg * 128, 128), (2048 + hg * 256, 256))):
                            so = (0, 128, 256)[q_]
                            if not samp:
                                P.dma("sync", convp_out[j, :, c0:c0 + wd_], tailtm[29:32, so:so + wd_], reads=[bTail])
                            else:
                                for t_ in range(1, 4):
                                    src = bass.AP(tailtm.tensor, tailtm[t_:t_ + 1, so:so + 1].offset, [[tailtm.ap[0][0] * 4, 16], [1, wd_]])
                                    P.dma("sync", convs_out[j, :, t_ - 1, c0:c0 + wd_], src, reads=[bTail])
                    for ch in range(4):
                        cc = chan[ch]
                        if samp:
                            srcs = [pcs[:, ch, :, k:k + 4] for k in range(4)]
                            dst = cv[:, ch, 0:64].rearrange("p (b t) -> p b t", t=4)
                        else:
                            srcs = [pc[:, ch, k:k + TW] for k in range(4)]
                            dst = cv[:, ch, 0:TW]
                        OP("vector", "tensor_scalar", [bAc[ch], bset], [bAc[ch]], out=dst, in0=srcs[0], scalar1=cw[:, cc, 0:1], scalar2=None, op0=ALU.mult)
                        for k in range(1, 4):
                            OP("vector", "scalar_tensor_tensor", [bAc[ch], bset], [bAc[ch]], out=dst, in0=srcs[k], scalar=cw[:, cc, k:k + 1], in1=dst, op0=ALU.mult, op1=ALU.add)
                        OP("scalar", "activation", [bAc[ch]], [bAc[ch]], out=cv[:, ch, 0:n], in_=cv[:, ch, 0:n], func=AF.Silu)
                        if not samp:
                            OP("scalar", "copy", [bAc[ch]], [bAc[ch]], out=pc[:, ch, 0:3], in_=pc[:, ch, TW:TW + 3])
                    for ch, dst, mul in ((0, qTn[par], 128.0 ** -0.5), (1, kTn[par], 1.0)):
                        OP("scalar", "activation", [bAc[ch], bAi], [bAi], out=sqb[:, 0:n], in_=cv[:, ch, 0:n], func=AF.Square)
                        ps, pb = ps_get()
                        OP("tensor", "matmul", [bAi, bconst], [pb], out=ps[:, 0:n], lhsT=ones_bf[:, :], rhs=sqb[:, 0:n], start=True, stop=True)
                        OP("scalar", "activation", [pb, bset, bAi], [bAi], out=rn[:, 0:n], in_=ps[:, 0:n], func=AF.Ln, bias=eps2[:, :], scale=1.0)
                        OP("scalar", "activation", [bAi], [bAi], out=rn[:, 0:n], in_=rn[:, 0:n], func=AF.Exp, scale=-0.5)
                        OP("vector", "scalar_tensor_tensor", [bAi, bAc[ch], A_], [A_], out=dst[:, 0:n], in0=cv[:, ch, 0:n], scalar=mul, in1=rn[:, 0:n], op0=ALU.mult, op1=ALU.mult)
                        if ch == 1:
                            OP("vector", "tensor_tensor", [bAi, bAc[1]], [bAi], out=kTf[:, 0:n], in0=cv[:, 1, 0:n], in1=rn[:, 0:n], op=ALU.mult)
                    for blk in range(nb):
                        ps, pb = ps_get()
                        OP("tensor", "transpose", [bAi, bident], [pb], out=ps[0:bs, 0:128], in_=kTf[:, blk * bs:(blk + 1) * bs], identity=ident[:, :])
                        for hl in range(2):
                            OP("tensor", "transpose", [bAc[2 + hl], bident], [pb], out=ps[0:bs, 128 + hl * 128:256 + hl * 128], in_=cv[:, 2 + hl, blk * bs:(blk + 1) * bs], identity=ident[:, :])
                        OP("vector", "tensor_copy", [pb, A_], [A_], out=ktok[par][0:bs, blk, :], in_=ps[0:bs, 0:128])
                        OP("scalar", "copy", [pb, A_], [A_], out=vtok[par][0:bs, blk, :, :], in_=ps[0:bs, 128:384].rearrange("p (h d) -> p h d", d=128))
                        psz, pzb = ps_get()
                        for k in range(8):
                            OP("tensor", "matmul", [wZb, bh[ti]], [pzb], out=psz[0:bs, 0:256], lhsT=hT[:, k, t0 + blk * bs:t0 + (blk + 1) * bs], rhs=wZ[:, k, 0:256],
                               start=(k == 0), stop=(k == 7))
                        OP("scalar", "activation", [pzb, bZ[par]], [bZ[par]], out=zs[par][0:bs, blk, :], in_=psz[0:bs, 0:256], func=AF.Silu)

                    def P4(t):
                        return t[0:bs, 0:NP_, 0:bs].rearrange("p (b h) c -> p b h c", h=2)

                    def scal4(arr, last):
                        return ap4(arr, 0, bs, arr[0:1, gblk0, 2 * hg:2 * hg + 1], [[16, nb], [1, 2], [0, last]])

                    def bc_tile(t, last):
                        return ap4(t, 0, bs, t[0:1, 0:1], [[0, nb], [0, 2], [1, last]])

                    psk, pkb = ps_get()
                    for blk in range(nb):
                        c0 = blk * bs
                        OP("tensor", "matmul", [A_], [pkb], out=psk[0:bs, blk * 128:blk * 128 + bs], lhsT=kTn[par][:, c0:c0 + bs], rhs=kTn[par][:, c0:c0 + bs], start=True, stop=True)
                        OP("tensor", "matmul", [A_], [pkb], out=psk[0:bs, 256 + blk * 128:256 + blk * 128 + bs], lhsT=kTn[par][:, c0:c0 + bs], rhs=qTn[par][:, c0:c0 + bs], start=True, stop=True)
                    OP("vector", "tensor_tensor", [bdG, btok, bident], [bdG], out=P4(dG), in0=bc_tile(ident, bs), in1=scal4(G_a, bs), op=ALU.mult)
                    psg, pgb = ps_get()
                    OP("tensor", "matmul", [bdG, bset], [pgb], out=psg[:, 0:NP_ * 128], lhsT=ones32[0:bs, :], rhs=dG[0:bs, 0:NP_, :].rearrange("p a c -> p (a c)"), start=True, stop=True)
                    psg4 = psg[0:bs, 0:NP_ * 128].rearrange("p (b h c) -> p b h c", h=2, c=128)[:, :, :, 0:bs]
                    OP("vector", "tensor_tensor", [pgb, btok, bdd], [bdd], out=P4(dd), in0=psg4, in1=scal4(G_a, bs), op=ALU.subtract)
                    OP("scalar", "activation", [pgb, B_], [B_], out=eGB[par][:, 0:NP_, :], in_=psg[:, 0:NP_ * 128].rearrange("p (a c) -> p a c", c=128), func=AF.Exp)
                    OP("vector", "tensor_tensor", [bdd, bdG, bset], [bdG], out=P4(dG), in0=P4(dd), in1=bc_tile(mk1[mi], bs), op=ALU.add)
                    OP("scalar", "activation", [bdG], [bdG], out=dG[0:bs, 0:NP_, 0:bs], in_=dG[0:bs, 0:NP_, 0:bs], func=AF.Exp)
                    OP("vector", "tensor_tensor", [bdd, bset], [bdd], out=P4(dd), in0=P4(dd), in1=bc_tile(mk2[mi], bs), op=ALU.subtract)
                    OP("scalar", "activation", [bdd], [bdd], out=dd[0:bs, 0:NP_, 0:bs], in_=dd[0:bs, 0:NP_, 0:bs], func=AF.Exp, scale=-1.0)
                    qk4 = ap4(psk, 0, bs, psk[0:1, 256:257], [[128, nb], [0, 2], [1, bs]])
                    kk4 = ap4(psk, 0, bs, psk[0:1, 0:1], [[128, nb], [0, 2], [1, bs]])
                    OP("vector", "tensor_tensor", [pkb, bdG, B_], [B_], out=P4(aqkT[par]), in0=qk4, in1=P4(dG), op=ALU.mult)
                    OP("vector", "tensor_tensor", [bdd, btok], [bdd], out=P4(dd), in0=P4(dd), in1=scal4(beta_a, bs), op=ALU.mult)
                    OP("vector", "tensor_tensor", [pkb, bdd], [bdd], out=P4(dd), in0=kk4, in1=P4(dd), op=ALU.mult)
                    OP("scalar", "mul", [bdd], [bdd], out=dd[0:bs, 0:NP_, 0:bs], in_=dd[0:bs, 0:NP_, 0:bs], mul=-1.0)
                    OP("scalar", "copy", [bdd, bMp[0]], [bMp[0]], out=Mp[0][0:bs, 0:NP_, 0:bs], in_=dd[0:bs, 0:NP_, 0:bs])
                    pst, ptb = ps_get()
                    for p in range(NP_):
                        OP("tensor", "transpose", [bdd, bident], [ptb], out=pst[0:bs, p * 128:p * 128 + bs], in_=dd[0:bs, p, 0:bs], identity=ident[0:bs, 0:bs])
                    pst3 = pst[0:bs, 0:NP_ * 128].rearrange("p (a c) -> p a c", c=128)[:, :, 0:bs]
                    OP("scalar", "copy", [ptb, bNp[0]], [bNp[0]], out=Np[0][0:bs, 0:NP_, 0:bs], in_=pst3)
                    OP("vector", "tensor_tensor", [ptb, bident, bTT], [bTT], out=TT[0:bs, 0:NP_, 0:bs], in0=pst3,
                       in1=ap4(ident, 0, bs, ident[0:1, 0:1], [[0, NP_], [1, bs]]), op=ALU.add)
                    cur = 0
                    for it in range(5):
                        nxt = 1 - cur
                        pa, pab = ps_get()
                        for p in range(NP_):
                            OP("tensor", "matmul", [bNp[cur], bMp[cur]], [pab], out=pa[0:bs, p * 128:p * 128 + bs], lhsT=Np[cur][0:bs, p, 0:bs], rhs=Mp[cur][0:bs, p, 0:bs], start=True, stop=True)
                        pa3 = pa[0:bs, 0:NP_ * 128].rearrange("p (a c) -> p a c", c=128)[:, :, 0:bs]
                        if it < 4:
                            pn_, pnb_ = ps_get()
                            for p in range(NP_):
                                OP("tensor", "matmul", [bNp[cur], bMp[cur]], [pnb_], out=pn_[0:bs, p * 128:p * 128 + bs], lhsT=Mp[cur][0:bs, p, 0:bs], rhs=Np[cur][0:bs, p, 0:bs], start=True, stop=True)
                            pn3 = pn_[0:bs, 0:NP_ * 128].rearrange("p (a c) -> p a c", c=128)[:, :, 0:bs]
                        OP("scalar", "copy", [pab, bMp[nxt]], [bMp[nxt]], out=Mp[nxt][0:bs, 0:NP_, 0:bs], in_=pa3)
                        if it < 4:
                            OP("scalar", "copy", [pnb_, bNp[nxt]], [bNp[nxt]], out=Np[nxt][0:bs, 0:NP_, 0:bs], in_=pn3)
                        pu, pub = ps_get()
                        for p in range(NP_):
                            OP("tensor", "matmul", [bMp[nxt], bTT], [pub], out=pu[0:bs, p * 128:p * 128 + bs], lhsT=Mp[nxt][0:bs, p, 0:bs], rhs=TT[0:bs, p, 0:bs], start=True, stop=True)
                        pu3 = pu[0:bs, 0:NP_ * 128].rearrange("p (a c) -> p a c", c=128)[:, :, 0:bs]
                        OP("vector", "tensor_tensor", [pub, bTT], [bTT], out=TT[0:bs, 0:NP_, 0:bs], in0=pu3, in1=TT[0:bs, 0:NP_, 0:bs], op=ALU.add)
                        cur = nxt
                    vt4 = vtok[par][0:bs, 0:nb, :, :]
                    kt4 = ap4(ktok[par], 0, bs, ktok[par][0:1, 0:1, 0:1], [[128, nb], [0, 2], [1, 128]])
                    V4 = lambda t: t[0:bs, 0:NP_, :].rearrange("p (b h) c -> p b h c", h=2)
                    OP("vector", "tensor_tensor", [A_, btok, bvb], [bvb], out=V4(vb), in0=vt4, in1=scal4(beta_a, 128), op=ALU.mult)
                    OP("vector", "tensor_tensor", [A_, btok, bkbg], [bkbg], out=V4(kbg), in0=kt4, in1=scal4(bexpG_a, 128), op=ALU.mult)
                    OP("vector", "tensor_tensor", [A_, btok, B_], [B_], out=V4(kdec[par]), in0=kt4, in1=scal4(kdecs_a, 128), op=ALU.mult)
                    pu, pub = ps_get()
                    pw, pwb = ps_get()
                    for p in range(NP_):
                        OP("tensor", "matmul", [bTT, bvb], [pub], out=pu[0:bs, p * 128:(p + 1) * 128], lhsT=TT[0:bs, p, 0:bs], rhs=vb[0:bs, p, :], start=True, stop=True)
                        OP("tensor", "matmul", [bTT, bkbg], [pwb], out=pw[:, p * 128:p * 128 + bs], lhsT=kbg[0:bs, p, :], rhs=TT[0:bs, p, 0:bs], start=True, stop=True)
                    OP("vector", "tensor_copy", [pub, B_], [B_], out=u32[par][0:bs, 0:NP_, :], in_=pu[0:bs, 0:NP_ * 128].rearrange("p (a c) -> p a c", c=128))
                    pw3 = pw[:, 0:NP_ * 128].rearrange("p (a c) -> p a c", c=128)[:, :, 0:bs]
                    qT4 = ap4(qTn[par], 0, 128, qTn[par][0:1, 0:1], [[bs, nb], [0, 2], [1, bs]])
                    eG4 = eGB[par][:, 0:NP_, 0:bs].rearrange("p (b h) c -> p b h c", h=2)
                    if not samp:
                        OP("scalar", "copy", [pwb, B_], [B_], out=wTb[par][:, 0:NP_, 0:bs], in_=pw3)
                        OP("vector", "tensor_tensor", [A_, B_], [B_], out=qgT[par][:, 0:NP_, 0:bs].rearrange("p (b h) c -> p b h c", h=2), in0=qT4, in1=eG4, op=ALU.mult)
                        for blk in range(nb):
                            for half in range(2):
                                hs = half * 64
                                un, unb = un_r.get()
                                p1, p1b = ps_get()
                                for hl in range(2):
                                    OP("tensor", "matmul", [B_, bS], [p1b], out=p1[hs:hs + 64, hl * 128:(hl + 1) * 128], lhsT=wTb[par][:, blk * 2 + hl, hs:hs + 64], rhs=Sbf[:, hl, :], start=True, stop=True)
                                OP("vector", "tensor_tensor", [p1b, B_, unb], [unb], out=un[hs:hs + 64, :, :],
                                   in0=u32[par][hs:hs + 64, blk * 2:blk * 2 + 2, :], in1=p1[hs:hs + 64, 0:256].rearrange("p (h d) -> p h d", d=128), op=ALU.subtract)
                                p2, p2b = ps_get()
                                for hl in range(2):
                                    OP("tensor", "matmul", [B_, bS], [p2b], out=p2[hs:hs + 64, hl * 128:(hl + 1) * 128], lhsT=qgT[par][:, blk * 2 + hl, hs:hs + 64], rhs=Sbf[:, hl, :], start=True, stop=False)
                                    OP("tensor", "matmul", [B_, unb], [p2b], out=p2[hs:hs + 64, hl * 128:(hl + 1) * 128], lhsT=aqkT[par][hs:hs + 64, blk * 2 + hl, hs:hs + 64], rhs=un[hs:hs + 64, hl, :], start=False, stop=True)
                                OP("scalar", "copy", [p2b, O_], [O_], out=otok[par][hs:hs + 64, blk, :, :], in_=p2[hs:hs + 64, 0:256].rearrange("p (h d) -> p h d", d=128))
                                p3, p3b = ps_get()
                                for hl in range(2):
                                    OP("tensor", "matmul", [B_, unb], [p3b], out=p3[:, hl * 128:(hl + 1) * 128], lhsT=kdec[par][hs:hs + 64, blk * 2 + hl, :], rhs=un[hs:hs + 64, hl, :], start=True, stop=True)
                                for hl in range(2):
                                    OP("vector", "scalar_tensor_tensor", [p3b, B_, bS], [bS], out=S32[:, hl, :], in0=S32[:, hl, :], scalar=eGB[par][:, blk * 2 + hl, hs + 63:hs + 64],
                                       in1=p3[:, hl * 128:(hl + 1) * 128], op0=ALU.mult, op1=ALU.add)
                                OP("scalar", "copy", [bS], [bS], out=Sbf[:, :, :], in_=S32[:, :, :])
                    else:
                        OP("scalar", "copy", [pwb, bSm], [bSm], out=wT32[:, :, :], in_=pw[:, 0:256].rearrange("p (h c) -> p h c", c=128)[:, :, 0:64])
                        for hl in range(2):
                            h = 2 * hg + hl
                            OP("vector", "tensor_tensor", [A_, B_, bSm], [bSm], out=qgT32[:, 0:64], in0=qTn[par][:, 0:64], in1=eGB[par][:, hl, 0:64], op=ALU.mult)
                            OP("vector", "tensor_scalar", [A_, btok, bSm], [bSm], out=kdec32[0:64, :], in0=ktok[par][0:64, 0, :], scalar1=kdecs_a[0:64, 16, h:h + 1], scalar2=None, op0=ALU.mult)
                            OP("vector", "tensor_copy", [B_, bSm], [bSm], out=aq32[0:64, 0:64], in_=aqkT[par][0:64, hl, 0:64])
                            OP("vector", "memset", [boacc], [boacc], ap=oacc[0:64, :], constant=0.0)
                            p2, p2b = psb[7], psB[7]
                            for b in range(NB):
                                s0t, s0b = s0_r.get()
                                P.dma("scalar", s0t[:, :], sdelta_in[j, b, h], writes=[s0b])
                                p1, p1b = ps_get()
                                OP("tensor", "matmul", [bSm, s0b], [p1b], out=p1[0:64, 0:128], lhsT=wT32[:, hl, :], rhs=s0t[:, :], start=True, stop=True)
                                OP("tensor", "matmul", [bSm, s0b], [p1b], out=p1[0:64, 128:256], lhsT=qgT32[:, 0:64], rhs=s0t[:, :], start=True, stop=True)
                                tm, tmb = tmp_r.get()
                                OP("vector", "tensor_tensor", [p1b, B_, tmb], [tmb], out=tm[0:64, :], in0=u32[par][0:64, hl, :], in1=p1[0:64, 0:128], op=ALU.subtract)
                                ub, ubb = unb_r.get()
                                OP("vector", "tensor_scalar", [tmb, bset, ubb], [ubb], out=ub[0:64, :], in0=tm[0:64, :], scalar1=bmask16[0:64, b:b + 1], scalar2=None, op0=ALU.mult)
                                OP("vector", "scalar_tensor_tensor", [p1b, bset, boacc], [boacc], out=oacc[0:64, :], in0=p1[0:64, 128:256], scalar=bmask16[0:64, b:b + 1], in1=oacc[0:64, :],
                                   op0=ALU.mult, op1=ALU.add)
                                OP("tensor", "matmul", [bSm, ubb], [p2b], out=p2[0:64, 0:128], lhsT=aq32[0:64, 0:64], rhs=ub[0:64, :], start=(b == 0), stop=(b == NB - 1))
                                p3, p3b = ps_get()
                                OP("tensor", "matmul", [bSm, ubb], [p3b], out=p3[:, 0:128], lhsT=kdec32[0:64, :], rhs=ub[0:64, :], start=True, stop=True)
                                sn, snb = sn_r.get()
                                OP("vector", "scalar_tensor_tensor", [p3b, B_, s0b, snb], [snb], out=sn[:, :], in0=s0t[:, :], scalar=eGB[par][:, hl, 4 * b + 3:4 * b + 4], in1=p3[:, 0:128],
                                   op0=ALU.mult, op1=ALU.add)
                                P.dma("sync", deltas_out[j, b, h], sn[:, :], reads=[snb])
                            OP("vector", "tensor_tensor", [p2b, boacc, O_], [O_], out=otok[par][0:64, 0, hl, :], in0=p2[0:64, 0:128], in1=oacc[0:64, :], op=ALU.add)
                    o8 = otok[par][0:bs, 0:nb, :, :].rearrange("p b h d -> p (b h) d")
                    nh = NP_
                    OP("scalar", "activation", [O_, bD], [bD], out=on[0:bs, 0:nh, :], in_=o8, func=AF.Square)
                    OP("vector", "tensor_reduce", [bD], [bD], out=ssq[0:bs, 0:nh], in_=on[0:bs, 0:nh, :], axis=AX.X, op=ALU.add)
                    OP("scalar", "activation", [bD, bset], [bD], out=ssq[0:bs, 0:nh], in_=ssq[0:bs, 0:nh], func=AF.Ln, bias=eps2[0:bs, :], scale=1.0 / 128)
                    OP("scalar", "activation", [bD], [bD], out=ssq[0:bs, 0:nh], in_=ssq[0:bs, 0:nh], func=AF.Exp, scale=-0.5)
                    OP("vector", "tensor_tensor", [O_, bD], [bD], out=on[0:bs, 0:nh, :], in0=o8, in1=ap4(ssq, 0, bs, ssq[0:1, 0:1], [[1, nh], [0, 128]]), op=ALU.mult)
                    OP("vector", "tensor_tensor", [bD, bset], [bD], out=on[0:bs, 0:nh, :], in0=on[0:bs, 0:nh, :], in1=ap4(nwB, 0, bs, nwB[0:1, 0:1], [[0, nh], [1, 128]]), op=ALU.mult)
                    OP("vector", "tensor_tensor", [bD, bZ[par]], [bD], out=on[0:bs, 0:nh, :], in0=on[0:bs, 0:nh, :], in1=zs[par][0:bs, 0:nb, :].rearrange("p b (h d) -> p (b h) d", d=128), op=ALU.mult)
                    pso, psob = ps_get()
                    for hl in range(2):
                        for blk in range(nb):
                            OP("tensor", "transpose", [bD, bident], [psob], out=pso[:, hl * 256 + blk * bs:hl * 256 + (blk + 1) * bs], in_=on[0:bs, blk * 2 + hl, :], identity=ident[0:bs, 0:bs])
                    OP("scalar", "copy", [psob, bD], [bD], out=oTb[:, :, 0:n], in_=pso[:, :].rearrange("p (h t) -> p h t", t=256)[:, :, 0:n])
                    for c in range(8):
                        ps, pb = ps_get()
                        for hl in range(2):
                            OP("tensor", "matmul", [wOb, bD], [pb], out=ps[:, 0:n], lhsT=wO[:, (c // 4) * 2 + hl, (c % 4) * 128:(c % 4 + 1) * 128], rhs=oTb[:, hl, 0:n],
                               start=(hl == 0), stop=(hl == 1))
                        OP("vector", "tensor_tensor", [pb, bx[ti][c]], [bx[ti][c]], out=xT[:, c, t0:t0 + n], in0=ps[:, 0:n], in1=xT[:, c, t0:t0 + n], op=ALU.add)
                for hl in range(2):
                    P.dma("sync", deltap_out[j, 2 * hg + hl], S32[:, hl, :], reads=[bS])

        load_x()
        for layer in range(DEPTH):
            if layer % 2 == 0 and stage >= 3:
                arena_reset()
                norm_alloc()
                rmsnorm_h(layer)
                arena_reset()
                deltanet(layer // 2)
            if layer % 2 == 1 and stage >= 2:
                arena_reset()
                norm_alloc()
                rmsnorm_h(layer)
                arena_reset()
                swa(layer // 2)
            arena_reset()
            norm_alloc()
            rmsnorm_h(4 + layer)
            arena_reset()
            ffn(layer)
        arena_reset()
        norm_alloc()
        final_out()
        P.emit()
    return nc


def core_inputs(inputs, c):
    f = lambda a: np.ascontiguousarray(np.asarray(a, dtype=np.float32))
    sl = slice(NB * c, NB * (c + 1))
    return {
        "xp": f(inputs["x_prompt"][c]),
        "xs": f(inputs["x_sample"][sl].reshape(TS, D)),
        "norm_mix": f(inputs["norm_mix"]),
        "norm_ffn": f(inputs["norm_ffn"]),
        "norm_final": f(inputs["norm_final"]).reshape(1, D),
        "ffn_w_gu": f(inputs["ffn_w_gu"]),
        "ffn_w_down": f(inputs["ffn_w_down"]),
        "swa_w_qkv": f(inputs["swa_w_qkv"]),
        "swa_b_qkv": f(inputs["swa_b_qkv"]),
        "swa_sinks": f(inputs["swa_sinks"]),
        "swa_w_o": f(inputs["swa_w_o"]),
        "swa_b_o": f(inputs["swa_b_o"]),
        "dn_w_in": f(inputs["dn_w_in"]),
        "dn_conv_w": f(inputs["dn_conv_w"]),
        "dn_a_log": f(inputs["dn_a_log"]),
        "dn_dt_bias": f(inputs["dn_dt_bias"]),
        "dn_norm_w": f(inputs["dn_norm_w"]),
        "dn_w_out": f(inputs["dn_w_out"]),
        "state_conv": f(inputs["state_conv"][:, sl].reshape(2, NB * 3, 4096)),
        "state_delta": f(inputs["state_delta"][:, sl]),
        "cache_k": f(inputs["cache_k"][:, sl].reshape(2, NB, 128, 256)),
        "cache_v": f(inputs["cache_v"][:, sl].reshape(2, NB, 128, 256)),
    }


def kernel(**inputs):
    nc = build_program()
    in_maps = [core_inputs(inputs, c) for c in range(NCORES)]
    res = run_bass_kernel_spmd(nc, in_maps, core_ids=list(range(NCORES)))
    R = res.results
    f = lambda a: np.asarray(a, dtype=np.float32)
    y_prompt = np.stack([f(R[c]["yp"]) for c in range(NCORES)], 0)
    y_sample = np.concatenate([f(R[c]["ys"]).reshape(NB, 4, D) for c in range(NCORES)], 0)
    conv_p = np.stack([f(R[c]["convp"]) for c in range(NCORES)], 1)
    delta_p = np.stack([f(R[c]["deltap"]) for c in range(NCORES)], 1)
    k_p = np.stack([f(R[c]["kp"]).reshape(2, 128, 4, 64) for c in range(NCORES)], 1)
    v_p = np.stack([f(R[c]["vp"]).reshape(2, 128, 4, 64) for c in range(NCORES)], 1)
    conv_s = np.concatenate([f(R[c]["convs"]) for c in range(NCORES)], 1)
    delta_s = np.concatenate([f(R[c]["deltas"]) for c in range(NCORES)], 1)
    k_s = np.concatenate([f(R[c]["ks"]).reshape(2, NB, 4, 4, 64) for c in range(NCORES)], 1)
    v_s = np.concatenate([f(R[c]["vs"]).reshape(2, NB, 4, 4, 64) for c in range(NCORES)], 1)
    return (y_prompt, y_sample, conv_p, delta_p, k_p, v_p, conv_s, delta_s, k_s, v_s)
```

```python
import contextlib
import numpy as np
import concourse.bass as bass
import concourse.mybir as mybir
from concourse.bass_utils import run_bass_kernel_spmd

F32 = mybir.dt.float32
BF16 = mybir.dt.bfloat16
AF = mybir.ActivationFunctionType
ALU = mybir.AluOpType
AX = mybir.AxisListType

ENGS = ("tensor", "vector", "scalar", "gpsimd", "sync")
SAME_ENG_SYNC = {"tensor": False, "vector": True, "scalar": True, "gpsimd": True, "sync": False}
N_DMA_SEMS = 48


class Buf:
    __slots__ = ("name", "last_w", "readers")

    def __init__(self, name="b"):
        self.name = name
        self.last_w = None
        self.readers = []


class Prog:
    def __init__(self, nc):
        self.nc = nc
        self.ops = {e: [] for e in ENGS}
        self.count = {e: 0 for e in ENGS}
        self.known = {e: {} for e in ENGS}
        self.clock = {}
        self.dma_n = 0
        self.dma_cnt = [0] * N_DMA_SEMS
        self.nwaits = 0

    def _need(self, eng, tok, waits):
        semkey, val, teng = tok
        if teng == eng and not SAME_ENG_SYNC[eng]:
            return
        k = self.known[eng]
        if k.get(semkey, 0) >= val:
            return
        waits.append((semkey, val))
        self.nwaits += 1
        c = self.clock.get((semkey, val))
        if c is not None:
            for s, v in c.items():
                if k.get(s, 0) < v:
                    k[s] = v
        k[semkey] = val

    def _deps(self, eng, reads, writes, waits):
        for b in reads:
            if b.last_w is not None:
                self._need(eng, b.last_w, waits)
        for b in writes:
            if b.last_w is not None:
                self._need(eng, b.last_w, waits)
            for r in b.readers:
                self._need(eng, r, waits)

    def _commit(self, tok, reads, writes):
        for b in reads:
            b.readers.append(tok)
            if len(b.readers) > 16:
                d = {}
                for t in b.readers:
                    if t[0] not in d or d[t[0]][1] < t[1]:
                        d[t[0]] = t
                b.readers = list(d.values())
        for b in writes:
            b.last_w = tok
            b.readers = []

    def op(self, eng, fn, reads=(), writes=()):
        waits = []
        self._deps(eng, reads, writes, waits)
        self.count[eng] += 1
        semkey = ("e", eng)
        val = self.count[eng]
        tok = (semkey, val, eng)
        c = dict(self.known[eng])
        c[semkey] = val
        self.clock[(semkey, val)] = c
        self.ops[eng].append((waits, fn, semkey, 1))
        self._commit(tok, reads, writes)
        return tok

    def dma(self, queue, out, in_, reads=(), writes=(), **kw):
        waits = []
        self._deps(queue, reads, writes, waits)
        j = self.dma_n % N_DMA_SEMS
        self.dma_n += 1
        semkey = ("d", j)
        prev = self.dma_cnt[j]
        if prev:
            self._need(queue, (semkey, prev * 16, "dma"), waits)
        self.dma_cnt[j] += 1
        val = self.dma_cnt[j] * 16
        tok = (semkey, val, "dma")
        self.clock[(semkey, val)] = dict(self.known[queue])

        def fn(e, out=out, in_=in_, kw=kw):
            return e.dma_start(out=out, in_=in_, **kw)

        self.ops[queue].append((waits, fn, semkey, 16))
        self._commit(tok, reads, writes)
        return tok

    def fence(self):
        toks = [(("e", x), self.count[x], x) for x in ENGS if self.count[x] > 0]
        toks += [(("d", j), self.dma_cnt[j] * 16, "dma") for j in range(N_DMA_SEMS) if self.dma_cnt[j]]
        for e in ENGS:
            waits = []
            for t in toks:
                if t[2] != e:
                    self._need(e, t, waits)
            if waits:
                self.ops[e].append((waits, None, None, 0))

    def emit(self):
        nc = self.nc
        with contextlib.ExitStack() as st:
            st.enter_context(nc.allow_non_contiguous_dma(reason="small strided layout DMAs"))
            sems = {}
            for e in ENGS:
                sems[("e", e)] = st.enter_context(nc.semaphore(f"s_{e}"))
            for j in range(N_DMA_SEMS):
                sems[("d", j)] = st.enter_context(nc.semaphore(f"s_d{j}"))
            fin = []
            for j in range(N_DMA_SEMS):
                if self.dma_cnt[j]:
                    fin.append((("d", j), self.dma_cnt[j] * 16))
            for e in ENGS:
                if e != "sync" and self.count[e]:
                    fin.append((("e", e), self.count[e]))
            block = st.enter_context(nc.Block())
            for e in ENGS:
                ops = self.ops[e]
                extra = fin if e == "sync" else []

                def body(eng, ops=ops, extra=extra):
                    for waits, fn, semkey, inc in ops:
                        for sk, v in waits:
                            eng.wait_ge(sems[sk], v)
                        if fn is not None:
                            inst = fn(eng)
                            inst.then_inc(sems[semkey], inc)
                    for sk, v in extra:
                        eng.wait_ge(sems[sk], v)

                getattr(block, e)(body)


D = 1024
TP = 2048
NB = 16
TS = 64
NT = TP + TS
DEPTH = 4
DFF = 2816
DN_IN = 6176
RMS_EPS = 1e-6
TILES = [(0, 512), (512, 512), (1024, 512), (1536, 512), (2048, 64)]
NCORES = 8


class Rot:
    def __init__(self, alloc, name, shape, dtype, n):
        self.t = [alloc(f"{name}{i}", shape, dtype) for i in range(n)]
        self.b = [Buf(f"{name}{i}") for i in range(n)]
        self.i = 0

    def get(self):
        i = self.i
        self.i = (i + 1) % len(self.t)
        return self.t[i], self.b[i]


def build_program(stage=99, debug=False):
    nc = bass.Bass("TRN2", target_bir_lowering=False)
    dbg_state = {"n": 0}

    def dump(P, name, ap, reads, once=True):
        if not debug:
            return
        key = "dbg_" + name
        if once and key in dbg_state:
            return
        dbg_state[key] = 1
        shp = list(ap.shape)
        dt_ = ap.dtype
        t = nc.dram_tensor(key, shp, dt_, kind="ExternalOutput").ap()
        P.dma("sync", t, ap, reads=reads)

    def din(name, shape):
        return nc.dram_tensor(name, list(shape), F32, kind="ExternalInput").ap()

    def dout(name, shape):
        return nc.dram_tensor(name, list(shape), F32, kind="ExternalOutput").ap()

    xp = din("xp", [TP, D])
    xs = din("xs", [TS, D])
    norm_mix = din("norm_mix", [DEPTH, D])
    norm_ffn = din("norm_ffn", [DEPTH, D])
    norm_final = din("norm_final", [1, D])
    ffn_w_gu = din("ffn_w_gu", [DEPTH, D, 2 * DFF])
    ffn_w_down = din("ffn_w_down", [DEPTH, DFF, D])
    swa_w_qkv = din("swa_w_qkv", [2, D, 1536])
    swa_b_qkv = din("swa_b_qkv", [2, 1536])
    swa_sinks = din("swa_sinks", [2, 16])
    swa_w_o = din("swa_w_o", [2, D, D])
    swa_b_o = din("swa_b_o", [2, D])
    ck_in = din("cache_k", [2, NB, 128, 256])
    cv_in = din("cache_v", [2, NB, 128, 256])
    dn_w_in = din("dn_w_in", [2, D, DN_IN])
    dn_conv_w = din("dn_conv_w", [2, 4, 4096])
    dn_a_log = din("dn_a_log", [2, 16])
    dn_dt_bias = din("dn_dt_bias", [2, 16])
    dn_norm_w = din("dn_norm_w", [2, 128])
    dn_w_out = din("dn_w_out", [2, 2048, D])
    sconv_in = din("state_conv", [2, NB * 3, 4096])
    sdelta_in = din("state_delta", [2, NB, 16, 128, 128])
    yp = dout("yp", [TP, D])
    ys = dout("ys", [TS, D])
    convp_out = dout("convp", [2, 3, 4096])
    deltap_out = dout("deltap", [2, 16, 128, 128])
    convs_out = dout("convs", [2, NB, 3, 4096])
    deltas_out = dout("deltas", [2, NB, 16, 128, 128])
    kp_out = dout("kp", [2, 128, 256])
    vp_out = dout("vp", [2, 128, 256])
    ks_out = dout("ks", [2, TS, 256])
    vs_out = dout("vs", [2, TS, 256])

    with contextlib.ExitStack() as st:
        def sb(name, shape, dt):
            return st.enter_context(nc.sbuf_tensor(name, list(shape), dt))[:]

        arena_state = {"off": 0, "ap": None, "size": 0}

        def carve(name, shape, dt):
            shape = list(shape)
            n = 1
            for x in shape[1:]:
                n *= x
            isz = 2 if dt == BF16 else 4
            n32 = (n * isz + 3) // 4
            n32 = (n32 + 7) // 8 * 8
            off = arena_state["off"]
            assert off + n32 <= arena_state["size"], (name, off, n32, arena_state["size"])
            arena_state["off"] = off + n32
            ap = arena_state["ap"][:, off:off + n32]
            if dt != F32:
                ap = ap.bitcast(dt)
            ap = ap[:, 0:n]
            if len(shape) == 3:
                ap = ap.rearrange("p (a b) -> p a b", b=shape[2])
            elif len(shape) == 4:
                ap = ap.rearrange("p (a b c) -> p a b c", b=shape[2], c=shape[3])
            return ap

        def arena_reset():
            P.fence()
            arena_state["off"] = 0

        P = Prog(nc)
        V = lambda fn, r=(), w=(): P.op("vector", fn, r, w)
        A = lambda fn, r=(), w=(): P.op("scalar", fn, r, w)
        T = lambda fn, r=(), w=(): P.op("tensor", fn, r, w)
        G = lambda fn, r=(), w=(): P.op("gpsimd", fn, r, w)

        xT = sb("xT", [128, 8, NT], F32)
        hT = sb("hT", [128, 8, NT], BF16)
        bx = [[Buf(f"x{t}_{c}") for c in range(8)] for t in range(5)]
        bh = [Buf(f"h{t}") for t in range(5)]

        ident = sb("ident", [128, 128], F32)
        ones_bf = sb("ones_bf", [128, 128], BF16)
        eps_rms = sb("eps_rms", [128, 1], F32)
        nw = sb("nw", [128, 9, 8], F32)
        bconst = Buf("const")
        bnw = Buf("nw")
        bident = Buf("ident")
        G(lambda e: e.memset(ident[:], 1.0), w=[bident])
        G(lambda e: e.affine_select(out=ident[:], in_=ident[:], pattern=[[-1, 128]],
                                    compare_op=ALU.is_equal, fill=0.0, base=0, channel_multiplier=1),
          r=[bident], w=[bident])
        V(lambda e: e.memset(ones_bf[:], 1.0), w=[bconst])
        V(lambda e: e.memset(eps_rms[:], RMS_EPS), r=[bconst], w=[bconst])
        P.dma("sync", nw[:, 0:4, :], norm_mix.rearrange("l (c p) -> p l c", p=128), writes=[bnw])
        P.dma("sync", nw[:, 4:8, :], norm_ffn.rearrange("l (c p) -> p l c", p=128), reads=[bnw], writes=[bnw])
        P.dma("sync", nw[:, 8:9, :], norm_final.rearrange("l (c p) -> p l c", p=128), reads=[bnw], writes=[bnw])
        P.fence()

        psb = [st.enter_context(nc.psum_tensor(f"ps{i}", [128, 512], F32)) for i in range(8)]
        psB = [Buf(f"ps{i}") for i in range(8)]
        pstate = {"i": 0}

        def ps_get():
            i = pstate["i"]
            pstate["i"] = (i + 1) % 6
            return psb[i], psB[i]

        wrot = Rot(sb, "wslot", [128, 8, 512], BF16, 4)
        asz = (nc.sbuf_bytes_remaining - 256) // 4 // 8 * 8
        arena_state["ap"] = sb("arena", [128, asz], F32)
        arena_state["size"] = asz

        def wload(dram2d, r0, nrow_chunks, c0, ncols):
            t, b = wrot.get()
            src = dram2d[r0:r0 + nrow_chunks * 128, c0:c0 + ncols].rearrange("(c p) n -> p c n", p=128)
            P.dma("gpsimd", t[:, 0:nrow_chunks, 0:ncols], src, writes=[b])
            return t, b


        def load_x():
            xin = Rot(carve, "xin", [128, D], F32, 3)
            blocks = [(xp, t0, 128, t0) for t0 in range(0, TP, 128)] + [(xs, 0, 64, TP)]
            for src, r0, n, tok0 in blocks:
                t, b = xin.get()
                P.dma("sync", t[0:n, :], src[r0:r0 + n, :], writes=[b])
                tt = min(tok0 // 512, 4)
                for half in range(2):
                    ps, pb = ps_get()
                    for cc in range(4):
                        c = half * 4 + cc
                        T(lambda e, ps=ps, t=t, c=c, cc=cc, n=n: e.transpose(
                            out=ps[:, cc * 128:cc * 128 + n], in_=t[0:n, c * 128:(c + 1) * 128],
                            identity=ident[0:n, 0:n]), r=[b, bident], w=[pb])
                    V(lambda e, ps=ps, half=half, tok0=tok0, n=n: e.tensor_copy(
                        out=xT[:, half * 4:half * 4 + 4, tok0:tok0 + n],
                        in_=ps[:].rearrange("p (c t) -> p c t", t=128)[:, :, 0:n]),
                      r=[pb], w=[bx[tt][c] for c in range(half * 4, half * 4 + 4)])

        nrm = {}

        def norm_alloc():
            nrm["sq"] = Rot(carve, "sq", [128, 8, 512], BF16, 2)
            nrm["rs"] = Rot(carve, "rs", [128, 512], F32, 2)

        def rmsnorm(widx, out_fn, out_bufs_fn):
            for ti, (t0, n) in enumerate(TILES):
                sq, sqb = nrm["sq"].get()
                A(lambda e, sq=sq, t0=t0, n=n: e.activation(out=sq[:, :, 0:n], in_=xT[:, :, t0:t0 + n], func=AF.Square),
                  r=bx[ti], w=[sqb])
                ps, pb = ps_get()
                for c in range(8):
                    T(lambda e, ps=ps, sq=sq, c=c, n=n: e.matmul(ps[:, 0:n], lhsT=ones_bf[:], rhs=sq[:, c, 0:n],
                                                               start=(c == 0), stop=(c == 7)),
                      r=[sqb, bconst], w=[pb])
                rs, rsb = nrm["rs"].get()
                A(lambda e, rs=rs, ps=ps, n=n: e.activation(out=rs[:, 0:n], in_=ps[:, 0:n], func=AF.Sqrt,
                                                           bias=eps_rms[:], scale=1.0 / D),
                  r=[pb, bconst], w=[rsb])
                V(lambda e, rs=rs, n=n: e.reciprocal(out=rs[:, 0:n], in_=rs[:, 0:n]), r=[rsb], w=[rsb])
                for c in range(8):
                    o_ap, o_bufs = out_fn(ti, c, t0, n), out_bufs_fn(ti, c)
                    V(lambda e, o_ap=o_ap, c=c, t0=t0, n=n, rs=rs: e.scalar_tensor_tensor(
                        out=o_ap, in0=xT[:, c, t0:t0 + n], scalar=nw[:, widx, c:c + 1], in1=rs[:, 0:n],
                        op0=ALU.mult, op1=ALU.mult),
                      r=[bx[ti][c], rsb, bnw], w=o_bufs)

        def rmsnorm_h(widx):
            rmsnorm(widx, lambda ti, c, t0, n: hT[:, c, t0:t0 + n], lambda ti, c: [bh[ti]])

        def ffn(layer):
            act = carve("act", [128, 8, NT], BF16)
            bact = [Buf(f"act{t}") for t in range(5)]
            sgr = Rot(carve, "sg", [128, 512], F32, 3)
            wgu = ffn_w_gu[layer]
            wdn = ffn_w_down[layer]
            for (j0, Gn) in [(0, 8), (8, 8), (16, 6)]:
                for s in range((Gn + 3) // 4):
                    nch = min(4, Gn - 4 * s)
                    wg, wgb = wload(wgu, 0, 8, (j0 + 4 * s) * 128, nch * 128)
                    wu, wub = wload(wgu, 0, 8, DFF + (j0 + 4 * s) * 128, nch * 128)
                    for jj in range(nch):
                        j = 4 * s + jj
                        for ti, (t0, n) in enumerate(TILES):
                            psg, pgb = ps_get()
                            for k in range(8):
                                T(lambda e, psg=psg, wg=wg, k=k, jj=jj, t0=t0, n=n: e.matmul(
                                    psg[:, 0:n], lhsT=wg[:, k, jj * 128:(jj + 1) * 128], rhs=hT[:, k, t0:t0 + n],
                                    start=(k == 0), stop=(k == 7)), r=[wgb, bh[ti]], w=[pgb])
                            psu, pub = ps_get()
                            for k in range(8):
                                T(lambda e, psu=psu, wu=wu, k=k, jj=jj, t0=t0, n=n: e.matmul(
                                    psu[:, 0:n], lhsT=wu[:, k, jj * 128:(jj + 1) * 128], rhs=hT[:, k, t0:t0 + n],
                                    start=(k == 0), stop=(k == 7)), r=[wub, bh[ti]], w=[pub])
                            sg, sgb = sgr.get()
                            A(lambda e, sg=sg, psg=psg, n=n: e.activation(out=sg[:, 0:n], in_=psg[:, 0:n], func=AF.Silu),
                              r=[pgb], w=[sgb])
                            V(lambda e, sg=sg, psu=psu, j=j, t0=t0, n=n: e.tensor_tensor(
                                out=act[:, j, t0:t0 + n], in0=sg[:, 0:n], in1=psu[:, 0:n], op=ALU.mult),
                              r=[sgb, pub], w=[bact[ti]])
                for half in range(2):
                    wd, wdb = wload(wdn, j0 * 128, Gn, half * 512, 512)
                    for ti, (t0, n) in enumerate(TILES):
                        for nn in range(4):
                            c = half * 4 + nn
                            ps, pb = ps_get()
                            for kk in range(Gn):
                                T(lambda e, ps=ps, wd=wd, kk=kk, nn=nn, t0=t0, n=n, Gn=Gn: e.matmul(
                                    ps[:, 0:n], lhsT=wd[:, kk, nn * 128:(nn + 1) * 128], rhs=act[:, kk, t0:t0 + n],
                                    start=(kk == 0), stop=(kk == Gn - 1)), r=[wdb, bact[ti]], w=[pb])
                            V(lambda e, ps=ps, c=c, t0=t0, n=n: e.tensor_tensor(
                                out=xT[:, c, t0:t0 + n], in0=ps[:, 0:n], in1=xT[:, c, t0:t0 + n], op=ALU.add),
                              r=[pb, bx[ti][c]], w=[bx[ti][c]])

        def final_out():
            yfr = Rot(carve, "yf", [128, 8, 512], F32, 1)
            your = Rot(carve, "yout", [128, D], F32, 2)
            yf, yfb = yfr.get()
            for ti, (t0, n) in enumerate(TILES):
                sq, sqb = nrm["sq"].get()
                A(lambda e, sq=sq, t0=t0, n=n: e.activation(out=sq[:, :, 0:n], in_=xT[:, :, t0:t0 + n], func=AF.Square),
                  r=bx[ti], w=[sqb])
                ps, pb = ps_get()
                for c in range(8):
                    T(lambda e, ps=ps, sq=sq, c=c, n=n: e.matmul(ps[:, 0:n], lhsT=ones_bf[:], rhs=sq[:, c, 0:n],
                                                               start=(c == 0), stop=(c == 7)),
                      r=[sqb, bconst], w=[pb])
                rs, rsb = nrm["rs"].get()
                A(lambda e, rs=rs, ps=ps, n=n: e.activation(out=rs[:, 0:n], in_=ps[:, 0:n], func=AF.Sqrt,
                                                           bias=eps_rms[:], scale=1.0 / D),
                  r=[pb, bconst], w=[rsb])
                V(lambda e, rs=rs, n=n: e.reciprocal(out=rs[:, 0:n], in_=rs[:, 0:n]), r=[rsb], w=[rsb])
                for c in range(8):
                    V(lambda e, c=c, t0=t0, n=n, rs=rs, yf=yf: e.scalar_tensor_tensor(
                        out=yf[:, c, 0:n], in0=xT[:, c, t0:t0 + n], scalar=nw[:, 8, c:c + 1], in1=rs[:, 0:n],
                        op0=ALU.mult, op1=ALU.mult),
                      r=[bx[ti][c], rsb, bnw], w=[yfb])
                for s0 in range(0, n, 128):
                    m = min(128, n - s0)
                    yo, yob = your.get()
                    for half in range(2):
                        ps, pb = ps_get()
                        for cc in range(4):
                            c = half * 4 + cc
                            T(lambda e, ps=ps, yf=yf, c=c, cc=cc, s0=s0, m=m: e.transpose(
                                out=ps[0:m, cc * 128:(cc + 1) * 128], in_=yf[:, c, s0:s0 + m], identity=ident[:]),
                              r=[yfb, bident], w=[pb])
                        A(lambda e, ps=ps, yo=yo, half=half, m=m: e.copy(out=yo[0:m, half * 512:(half + 1) * 512], in_=ps[0:m, :]),
                          r=[pb], w=[yob])
                    dst = yp[t0 + s0:t0 + s0 + m, :] if t0 < TP else ys[s0:s0 + m, :]
                    P.dma("sync", dst, yo[0:m, :], reads=[yob])


        SLOPES = [2.0 ** (-8.0 * (h + 1) / 16) for h in range(16)]
        SCALE = 0.125
        NEG = -30000.0

        def pair_heads(i):
            kc, g = divmod(i, 4)
            return (2 * kc) * 4 + g, (2 * kc + 1) * 4 + g

        def swa(j):
            wqkv = swa_w_qkv[j]
            qT = carve("qT", [128, 8, NT], BF16)
            kT = carve("kT", [128, 2, NT], BF16)
            vtok = carve("vtok", [128, 17, 4, 65], BF16)
            maskT = carve("maskT", [128, 256], F32)
            relT = carve("relT", [128, 256], F32)
            reli = carve("reli", [128, 256], mybir.dt.int32) if False else None
            bq = carve("bq", [128, 8], F32)
            bk = carve("bk", [128, 2], F32)
            bo = carve("bo", [128, 8], F32)
            bkv = carve("bkv", [128, 512], F32)
            sinkEB = carve("sinkEB", [128, 16], F32)
            sink16 = carve("sink16", [128, 4], F32)
            biasC = carve("biasC", [128, 16, 4], F32)
            relC = carve("relC", [128, 4], F32)
            maskC = carve("maskC", [128, 4], F32)
            relN = carve("relN", [128, 16, 4], F32)
            maskN = carve("maskN", [128, 16, 4], F32)
            biasN = carve("biasN", [128, 16, 16, 4], F32)
            bq_b, bk_b, bo_b, bkv_b, bsink, bmask, bvt = [Buf(x) for x in "bq bk bo bkv sink mask vtok".split()]
            bqT = [Buf(f"qT{t}") for t in range(5)]
            bkT = [Buf(f"kT{t}") for t in range(5)]
            bvtok = [Buf(f"vt{b}") for b in range(17)]
            e_r = Rot(carve, "e", [128, 256], F32, 2)
            p_r = Rot(carve, "p", [128, 256], BF16, 2)
            otok_r = Rot(carve, "otok", [128, 8, 128], F32, 1)
            den_r = Rot(carve, "den", [128, 4], F32, 2)
            kvtm_r = Rot(carve, "kvtm", [128, 256], F32, 1)
            ckst_r = Rot(carve, "ckst", [128, 256], F32, 1)
            cvst_r = Rot(carve, "cvst", [128, 256], F32, 1)
            kct_r = Rot(carve, "kct", [128, 2, 128], BF16, 2)
            vc_r = Rot(carve, "vc", [128, 4, 65], BF16, 2)
            es_r = Rot(carve, "es", [128, 64], F32, 2)
            psb_r = Rot(carve, "psb", [128, 64], BF16, 2)
            esn_r = Rot(carve, "esn", [128, 64], F32, 2)
            pn_r = Rot(carve, "pn", [128, 64], BF16, 2)
            os_r = Rot(carve, "os", [128, 4, 64], F32, 1)

            for i in range(8):
                ha, hb = pair_heads(i)
                P.dma("sync", bq[0:64, i:i + 1], swa_b_qkv[j, ha * 64:(ha + 1) * 64].rearrange("(p o) -> p o", o=1), writes=[bq_b], reads=[bq_b])
                P.dma("sync", bq[64:128, i:i + 1], swa_b_qkv[j, hb * 64:(hb + 1) * 64].rearrange("(p o) -> p o", o=1), writes=[bq_b], reads=[bq_b])
            P.dma("sync", bk[:, :], swa_b_qkv[j, 1024:1280].rearrange("(c p) -> p c", p=128), writes=[bk_b])
            P.dma("sync", bo[:, :], swa_b_o[j].rearrange("(c p) -> p c", p=128), writes=[bo_b])
            P.dma("sync", bkv[:, :], swa_b_qkv[j:j + 1, 1024:1536].partition_broadcast(128).rearrange("p o n -> p (o n)"), writes=[bkv_b])
            P.dma("sync", sinkEB[:, :], swa_sinks[j:j + 1, :].partition_broadcast(128).rearrange("p o n -> p (o n)"), writes=[bsink])
            A(lambda e: e.activation(out=sinkEB[:, :], in_=sinkEB[:, :], func=AF.Exp), r=[bsink], w=[bsink])
            for g in range(4):
                P.dma("sync", sink16[4 * g:4 * g + 4, 0:4],
                      swa_sinks[j:j + 1, :].rearrange("o (kv g) -> o kv g", g=4)[:, :, g].partition_broadcast(4).rearrange("p o n -> p (o n)"),
                      writes=[bsink], reads=[bsink])
            A(lambda e: e.activation(out=sink16[0:16, :], in_=sink16[0:16, :], func=AF.Exp), r=[bsink], w=[bsink])
            P.fence()
            I32 = mybir.dt.int32
            ri = carve("ri", [128, 256], I32)
            G(lambda e: e.iota(ri[:, 0:128], pattern=[[1, 128]], base=128, channel_multiplier=-1), w=[bmask])
            G(lambda e: e.iota(ri[:, 128:256], pattern=[[1, 128]], base=0, channel_multiplier=-1), r=[bmask], w=[bmask])
            G(lambda e: e.tensor_copy(out=relT[:, :], in_=ri[:, :]), r=[bmask], w=[bmask])
            G(lambda e: e.memset(maskT[:, :], 0.0), r=[bmask], w=[bmask])
            G(lambda e: e.affine_select(out=maskT[:, 0:128], in_=maskT[:, 0:128], pattern=[[-1, 128]],
                                        compare_op=ALU.is_ge, fill=NEG, base=0, channel_multiplier=1), r=[bmask], w=[bmask])
            G(lambda e: e.affine_select(out=maskT[:, 128:256], in_=maskT[:, 128:256], pattern=[[1, 128]],
                                        compare_op=ALU.is_ge, fill=NEG, base=0, channel_multiplier=-1), r=[bmask], w=[bmask])
            G(lambda e: e.iota(ri[:, 0:4], pattern=[[1, 4]], base=128, channel_multiplier=-1), r=[bmask], w=[bmask])
            G(lambda e: e.tensor_copy(out=relC[:, :], in_=ri[:, 0:4]), r=[bmask], w=[bmask])
            G(lambda e: e.memset(maskC[:, :], 0.0), r=[bmask], w=[bmask])
            G(lambda e: e.affine_select(out=maskC[:, :], in_=maskC[:, :], pattern=[[-1, 4]],
                                        compare_op=ALU.is_ge, fill=NEG, base=0, channel_multiplier=1), r=[bmask], w=[bmask])
            G(lambda e: e.iota(ri[:, 0:64], pattern=[[4, 16], [1, 4]], base=0, channel_multiplier=-1), r=[bmask], w=[bmask])
            G(lambda e: e.tensor_copy(out=relN[:, :, :], in_=ri[:, 0:64].rearrange("p (b t) -> p b t", t=4)), r=[bmask], w=[bmask])
            G(lambda e: e.memset(maskN[:, :, :], 0.0), r=[bmask], w=[bmask])
            G(lambda e: e.affine_select(out=maskN[:, :, :], in_=maskN[:, :, :], pattern=[[4, 16], [1, 4]],
                                        compare_op=ALU.is_ge, fill=NEG, base=0, channel_multiplier=-1), r=[bmask], w=[bmask])
            G(lambda e: e.affine_select(out=maskN[:, :, :], in_=maskN[:, :, :], pattern=[[-4, 16], [0, 4]],
                                        compare_op=ALU.is_ge, fill=NEG, base=0, channel_multiplier=1), r=[bmask], w=[bmask])
            for h in range(16):
                V(lambda e, h=h: e.scalar_tensor_tensor(out=biasC[:, h, :], in0=relC[:, :], scalar=-SLOPES[h], in1=maskC[:, :],
                                                        op0=ALU.mult, op1=ALU.add), r=[bmask], w=[bmask])
                V(lambda e, h=h: e.scalar_tensor_tensor(out=biasN[:, :, h, :], in0=relN[:, :, :], scalar=-SLOPES[h], in1=maskN[:, :, :],
                                                        op0=ALU.mult, op1=ALU.add), r=[bmask], w=[bmask])
            V(lambda e: e.memset(vtok[:, :, :, 64:65], 1.0), w=[bvt])
            P.fence()

            for s_ in range(2):
                wq_, wqb = wrot.get()
                for ii in range(4):
                    ha, hb = pair_heads(s_ * 4 + ii)
                    for hf, hh in ((0, ha), (1, hb)):
                        P.dma("gpsimd", wq_[:, :, ii * 128 + hf * 64:ii * 128 + hf * 64 + 64],
                              wqkv[:, hh * 64:(hh + 1) * 64].rearrange("(c p) n -> p c n", p=128), writes=[wqb], reads=[wqb])
                for ii in range(4):
                    i = s_ * 4 + ii
                    for ti, (t0, n) in enumerate(TILES):
                        ps, pb = ps_get()
                        for k in range(8):
                            T(lambda e, ps=ps, wq_=wq_, ii=ii, k=k, t0=t0, n=n: e.matmul(
                                ps[:, 0:n], lhsT=wq_[:, k, ii * 128:(ii + 1) * 128], rhs=hT[:, k, t0:t0 + n], start=(k == 0), stop=(k == 7)),
                              r=[wqb, bh[ti]], w=[pb])
                        A(lambda e, ps=ps, i=i, t0=t0, n=n: e.activation(out=qT[:, i, t0:t0 + n], in_=ps[:, 0:n], func=AF.Identity,
                                                                        bias=bq[:, i:i + 1], scale=1.0),
                          r=[pb, bq_b], w=[bqT[ti]])
            wkv, wkvb = wload(wqkv, 0, 8, 1024, 512)
            for c in range(2):
                for ti, (t0, n) in enumerate(TILES):
                    ps, pb = ps_get()
                    for k in range(8):
                        T(lambda e, ps=ps, c=c, k=k, t0=t0, n=n: e.matmul(
                            ps[:, 0:n], lhsT=wkv[:, k, c * 128:(c + 1) * 128], rhs=hT[:, k, t0:t0 + n], start=(k == 0), stop=(k == 7)),
                          r=[wkvb, bh[ti]], w=[pb])
                    A(lambda e, ps=ps, c=c, t0=t0, n=n: e.activation(out=kT[:, c, t0:t0 + n], in_=ps[:, 0:n], func=AF.Identity,
                                                                    bias=bk[:, c:c + 1], scale=1.0),
                      r=[pb, bk_b], w=[bkT[ti]])
            for blk in range(17):
                t0, n = (blk * 128, 128) if blk < 16 else (TP, TS)
                ti = min(blk // 4, 4)
                need_k = blk >= 15
                ps, pb = ps_get()
                c0 = 0 if need_k else 256
                for k in range(8):
                    T(lambda e, ps=ps, k=k, t0=t0, n=n, c0=c0: e.matmul(
                        ps[0:n, c0:512], lhsT=hT[:, k, t0:t0 + n], rhs=wkv[:, k, c0:512], start=(k == 0), stop=(k == 7)),
                      r=[wkvb, bh[ti]], w=[pb])
                if need_k:
                    for which, c1, dst in ((0, 0, (kp_out[j] if blk == 15 else ks_out[j])), (1, 256, (vp_out[j] if blk == 15 else vs_out[j]))):
                        kv_, kvb_ = kvtm_r.get()
                        V(lambda e, ps=ps, kv_=kv_, c1=c1, n=n: e.tensor_tensor(out=kv_[0:n, :], in0=ps[0:n, c1:c1 + 256], in1=bkv[0:n, c1:c1 + 256], op=ALU.add),
                          r=[pb, bkv_b], w=[kvb_])
                        P.dma("sync", dst[0:n, :], kv_[0:n, :], reads=[kvb_])
                        if which == 1:
                            V(lambda e, kv_=kv_, blk=blk, n=n: e.tensor_copy(out=vtok[0:n, blk, :, 0:64], in_=kv_[0:n, :].rearrange("p (k d) -> p k d", d=64)),
                              r=[kvb_, bvt], w=[bvtok[blk]])
                else:
                    V(lambda e, ps=ps, blk=blk, n=n: e.tensor_tensor(out=vtok[0:n, blk, :, 0:64], in0=ps[0:n, 256:512].rearrange("p (k d) -> p k d", d=64),
                                                                    in1=bkv[0:n, 256:512].rearrange("p (k d) -> p k d", d=64), op=ALU.add),
                      r=[pb, bkv_b, bvt], w=[bvtok[blk]])

            for qb in range(16):
                ti = qb // 4
                q0 = qb * 128
                ot, otb = otok_r.get()
                for kv in range(4):
                    half, kc = kv % 2, kv // 2
                    pso, psob = ps_get()
                    for g in range(4):
                        h = kv * 4 + g
                        i = kc * 4 + g
                        ps, pb = ps_get()
                        rb = [bkT[ti], bqT[ti]] + ([bkT[(qb - 1) // 4]] if qb > 0 else [])
                        T(lambda e, ps=ps, half=half, kc=kc, i=i, q0=q0: e.matmul(
                            ps[:, 128:256], lhsT=kT[half * 64:(half + 1) * 64, kc, q0:q0 + 128],
                            rhs=qT[half * 64:(half + 1) * 64, i, q0:q0 + 128], start=True, stop=True), r=rb, w=[pb])
                        if qb > 0:
                            T(lambda e, ps=ps, half=half, kc=kc, i=i, q0=q0: e.matmul(
                                ps[:, 0:128], lhsT=kT[half * 64:(half + 1) * 64, kc, q0 - 128:q0],
                                rhs=qT[half * 64:(half + 1) * 64, i, q0:q0 + 128], start=True, stop=True), r=rb, w=[pb])
                        lo = 0 if qb > 0 else 128
                        e_, eb_ = e_r.get()
                        V(lambda e, ps=ps, e_=e_, lo=lo: e.scalar_tensor_tensor(out=e_[:, lo:256], in0=ps[:, lo:256], scalar=SCALE, in1=maskT[:, lo:256],
                                                                             op0=ALU.mult, op1=ALU.add), r=[pb, bmask], w=[eb_])
                        V(lambda e, e_=e_, lo=lo, h=h: e.scalar_tensor_tensor(out=e_[:, lo:256], in0=relT[:, lo:256], scalar=-SLOPES[h], in1=e_[:, lo:256],
                                                                            op0=ALU.mult, op1=ALU.add), r=[eb_, bmask], w=[eb_])
                        p_, pb_ = p_r.get()
                        A(lambda e, e_=e_, p_=p_, lo=lo: e.activation(out=p_[:, lo:256], in_=e_[:, lo:256], func=AF.Exp), r=[eb_], w=[pb_])
                        T(lambda e, pso=pso, p_=p_, qb=qb, kv=kv, g=g: e.matmul(
                            pso[:, g * 65:(g + 1) * 65], lhsT=p_[:, 128:256], rhs=vtok[:, qb, kv, :], start=True, stop=(qb == 0)),
                          r=[pb_, bvtok[qb]], w=[psob])
                        if qb > 0:
                            T(lambda e, pso=pso, p_=p_, qb=qb, kv=kv, g=g: e.matmul(
                                pso[:, g * 65:(g + 1) * 65], lhsT=p_[:, 0:128], rhs=vtok[:, qb - 1, kv, :], start=False, stop=True),
                              r=[pb_, bvtok[qb - 1]], w=[psob])
                    dn, dnb = den_r.get()
                    pso3 = pso[:, 0:260].rearrange("p (g d) -> p g d", d=65)
                    V(lambda e, dn=dn, pso3=pso3, kv=kv: e.tensor_tensor(out=dn[:, :], in0=pso3[:, :, 64], in1=sinkEB[:, kv * 4:kv * 4 + 4], op=ALU.add),
                      r=[psob, bsink], w=[dnb])
                    V(lambda e, dn=dn: e.reciprocal(out=dn[:, :], in_=dn[:, :]), r=[dnb], w=[dnb])
                    V(lambda e, ot=ot, pso3=pso3, dn=dn, kc=kc, half=half: e.tensor_tensor(
                        out=ot[:, kc * 4:kc * 4 + 4, half * 64:(half + 1) * 64], in0=pso3[:, :, 0:64],
                        in1=dn[:, :].to_broadcast([128, 4, 64]) if False else bass.AP(dn.tensor, dn.offset, [list(dn.ap[0]), [1, 4], [0, 64]]),
                        op=ALU.mult), r=[psob, dnb], w=[otb])
                for hf in range(2):
                    ps, pb = ps_get()
                    for cc in range(4):
                        i = hf * 4 + cc
                        T(lambda e, ps=ps, ot=ot, i=i, cc=cc: e.transpose(out=ps[:, cc * 128:(cc + 1) * 128], in_=ot[:, i, :], identity=ident[:, :]),
                          r=[otb, bident], w=[pb])
                    A(lambda e, ps=ps, hf=hf, q0=q0: e.copy(out=hT[:, hf * 4:hf * 4 + 4, q0:q0 + 128], in_=ps[:, :].rearrange("p (c t) -> p c t", t=128)),
                      r=[pb], w=[bh[ti]])

            for b in range(NB):
                tk0 = TP + 4 * b
                ckst, ckb = ckst_r.get()
                cvst, cvb = cvst_r.get()
                P.dma("sync", ckst[:, :], ck_in[j, b], writes=[ckb])
                P.dma("sync", cvst[:, :], cv_in[j, b], writes=[cvb])
                vc, vcb = vc_r.get()
                V(lambda e, vc=vc: e.memset(vc[:, :, 64:65], 1.0), w=[vcb])
                V(lambda e, vc=vc, cvst=cvst: e.tensor_copy(out=vc[:, :, 0:64], in_=cvst[:, :].rearrange("p (k d) -> p k d", d=64)), r=[cvb], w=[vcb])
                ps, pb = ps_get()
                for c in range(2):
                    T(lambda e, ps=ps, ckst=ckst, c=c: e.transpose(out=ps[:, c * 128:(c + 1) * 128], in_=ckst[:, c * 128:(c + 1) * 128], identity=ident[:, :]),
                      r=[ckb, bident], w=[pb])
                kct, kctb = kct_r.get()
                A(lambda e, ps=ps, kct=kct: e.copy(out=kct[:, :, :], in_=ps[:, 0:256].rearrange("p (c t) -> p c t", t=128)), r=[pb], w=[kctb])
                psc, pscb = ps_get()
                for kv in range(4):
                    half, kc = kv % 2, kv // 2
                    for g in range(4):
                        T(lambda e, psc=psc, kct=kct, half=half, kc=kc, kv=kv, g=g, tk0=tk0: e.matmul(
                            psc[:, kv * 16 + g * 4:kv * 16 + g * 4 + 4], lhsT=kct[half * 64:(half + 1) * 64, kc, :],
                            rhs=qT[half * 64:(half + 1) * 64, kc * 4 + g, tk0:tk0 + 4], start=True, stop=True),
                          r=[kctb, bqT[4]], w=[pscb])
                        T(lambda e, psc=psc, half=half, kc=kc, kv=kv, g=g, tk0=tk0: e.matmul(
                            psc[0:64, 64 + kv * 16 + g * 4:64 + kv * 16 + g * 4 + 4], lhsT=kT[half * 64:(half + 1) * 64, kc, TP:TP + 64],
                            rhs=qT[half * 64:(half + 1) * 64, kc * 4 + g, tk0:tk0 + 4], start=True, stop=True),
                          r=[bkT[4], bqT[4]], w=[pscb])
                es, esb = es_r.get()
                V(lambda e, es=es, psc=psc: e.scalar_tensor_tensor(out=es[:, :], in0=psc[:, 0:64], scalar=SCALE, in1=biasC[:, :, :].rearrange("p h t -> p (h t)"),
                                                                 op0=ALU.mult, op1=ALU.add), r=[pscb, bmask], w=[esb])
                pc_, pcb = psb_r.get()
                A(lambda e, es=es, pc_=pc_: e.activation(out=pc_[:, :], in_=es[:, :], func=AF.Exp), r=[esb], w=[pcb])
                esn, esnb = esn_r.get()
                V(lambda e, esn=esn, psc=psc, b=b: e.scalar_tensor_tensor(out=esn[0:64, :], in0=psc[0:64, 64:128], scalar=SCALE,
                                                                        in1=biasN[0:64, b, :, :].rearrange("p h t -> p (h t)"),
                                                                        op0=ALU.mult, op1=ALU.add), r=[pscb, bmask], w=[esnb])
                pn, pnb = pn_r.get()
                A(lambda e, esn=esn, pn=pn: e.activation(out=pn[0:64, :], in_=esn[0:64, :], func=AF.Exp), r=[esnb], w=[pnb])
                pso, psob = ps_get()
                for kv in range(4):
                    T(lambda e, pso=pso, pc_=pc_, vc=vc, kv=kv: e.matmul(pso[0:16, kv * 65:(kv + 1) * 65], lhsT=pc_[:, kv * 16:(kv + 1) * 16], rhs=vc[:, kv, :],
                                                                       start=True, stop=False), r=[pcb, vcb], w=[psob])
                    T(lambda e, pso=pso, pn=pn, kv=kv: e.matmul(pso[0:16, kv * 65:(kv + 1) * 65], lhsT=pn[0:64, kv * 16:(kv + 1) * 16], rhs=vtok[0:64, 16, kv, :],
                                                              start=False, stop=True), r=[pnb, bvtok[16]], w=[psob])
                dn, dnb = den_r.get()
                pso3 = pso[0:16, 0:260].rearrange("p (k d) -> p k d", d=65)
                V(lambda e, dn=dn, pso3=pso3: e.tensor_tensor(out=dn[0:16, :], in0=pso3[:, :, 64], in1=sink16[0:16, :], op=ALU.add), r=[psob, bsink], w=[dnb])
                V(lambda e, dn=dn: e.reciprocal(out=dn[0:16, :], in_=dn[0:16, :]), r=[dnb], w=[dnb])
                os_, osb = os_r.get()
                V(lambda e, os_=os_, pso3=pso3, dn=dn: e.tensor_tensor(out=os_[0:16, :, :], in0=pso3[:, :, 0:64],
                                                                     in1=bass.AP(dn.tensor, dn.offset, [[dn.ap[0][0], 16], [1, 4], [0, 64]]), op=ALU.mult),
                  r=[psob, dnb], w=[osb])
                pst, pstb = ps_get()
                for kc in range(2):
                    T(lambda e, pst=pst, os_=os_, kc=kc: e.transpose(out=pst[:, kc * 16:(kc + 1) * 16],
                                                                    in_=os_[0:16, 2 * kc:2 * kc + 2, :].rearrange("p k d -> p (k d)"), identity=ident[0:16, 0:16]),
                      r=[osb, bident], w=[pstb])
                A(lambda e, pst=pst, tk0=tk0: e.copy(out=hT[:, :, tk0:tk0 + 4], in_=pst[:, 0:32].rearrange("p (i t) -> p i t", t=4)), r=[pstb], w=[bh[4]])

            wo = swa_w_o[j]
            for half in range(2):
                t_, b_ = wrot.get()
                for i in range(8):
                    ha, hb = pair_heads(i)
                    P.dma("gpsimd", t_[0:64, i, 0:512], wo[ha * 64:(ha + 1) * 64, half * 512:(half + 1) * 512], writes=[b_], reads=[b_])
                    P.dma("gpsimd", t_[64:128, i, 0:512], wo[hb * 64:(hb + 1) * 64, half * 512:(half + 1) * 512], writes=[b_], reads=[b_])
                for ti, (t0, n) in enumerate(TILES):
                    for nn in range(4):
                        c = half * 4 + nn
                        ps, pb = ps_get()
                        for i in range(8):
                            T(lambda e, ps=ps, t_=t_, i=i, nn=nn, t0=t0, n=n: e.matmul(
                                ps[:, 0:n], lhsT=t_[:, i, nn * 128:(nn + 1) * 128], rhs=hT[:, i, t0:t0 + n], start=(i == 0), stop=(i == 7)),
                              r=[b_, bh[ti]], w=[pb])
                        V(lambda e, ps=ps, c=c, t0=t0, n=n: e.scalar_tensor_tensor(
                            out=xT[:, c, t0:t0 + n], in0=ps[:, 0:n], scalar=bo[:, c:c + 1], in1=xT[:, c, t0:t0 + n], op0=ALU.add, op1=ALU.add),
                          r=[pb, bo_b, bx[ti][c]], w=[bx[ti][c]])


        def deltanet(j):
            w_in = dn_w_in[j]
            I32 = mybir.dt.int32
            cw = carve("cw", [128, 32, 4], F32)
            dtb = carve("dtb", [128, 16], F32)
            negA = carve("negA", [128, 16], F32)
            nwB = carve("nwB", [128, 128], F32)
            one1 = carve("one1", [128, 1], F32)
            eps2 = carve("eps2", [128, 1], F32)
            ones32 = carve("ones32", [128, 128], F32)
            Ltri = [carve(f"Ltri{i}", [128, 128], F32) for i in range(2)]
            Lsel = [carve(f"Lsel{i}", [128, 128], F32) for i in range(2)]
            mk1 = [carve(f"mk1{i}", [128, 128], F32) for i in range(2)]
            mk2 = [carve(f"mk2{i}", [128, 128], F32) for i in range(2)]
            bmask16 = carve("bmask16", [128, 16], F32)
            tmpi = carve("tmpi", [128, 128], I32)
            beta_a = carve("beta_a", [128, 17, 16], F32)
            G_a = carve("G_a", [128, 17, 16], F32)
            bexpG_a = carve("bexpG_a", [128, 17, 16], F32)
            kdecs_a = carve("kdecs_a", [128, 17, 16], F32)
            bset = Buf("dnset")
            btok = Buf("dntok")
            for k_ in range(4):
                P.dma("sync", cw[:, :, k_], dn_conv_w[j, k_].rearrange("(c p) -> p c", p=128), writes=[bset], reads=[bset])
            P.dma("sync", dtb[:, :], dn_dt_bias[j:j + 1, :].partition_broadcast(128).rearrange("p o n -> p (o n)"), writes=[bset], reads=[bset])
            P.dma("sync", negA[:, :], dn_a_log[j:j + 1, :].partition_broadcast(128).rearrange("p o n -> p (o n)"), writes=[bset], reads=[bset])
            P.dma("sync", nwB[:, :], dn_norm_w[j:j + 1, :].partition_broadcast(128).rearrange("p o n -> p (o n)"), writes=[bset], reads=[bset])
            P.fence()
            A(lambda e: e.activation(out=negA[:, :], in_=negA[:, :], func=AF.Exp), r=[bset], w=[bset])
            V(lambda e: e.tensor_scalar(out=negA[:, :], in0=negA[:, :], scalar1=-1.0, scalar2=None, op0=ALU.mult), r=[bset], w=[bset])
            V(lambda e: e.memset(one1[:, :], 1.0), r=[bset], w=[bset])
            V(lambda e: e.memset(eps2[:, :], 1e-6), r=[bset], w=[bset])
            V(lambda e: e.memset(ones32[:, :], 1.0), r=[bset], w=[bset])
            for i, C in enumerate((64, 4)):
                sh = 6 if C == 64 else 2
                G(lambda e, i=i: e.memset(Ltri[i][:, :], 0.0), r=[bset], w=[bset])
                G(lambda e, i=i: e.memset(Lsel[i][:, :], 0.0), r=[bset], w=[bset])
                G(lambda e, i=i: e.memset(mk1[i][:, :], NEG), r=[bset], w=[bset])
                G(lambda e, i=i: e.memset(mk2[i][:, :], NEG), r=[bset], w=[bset])
            P.fence()
            mark_tmp = arena_state["off"]
            pidx = carve("pidx", [128, 128], F32)
            cidx = carve("cidx", [128, 128], F32)
            G(lambda e: e.iota(tmpi[:, :], pattern=[[0, 128]], base=0, channel_multiplier=1), r=[bset], w=[bset])
            G(lambda e: e.tensor_copy(out=pidx[:, :], in_=tmpi[:, :]), r=[bset], w=[bset])
            G(lambda e: e.iota(tmpi[:, :], pattern=[[1, 128]], base=0, channel_multiplier=0), r=[bset], w=[bset])
            G(lambda e: e.tensor_copy(out=cidx[:, :], in_=tmpi[:, :]), r=[bset], w=[bset])
            pch = carve("pch", [128, 128], F32)
            cch = carve("cch", [128, 128], F32)
            same = carve("same", [128, 128], F32)
            tmpf = carve("tmpf", [128, 128], F32)
            for i, C in enumerate((64, 4)):
                sh = 6 if C == 64 else 2
                for src, dst in ((pidx, pch), (cidx, cch)):
                    V(lambda e, src=src: e.tensor_copy(out=tmpi[:, :], in_=src[:, :]), r=[bset], w=[bset])
                    V(lambda e, sh=sh: e.tensor_scalar(out=tmpi[:, :], in0=tmpi[:, :], scalar1=sh, scalar2=None, op0=ALU.arith_shift_right), r=[bset], w=[bset])
                    V(lambda e, dst=dst: e.tensor_copy(out=dst[:, :], in_=tmpi[:, :]), r=[bset], w=[bset])
                V(lambda e: e.tensor_tensor(out=same[:, :], in0=pch[:, :], in1=cch[:, :], op=ALU.is_equal), r=[bset], w=[bset])
                V(lambda e: e.tensor_tensor(out=tmpf[:, :], in0=pidx[:, :], in1=cidx[:, :], op=ALU.is_le), r=[bset], w=[bset])
                V(lambda e, i=i: e.tensor_tensor(out=Ltri[i][:, :], in0=tmpf[:, :], in1=same[:, :], op=ALU.mult), r=[bset], w=[bset])
                V(lambda e, i=i: e.tensor_scalar(out=mk1[i][:, :], in0=Ltri[i][:, :], scalar1=-1.0, scalar2=-NEG, op0=ALU.add, op1=ALU.mult), r=[bset], w=[bset])
                V(lambda e: e.tensor_tensor(out=tmpf[:, :], in0=pidx[:, :], in1=cidx[:, :], op=ALU.is_gt), r=[bset], w=[bset])
                V(lambda e: e.tensor_tensor(out=tmpf[:, :], in0=tmpf[:, :], in1=same[:, :], op=ALU.mult), r=[bset], w=[bset])
                V(lambda e, i=i: e.tensor_scalar(out=mk2[i][:, :], in0=tmpf[:, :], scalar1=-1.0, scalar2=-NEG, op0=ALU.add, op1=ALU.mult), r=[bset], w=[bset])
                V(lambda e, C=C: e.tensor_scalar(out=tmpf[:, :], in0=cch[:, :], scalar1=float(C), scalar2=float(C - 1), op0=ALU.mult, op1=ALU.add), r=[bset], w=[bset])
                V(lambda e, i=i: e.tensor_tensor(out=Lsel[i][:, :], in0=tmpf[:, :], in1=pidx[:, :], op=ALU.is_equal), r=[bset], w=[bset])
            V(lambda e: e.tensor_tensor(out=bmask16[:, :], in0=pch[:, 0:16], in1=cidx[:, 0:16], op=ALU.is_equal), r=[bset], w=[bset])
            P.fence()
            arena_state["off"] = mark_tmp

            wba, wbab = wload(w_in, 0, 8, 6144, 32)
            for blk in range(17):
                t0, n = (blk * 128, 128) if blk < 16 else (TP, TS)
                mi = 0 if blk < 16 else 1
                ti = min(blk // 4, 4)
                ps, pb = ps_get()
                for k in range(8):
                    T(lambda e, ps=ps, k=k, t0=t0, n=n: e.matmul(ps[0:n, 0:32], lhsT=hT[:, k, t0:t0 + n], rhs=wba[:, k, 0:32], start=(k == 0), stop=(k == 7)),
                      r=[wbab, bh[ti]], w=[pb])
                A(lambda e, ps=ps, blk=blk, n=n: e.activation(out=beta_a[0:n, blk, :], in_=ps[0:n, 0:16], func=AF.Sigmoid), r=[pb], w=[btok])
                V(lambda e, ps=ps, blk=blk, n=n: e.tensor_tensor(out=G_a[0:n, blk, :], in0=ps[0:n, 16:32], in1=dtb[0:n, :], op=ALU.add), r=[pb, bset, btok], w=[btok])
                A(lambda e, blk=blk, n=n: e.activation(out=G_a[0:n, blk, :], in_=G_a[0:n, blk, :], func=AF.Exp), r=[btok], w=[btok])
                A(lambda e, blk=blk, n=n: e.activation(out=G_a[0:n, blk, :], in_=G_a[0:n, blk, :], func=AF.Ln, bias=one1[0:n, :], scale=1.0), r=[btok, bset], w=[btok])
                V(lambda e, blk=blk, n=n: e.tensor_tensor(out=kdecs_a[0:n, blk, :], in0=G_a[0:n, blk, :], in1=negA[0:n, :], op=ALU.mult), r=[btok, bset], w=[btok])
                ps2, pb2 = ps_get()
                T(lambda e, ps2=ps2, blk=blk, n=n, mi=mi: e.matmul(ps2[0:n, 0:16], lhsT=Ltri[mi][0:n, 0:n], rhs=kdecs_a[0:n, blk, :], start=True, stop=True), r=[btok, bset], w=[pb2])
                V(lambda e, ps2=ps2, blk=blk, n=n: e.tensor_copy(out=G_a[0:n, blk, :], in_=ps2[0:n, 0:16]), r=[pb2, btok], w=[btok])
                ps3, pb3 = ps_get()
                T(lambda e, ps3=ps3, blk=blk, n=n, mi=mi: e.matmul(ps3[0:n, 0:16], lhsT=Lsel[mi][0:n, 0:n], rhs=G_a[0:n, blk, :], start=True, stop=True), r=[btok, bset], w=[pb3])
                V(lambda e, ps3=ps3, blk=blk, n=n: e.tensor_tensor(out=kdecs_a[0:n, blk, :], in0=ps3[0:n, 0:16], in1=G_a[0:n, blk, :], op=ALU.subtract), r=[pb3, btok], w=[btok])
                A(lambda e, blk=blk, n=n: e.activation(out=kdecs_a[0:n, blk, :], in_=kdecs_a[0:n, blk, :], func=AF.Exp), r=[btok], w=[btok])
                A(lambda e, blk=blk, n=n: e.activation(out=bexpG_a[0:n, blk, :], in_=G_a[0:n, blk, :], func=AF.Exp), r=[btok], w=[btok])
                V(lambda e, blk=blk, n=n: e.tensor_tensor(out=bexpG_a[0:n, blk, :], in0=bexpG_a[0:n, blk, :], in1=beta_a[0:n, blk, :], op=ALU.mult), r=[btok], w=[btok])
            P.fence()

            def OP(eng, name, r, w, **kw):
                return P.op(eng, lambda e, kw=kw, name=name: getattr(e, name)(**kw), r, w)

            def ap4(t, p0, np_, off_ap, dims):
                ta = t if hasattr(t, "tensor") else t[:]
                return bass.AP(ta.tensor, off_ap.offset, [[ta.ap[0][0], np_]] + [list(d) for d in dims])

            TW = 256
            blk_pc = carve("pc", [128, 4 * (TW + 4)], F32)
            pc = blk_pc.rearrange("p (c t) -> p c t", t=TW + 4)
            pcs = carve("pcs", [128, 4, 16, 7], F32)
            blk_cv = carve("cv", [128, 4 * TW], F32)
            cv = blk_cv.rearrange("p (c t) -> p c t", t=TW)
            sqb = carve("sqb", [128, TW], BF16)
            rn = carve("rn", [128, TW], F32)
            kTf = carve("kTf", [128, TW], F32)
            tailtm = carve("tailtm", [128, 512], F32)
            scst = carve("scst", [128, 512], F32)
            qTn = [carve("qTn", [128, TW], BF16)] * 2
            kTn = [carve("kTn", [128, TW], BF16)] * 2
            ktok = [carve("ktok", [128, 2, 128], F32)] * 2
            vtok = [carve("vtokd", [128, 2, 2, 128], F32)] * 2
            zs = [carve(f"zs{i}", [128, 2, 256], F32) for i in range(2)]
            bA = [Buf("dnA")] * 2
            bZ = [Buf(f"dnZ{i}") for i in range(2)]
            dG = carve("dG", [128, 4, 128], F32)
            dd = carve("dd", [128, 4, 128], F32)
            Mp = [carve(f"Mp{i}", [128, 4, 128], BF16) for i in range(2)]
            Np = [carve(f"Np{i}", [128, 4, 128], BF16) for i in range(2)]
            TT = carve("TT", [128, 4, 128], BF16)
            vb = carve("vb", [128, 4, 128], BF16)
            kbg = carve("kbg", [128, 4, 128], BF16)
            bBi = Buf("dnBi")
            bdG = Buf("dG"); bdd = Buf("dd"); bTT = bBi; bvb = Buf("vb"); bkbg = Buf("kbg")
            bMp = [bBi, bBi]
            bNp = [bBi, bBi]
            eGB = [carve(f"eGB{i}", [128, 4, 128], F32) for i in range(2)]
            u32 = [carve(f"u32{i}", [128, 4, 128], F32) for i in range(2)]
            wTb = [carve(f"wTb{i}", [128, 4, 128], BF16) for i in range(2)]
            qgT = [carve(f"qgT{i}", [128, 4, 128], BF16) for i in range(2)]
            aqkT = [carve(f"aqkT{i}", [128, 4, 128], BF16) for i in range(2)]
            kdec = [carve(f"kdec{i}", [128, 4, 128], BF16) for i in range(2)]
            bB = [Buf(f"dnB{i}") for i in range(2)]
            S32 = carve("S32", [128, 2, 128], F32)
            Sbf = carve("Sbf", [128, 2, 128], BF16)
            un_r = Rot(carve, "unew", [128, 2, 128], BF16, 2)
            otok = [carve("otokd", [128, 2, 2, 128], F32)] * 2
            bO = [Buf("dnO")] * 2
            bS = Buf("S")
            on = carve("on", [128, 4, 128], F32)
            ssq = carve("ssq", [128, 4], F32)
            oTb = carve("oTb", [128, 2, TW], BF16)
            bD = Buf("dnD")
            s0_r = Rot(carve, "s0", [128, 128], F32, 3)
            unb_r = Rot(carve, "unb", [128, 128], F32, 2)
            tmp_r = Rot(carve, "tmpu", [128, 128], F32, 1)
            oacc = carve("oacc", [128, 128], F32)
            boacc = Buf("oacc")
            kdec32 = carve("kdec32", [128, 128], F32)
            aq32 = carve("aq32", [128, 64], F32)
            wT32 = carve("wT32", [128, 2, 64], F32)
            qgT32 = carve("qgT32", [128, 64], F32)
            sn_r = Rot(carve, "sn", [128, 128], F32, 2)
            bSm = Buf("dnSm")
            bAi = Buf("dnAi")
            bAc = [Buf(f"dnAc{c}") for c in range(4)]
            bTail = Buf("dnTail")
            DN_TILES = [(t0, TW) for t0 in range(0, TP, TW)] + [(TP, TS)]
            tcount = 0

            for hg in range(8):
                wA, wAb = wrot.get()
                P.dma("gpsimd", wA[:, :, 0:128], w_in[:, hg * 128:(hg + 1) * 128].rearrange("(c p) n -> p c n", p=128), writes=[wAb], reads=[wAb])
                P.dma("gpsimd", wA[:, :, 128:256], w_in[:, 1024 + hg * 128:1024 + (hg + 1) * 128].rearrange("(c p) n -> p c n", p=128), writes=[wAb], reads=[wAb])
                P.dma("gpsimd", wA[:, :, 256:512], w_in[:, 2048 + hg * 256:2048 + (hg + 1) * 256].rearrange("(c p) n -> p c n", p=128), writes=[wAb], reads=[wAb])
                wZ, wZb = wload(w_in, 0, 8, 4096 + hg * 256, 256)
                wO, wOb = wrot.get()
                for half in range(2):
                    P.dma("gpsimd", wO[:, half * 2:half * 2 + 2, 0:512],
                          dn_w_out[j, hg * 256:(hg + 1) * 256, half * 512:(half + 1) * 512].rearrange("(c p) n -> p c n", p=128), writes=[wOb], reads=[wOb])
                chan = [hg, 8 + hg, 16 + 2 * hg, 17 + 2 * hg]
                for ch in range(4):
                    OP("vector", "memset", [bAc[ch]], [bAc[ch]], ap=pc[:, ch, 0:3], constant=0.0)
                OP("vector", "memset", [bS], [bS], ap=S32[:, :, :], constant=0.0)
                OP("vector", "memset", [bS], [bS], ap=Sbf[:, :, :], constant=0.0)
                for (t0, n) in DN_TILES:
                    samp = t0 >= TP
                    ti = min(t0 // 512, 4)
                    mi = 1 if samp else 0
                    nb, bs = (1, 64) if samp else (2, 128)
                    NP_ = nb * 2
                    par = tcount % 2
                    tcount += 1
                    gblk0 = (t0 // 128) if not samp else 16
                    A_ = bA[par]
                    B_ = bB[par]
                    O_ = bO[par]
                    if samp:
                        for q_, (c0, wd_) in enumerate(((hg * 128, 128), (1024 + hg * 128, 128), (2048 + hg * 256, 256))):
                            so = (0, 128, 256)[q_]
                            P.dma("sync", scst[0:48, so:so + wd_], sconv_in[j, :, c0:c0 + wd_], reads=[bAi], writes=[bAi])
                        ps, pb = ps_get()
                        for ch in range(4):
                            OP("tensor", "transpose", [bAi, bident], [pb], out=ps[:, ch * 48:(ch + 1) * 48], in_=scst[0:48, ch * 128:(ch + 1) * 128], identity=ident[0:48, 0:48])
                        OP("vector", "tensor_copy", [pb] + bAc, bAc, out=pcs[:, :, :, 0:3], in_=ps[:, 0:192].rearrange("p (c b r) -> p c b r", b=16, r=3))
                    for ch in range(4):
                        ps, pb = ps_get()
                        for k in range(8):
                            OP("tensor", "matmul", [wAb, bh[ti]], [pb], out=ps[:, 0:n], lhsT=wA[:, k, ch * 128:(ch + 1) * 128], rhs=hT[:, k, t0:t0 + n],
                               start=(k == 0), stop=(k == 7))
                        if samp:
                            OP("scalar", "copy", [pb, bAc[ch]], [bAc[ch]], out=pcs[:, ch, :, 3:7], in_=ps[:, 0:64].rearrange("p (b t) -> p b t", t=4))
                        else:
                            OP("scalar", "copy", [pb, bAc[ch]], [bAc[ch]], out=pc[:, ch, 3:3 + TW], in_=ps[:, 0:TW])
                    if t0 == TP - TW or samp:
                        r0, nr = (TP - 32, 32) if not samp else (TP, 64)
                        ps, pb = ps_get()
                        for k in range(8):
                            OP("tensor", "matmul", [wAb, bh[ti]], [pb], out=ps[0:nr, :], lhsT=hT[:, k, r0:r0 + nr], rhs=wA[:, k, :], start=(k == 0), stop=(k == 7))
                        OP("vector", "tensor_copy", [pb, bTail], [bTail], out=tailtm[0:nr, :], in_=ps[0:nr, :])
                        for q_, (c0, wd_) in enumerate(((hg * 128, 128), (1024 + hg * 128, 128), (2048 + hg * 256, 256))):
                            so = (0, 128, 256)[q_]
                            if not samp:
                                P.dma("sync", convp_out[j, :, c0:c0 + wd_], tailtm[29:32, so:so + wd_], reads=[bTail])
                            else:
                                for t_ in range(1, 4):
                                    src = bass.AP(tailtm.tensor, tailtm[t_:t_ + 1, so:so + 1].offset, [[tailtm.ap[0][0] * 4, 16], [1, wd_]])
                                    P.dma("sync", convs_out[j, :, t_ - 1, c0:c0 + wd_], src, reads=[bTail])
                    for ch in range(4):
                        cc = chan[ch]
                        if samp:
                            srcs = [pcs[:, ch, :, k:k + 4] for k in range(4)]
                            dst = cv[:, ch, 0:64].rearrange("p (b t) -> p b t", t=4)
                        else:
                            srcs = [pc[:, ch, k:k + TW] for k in range(4)]
                            dst = cv[:, ch, 0:TW]
                        OP("vector", "tensor_scalar", [bAc[ch], bset], [bAc[ch]], out=dst, in0=srcs[0], scalar1=cw[:, cc, 0:1], scalar2=None, op0=ALU.mult)
                        for k in range(1, 4):
                            OP("vector", "scalar_tensor_tensor", [bAc[ch], bset], [bAc[ch]], out=dst, in0=srcs[k], scalar=cw[:, cc, k:k + 1], in1=dst, op0=ALU.mult, op1=ALU.add)
                        OP("scalar", "activation", [bAc[ch]], [bAc[ch]], out=cv[:, ch, 0:n], in_=cv[:, ch, 0:n], func=AF.Silu)
                        if not samp:
                            OP("scalar", "copy", [bAc[ch]], [bAc[ch]], out=pc[:, ch, 0:3], in_=pc[:, ch, TW:TW + 3])
                    for ch, dst, mul in ((0, qTn[par], 128.0 ** -0.5), (1, kTn[par], 1.0)):
                        OP("scalar", "activation", [bAc[ch], bAi], [bAi], out=sqb[:, 0:n], in_=cv[:, ch, 0:n], func=AF.Square)
                        ps, pb = ps_get()
                        OP("tensor", "matmul", [bAi, bconst], [pb], out=ps[:, 0:n], lhsT=ones_bf[:, :], rhs=sqb[:, 0:n], start=True, stop=True)
                        OP("scalar", "activation", [pb, bset, bAi], [bAi], out=rn[:, 0:n], in_=ps[:, 0:n], func=AF.Ln, bias=eps2[:, :], scale=1.0)
                        OP("scalar", "activation", [bAi], [bAi], out=rn[:, 0:n], in_=rn[:, 0:n], func=AF.Exp, scale=-0.5)
                        OP("vector", "scalar_tensor_tensor", [bAi, bAc[ch], A_], [A_], out=dst[:, 0:n], in0=cv[:, ch, 0:n], scalar=mul, in1=rn[:, 0:n], op0=ALU.mult, op1=ALU.mult)
                        if ch == 1:
                            OP("vector", "tensor_tensor", [bAi, bAc[1]], [bAi], out=kTf[:, 0:n], in0=cv[:, 1, 0:n], in1=rn[:, 0:n], op=ALU.mult)
                    for blk in range(nb):
                        ps, pb = ps_get()
                        OP("tensor", "transpose", [bAi, bident], [pb], out=ps[0:bs, 0:128], in_=kTf[:, blk * bs:(blk + 1) * bs], identity=ident[:, :])
                        for hl in range(2):
                            OP("tensor", "transpose", [bAc[2 + hl], bident], [pb], out=ps[0:bs, 128 + hl * 128:256 + hl * 128], in_=cv[:, 2 + hl, blk * bs:(blk + 1) * bs], identity=ident[:, :])
                        OP("vector", "tensor_copy", [pb, A_], [A_], out=ktok[par][0:bs, blk, :], in_=ps[0:bs, 0:128])
                        OP("scalar", "copy", [pb, A_], [A_], out=vtok[par][0:bs, blk, :, :], in_=ps[0:bs, 128:384].rearrange("p (h d) -> p h d", d=128))
                        psz, pzb = ps_get()
                        for k in range(8):
                            OP("tensor", "matmul", [wZb, bh[ti]], [pzb], out=psz[0:bs, 0:256], lhsT=hT[:, k, t0 + blk * bs:t0 + (blk + 1) * bs], rhs=wZ[:, k, 0:256],
                               start=(k == 0), stop=(k == 7))
                        OP("scalar", "activation", [pzb, bZ[par]], [bZ[par]], out=zs[par][0:bs, blk, :], in_=psz[0:bs, 0:256], func=AF.Silu)

                    def P4(t):
                        return t[0:bs, 0:NP_, 0:bs].rearrange("p (b h) c -> p b h c", h=2)

                    def scal4(arr, last):
                        return ap4(arr, 0, bs, arr[0:1, gblk0, 2 * hg:2 * hg + 1], [[16, nb], [1, 2], [0, last]])

                    def bc_tile(t, last):
                        return ap4(t, 0, bs, t[0:1, 0:1], [[0, nb], [0, 2], [1, last]])

                    psk, pkb = ps_get()
                    for blk in range(nb):
                        c0 = blk * bs
                        OP("tensor", "matmul", [A_], [pkb], out=psk[0:bs, blk * 128:blk * 128 + bs], lhsT=kTn[par][:, c0:c0 + bs], rhs=kTn[par][:, c0:c0 + bs], start=True, stop=True)
                        OP("tensor", "matmul", [A_], [pkb], out=psk[0:bs, 256 + blk * 128:256 + blk * 128 + bs], lhsT=kTn[par][:, c0:c0 + bs], rhs=qTn[par][:, c0:c0 + bs], start=True, stop=True)
                    OP("vector", "tensor_tensor", [bdG, btok, bident], [bdG], out=P4(dG), in0=bc_tile(ident, bs), in1=scal4(G_a, bs), op=ALU.mult)
                    psg, pgb = ps_get()
                    OP("tensor", "matmul", [bdG, bset], [pgb], out=psg[:, 0:NP_ * 128], lhsT=ones32[0:bs, :], rhs=dG[0:bs, 0:NP_, :].rearrange("p a c -> p (a c)"), start=True, stop=True)
                    psg4 = psg[0:bs, 0:NP_ * 128].rearrange("p (b h c) -> p b h c", h=2, c=128)[:, :, :, 0:bs]
                    OP("vector", "tensor_tensor", [pgb, btok, bdd], [bdd], out=P4(dd), in0=psg4, in1=scal4(G_a, bs), op=ALU.subtract)
                    OP("scalar", "activation", [pgb, B_], [B_], out=eGB[par][:, 0:NP_, :], in_=psg[:, 0:NP_ * 128].rearrange("p (a c) -> p a c", c=128), func=AF.Exp)
                    OP("vector", "tensor_tensor", [bdd, bdG, bset], [bdG], out=P4(dG), in0=P4(dd), in1=bc_tile(mk1[mi], bs), op=ALU.add)
                    OP("scalar", "activation", [bdG], [bdG], out=dG[0:bs, 0:NP_, 0:bs], in_=dG[0:bs, 0:NP_, 0:bs], func=AF.Exp)
                    OP("vector", "tensor_tensor", [bdd, bset], [bdd], out=P4(dd), in0=P4(dd), in1=bc_tile(mk2[mi], bs), op=ALU.subtract)
                    OP("scalar", "activation", [bdd], [bdd], out=dd[0:bs, 0:NP_, 0:bs], in_=dd[0:bs, 0:NP_, 0:bs], func=AF.Exp, scale=-1.0)
                    qk4 = ap4(psk, 0, bs, psk[0:1, 256:257], [[128, nb], [0, 2], [1, bs]])
                    kk4 = ap4(psk, 0, bs, psk[0:1, 0:1], [[128, nb], [0, 2], [1, bs]])
                    OP("vector", "tensor_tensor", [pkb, bdG, B_], [B_], out=P4(aqkT[par]), in0=qk4, in1=P4(dG), op=ALU.mult)
                    OP("vector", "tensor_tensor", [bdd, btok], [bdd], out=P4(dd), in0=P4(dd), in1=scal4(beta_a, bs), op=ALU.mult)
                    OP("vector", "tensor_tensor", [pkb, bdd], [bdd], out=P4(dd), in0=kk4, in1=P4(dd), op=ALU.mult)
                    OP("scalar", "mul", [bdd], [bdd], out=dd[0:bs, 0:NP_, 0:bs], in_=dd[0:bs, 0:NP_, 0:bs], mul=-1.0)
                    OP("scalar", "copy", [bdd, bMp[0]], [bMp[0]], out=Mp[0][0:bs, 0:NP_, 0:bs], in_=dd[0:bs, 0:NP_, 0:bs])
                    pst, ptb = ps_get()
                    for p in range(NP_):
                        OP("tensor", "transpose", [bdd, bident], [ptb], out=pst[0:bs, p * 128:p * 128 + bs], in_=dd[0:bs, p, 0:bs], identity=ident[0:bs, 0:bs])
                    pst3 = pst[0:bs, 0:NP_ * 128].rearrange("p (a c) -> p a c", c=128)[:, :, 0:bs]
                    OP("scalar", "copy", [ptb, bNp[0]], [bNp[0]], out=Np[0][0:bs, 0:NP_, 0:bs], in_=pst3)
                    OP("vector", "tensor_tensor", [ptb, bident, bTT], [bTT], out=TT[0:bs, 0:NP_, 0:bs], in0=pst3,
                       in1=ap4(ident, 0, bs, ident[0:1, 0:1], [[0, NP_], [1, bs]]), op=ALU.add)
                    cur = 0
                    for it in range(5):
                        nxt = 1 - cur
                        pa, pab = ps_get()
                        for p in range(NP_):
                            OP("tensor", "matmul", [bNp[cur], bMp[cur]], [pab], out=pa[0:bs, p * 128:p * 128 + bs], lhsT=Np[cur][0:bs, p, 0:bs], rhs=Mp[cur][0:bs, p, 0:bs], start=True, stop=True)
                        pa3 = pa[0:bs, 0:NP_ * 128].rearrange("p (a c) -> p a c", c=128)[:, :, 0:bs]
                        if it < 4:
                            pn_, pnb_ = ps_get()
                            for p in range(NP_):
                                OP("tensor", "matmul", [bNp[cur], bMp[cur]], [pnb_], out=pn_[0:bs, p * 128:p * 128 + bs], lhsT=Mp[cur][0:bs, p, 0:bs], rhs=Np[cur][0:bs, p, 0:bs], start=True, stop=True)
                            pn3 = pn_[0:bs, 0:NP_ * 128].rearrange("p (a c) -> p a c", c=128)[:, :, 0:bs]
                        OP("scalar", "copy", [pab, bMp[nxt]], [bMp[nxt]], out=Mp[nxt][0:bs, 0:NP_, 0:bs], in_=pa3)
                        if it < 4:
                            OP("scalar", "copy", [pnb_, bNp[nxt]], [bNp[nxt]], out=Np[nxt][0:bs, 0:NP_, 0:bs], in_=pn3)
                        pu, pub = ps_get()
                        for p in range(NP_):
                            OP("tensor", "matmul", [bMp[nxt], bTT], [pub], out=pu[0:bs, p * 128:p * 128 + bs], lhsT=Mp[nxt][0:bs, p, 0:bs], rhs=TT[0:bs, p, 0:bs], start=True, stop=True)
                        pu3 = pu[0:bs, 0:NP_ * 128].rearrange("p (a c) -> p a c", c=128)[:, :, 0:bs]
                        OP("vector", "tensor_tensor", [pub, bTT], [bTT], out=TT[0:bs, 0:NP_, 0:bs], in0=pu3, in1=TT[0:bs, 0:NP_, 0:bs], op=ALU.add)
                        cur = nxt
                    vt4 = vtok[par][0:bs, 0:nb, :, :]
                    kt4 = ap4(ktok[par], 0, bs, ktok[par][0:1, 0:1, 0:1], [[128, nb], [0, 2], [1, 128]])
                    V4 = lambda t: t[0:bs, 0:NP_, :].rearrange("p (b h) c -> p b h c", h=2)
                    OP("vector", "tensor_tensor", [A_, btok, bvb], [bvb], out=V4(vb), in0=vt4, in1=scal4(beta_a, 128), op=ALU.mult)
                    OP("vector", "tensor_tensor", [A_, btok, bkbg], [bkbg], out=V4(kbg), in0=kt4, in1=scal4(bexpG_a, 128), op=ALU.mult)
                    OP("vector", "tensor_tensor", [A_, btok, B_], [B_], out=V4(kdec[par]), in0=kt4, in1=scal4(kdecs_a, 128), op=ALU.mult)
                    pu, pub = ps_get()
                    pw, pwb = ps_get()
                    for p in range(NP_):
                        OP("tensor", "matmul", [bTT, bvb], [pub], out=pu[0:bs, p * 128:(p + 1) * 128], lhsT=TT[0:bs, p, 0:bs], rhs=vb[0:bs, p, :], start=True, stop=True)
                        OP("tensor", "matmul", [bTT, bkbg], [pwb], out=pw[:, p * 128:p * 128 + bs], lhsT=kbg[0:bs, p, :], rhs=TT[0:bs, p, 0:bs], start=True, stop=True)
                    OP("vector", "tensor_copy", [pub, B_], [B_], out=u32[par][0:bs, 0:NP_, :], in_=pu[0:bs, 0:NP_ * 128].rearrange("p (a c) -> p a c", c=128))
                    pw3 = pw[:, 0:NP_ * 128].rearrange("p (a c) -> p a c", c=128)[:, :, 0:bs]
                    qT4 = ap4(qTn[par], 0, 128, qTn[par][0:1, 0:1], [[bs, nb], [0, 2], [1, bs]])
                    eG4 = eGB[par][:, 0:NP_, 0:bs].rearrange("p (b h) c -> p b h c", h=2)
                    if not samp:
                        OP("scalar", "copy", [pwb, B_], [B_], out=wTb[par][:, 0:NP_, 0:bs], in_=pw3)
                        OP("vector", "tensor_tensor", [A_, B_], [B_], out=qgT[par][:, 0:NP_, 0:bs].rearrange("p (b h) c -> p b h c", h=2), in0=qT4, in1=eG4, op=ALU.mult)
                        for blk in range(nb):
                            for half in range(2):
                                hs = half * 64
                                un, unb = un_r.get()
                                p1, p1b = ps_get()
                                for hl in range(2):
                                    OP("tensor", "matmul", [B_, bS], [p1b], out=p1[hs:hs + 64, hl * 128:(hl + 1) * 128], lhsT=wTb[par][:, blk * 2 + hl, hs:hs + 64], rhs=Sbf[:, hl, :], start=True, stop=True)
                                OP("vector", "tensor_tensor", [p1b, B_, unb], [unb], out=un[hs:hs + 64, :, :],
                                   in0=u32[par][hs:hs + 64, blk * 2:blk * 2 + 2, :], in1=p1[hs:hs + 64, 0:256].rearrange("p (h d) -> p h d", d=128), op=ALU.subtract)
                                p2, p2b = ps_get()
                                for hl in range(2):
                                    OP("tensor", "matmul", [B_, bS], [p2b], out=p2[hs:hs + 64, hl * 128:(hl + 1) * 128], lhsT=qgT[par][:, blk * 2 + hl, hs:hs + 64], rhs=Sbf[:, hl, :], start=True, stop=False)
                                    OP("tensor", "matmul", [B_, unb], [p2b], out=p2[hs:hs + 64, hl * 128:(hl + 1) * 128], lhsT=aqkT[par][hs:hs + 64, blk * 2 + hl, hs:hs + 64], rhs=un[hs:hs + 64, hl, :], start=False, stop=True)
                                OP("scalar", "copy", [p2b, O_], [O_], out=otok[par][hs:hs + 64, blk, :, :], in_=p2[hs:hs + 64, 0:256].rearrange("p (h d) -> p h d", d=128))
                                p3, p3b = ps_get()
                                for hl in range(2):
                                    OP("tensor", "matmul", [B_, unb], [p3b], out=p3[:, hl * 128:(hl + 1) * 128], lhsT=kdec[par][hs:hs + 64, blk * 2 + hl, :], rhs=un[hs:hs + 64, hl, :], start=True, stop=True)
                                for hl in range(2):
                                    OP("vector", "scalar_tensor_tensor", [p3b, B_, bS], [bS], out=S32[:, hl, :], in0=S32[:, hl, :], scalar=eGB[par][:, blk * 2 + hl, hs + 63:hs + 64],
                                       in1=p3[:, hl * 128:(hl + 1) * 128], op0=ALU.mult, op1=ALU.add)
                                OP("scalar", "copy", [bS], [bS], out=Sbf[:, :, :], in_=S32[:, :, :])
                    else:
                        OP("scalar", "copy", [pwb, bSm], [bSm], out=wT32[:, :, :], in_=pw[:, 0:256].rearrange("p (h c) -> p h c", c=128)[:, :, 0:64])
                        for hl in range(2):
                            h = 2 * hg + hl
                            OP("vector", "tensor_tensor", [A_, B_, bSm], [bSm], out=qgT32[:, 0:64], in0=qTn[par][:, 0:64], in1=eGB[par][:, hl, 0:64], op=ALU.mult)
                            OP("vector", "tensor_scalar", [A_, btok, bSm], [bSm], out=kdec32[0:64, :], in0=ktok[par][0:64, 0, :], scalar1=kdecs_a[0:64, 16, h:h + 1], scalar2=None, op0=ALU.mult)
                            OP("vector", "tensor_copy", [B_, bSm], [bSm], out=aq32[0:64, 0:64], in_=aqkT[par][0:64, hl, 0:64])
                            OP("vector", "memset", [boacc], [boacc], ap=oacc[0:64, :], constant=0.0)
                            p2, p2b = psb[7], psB[7]
                            stt = {}
                            for it_ in range(NB + 2):
                                if it_ < NB:
                                    b = it_
                                    s0t, s0b = s0_r.get()
                                    P.dma("scalar", s0t[:, :], sdelta_in[j, b, h], writes=[s0b])
                                    p1, p1b = ps_get()
                                    OP("tensor", "matmul", [bSm, s0b], [p1b], out=p1[0:64, 0:128], lhsT=wT32[:, hl, :], rhs=s0t[:, :], start=True, stop=True)
                                    OP("tensor", "matmul", [bSm, s0b], [p1b], out=p1[0:64, 128:256], lhsT=qgT32[:, 0:64], rhs=s0t[:, :], start=True, stop=True)
                                    stt[b] = {"s0t": s0t, "s0b": s0b, "p1": p1, "p1b": p1b}
                                if 0 <= it_ - 1 < NB:
                                    b = it_ - 1
                                    d_ = stt[b]
                                    p1, p1b = d_["p1"], d_["p1b"]
                                    tm, tmb = tmp_r.get()
                                    OP("vector", "tensor_tensor", [p1b, B_, tmb], [tmb], out=tm[0:64, :], in0=u32[par][0:64, hl, :], in1=p1[0:64, 0:128], op=ALU.subtract)
                                    ub, ubb = unb_r.get()
                                    OP("vector", "tensor_scalar", [tmb, bset, ubb], [ubb], out=ub[0:64, :], in0=tm[0:64, :], scalar1=bmask16[0:64, b:b + 1], scalar2=None, op0=ALU.mult)
                                    OP("vector", "scalar_tensor_tensor", [p1b, bset, boacc], [boacc], out=oacc[0:64, :], in0=p1[0:64, 128:256], scalar=bmask16[0:64, b:b + 1], in1=oacc[0:64, :],
                                       op0=ALU.mult, op1=ALU.add)
                                    OP("tensor", "matmul", [bSm, ubb], [p2b], out=p2[0:64, 0:128], lhsT=aq32[0:64, 0:64], rhs=ub[0:64, :], start=(b == 0), stop=(b == NB - 1))
                                    p3, p3b = ps_get()
                                    OP("tensor", "matmul", [bSm, ubb], [p3b], out=p3[:, 0:128], lhsT=kdec32[0:64, :], rhs=ub[0:64, :], start=True, stop=True)
                                    d_["p3"], d_["p3b"] = p3, p3b
                                if 0 <= it_ - 2 < NB:
                                    b = it_ - 2
                                    d_ = stt.pop(b)
                                    sn, snb = sn_r.get()
                                    OP("vector", "scalar_tensor_tensor", [d_["p3b"], B_, d_["s0b"], snb], [snb], out=sn[:, :], in0=d_["s0t"][:, :], scalar=eGB[par][:, hl, 4 * b + 3:4 * b + 4],
                                       in1=d_["p3"][:, 0:128], op0=ALU.mult, op1=ALU.add)
                                    P.dma("sync", deltas_out[j, b, h], sn[:, :], reads=[snb])
                            OP("vector", "tensor_tensor", [p2b, boacc, O_], [O_], out=otok[par][0:64, 0, hl, :], in0=p2[0:64, 0:128], in1=oacc[0:64, :], op=ALU.add)
                    o8 = otok[par][0:bs, 0:nb, :, :].rearrange("p b h d -> p (b h) d")
                    nh = NP_
                    OP("scalar", "activation", [O_, bD], [bD], out=on[0:bs, 0:nh, :], in_=o8, func=AF.Square)
                    OP("vector", "tensor_reduce", [bD], [bD], out=ssq[0:bs, 0:nh], in_=on[0:bs, 0:nh, :], axis=AX.X, op=ALU.add)
                    OP("scalar", "activation", [bD, bset], [bD], out=ssq[0:bs, 0:nh], in_=ssq[0:bs, 0:nh], func=AF.Ln, bias=eps2[0:bs, :], scale=1.0 / 128)
                    OP("scalar", "activation", [bD], [bD], out=ssq[0:bs, 0:nh], in_=ssq[0:bs, 0:nh], func=AF.Exp, scale=-0.5)
                    OP("vector", "tensor_tensor", [O_, bD], [bD], out=on[0:bs, 0:nh, :], in0=o8, in1=ap4(ssq, 0, bs, ssq[0:1, 0:1], [[1, nh], [0, 128]]), op=ALU.mult)
                    OP("vector", "tensor_tensor", [bD, bset], [bD], out=on[0:bs, 0:nh, :], in0=on[0:bs, 0:nh, :], in1=ap4(nwB, 0, bs, nwB[0:1, 0:1], [[0, nh], [1, 128]]), op=ALU.mult)
                    OP("vector", "tensor_tensor", [bD, bZ[par]], [bD], out=on[0:bs, 0:nh, :], in0=on[0:bs, 0:nh, :], in1=zs[par][0:bs, 0:nb, :].rearrange("p b (h d) -> p (b h) d", d=128), op=ALU.mult)
                    pso, psob = ps_get()
                    for hl in range(2):
                        for blk in range(nb):
                            OP("tensor", "transpose", [bD, bident], [psob], out=pso[:, hl * 256 + blk * bs:hl * 256 + (blk + 1) * bs], in_=on[0:bs, blk * 2 + hl, :], identity=ident[0:bs, 0:bs])
                    OP("scalar", "copy", [psob, bD], [bD], out=oTb[:, :, 0:n], in_=pso[:, :].rearrange("p (h t) -> p h t", t=256)[:, :, 0:n])
                    for c in range(8):
                        ps, pb = ps_get()
                        for hl in range(2):
                            OP("tensor", "matmul", [wOb, bD], [pb], out=ps[:, 0:n], lhsT=wO[:, (c // 4) * 2 + hl, (c % 4) * 128:(c % 4 + 1) * 128], rhs=oTb[:, hl, 0:n],
                               start=(hl == 0), stop=(hl == 1))
                        OP("vector", "tensor_tensor", [pb, bx[ti][c]], [bx[ti][c]], out=xT[:, c, t0:t0 + n], in0=ps[:, 0:n], in1=xT[:, c, t0:t0 + n], op=ALU.add)
                for hl in range(2):
                    P.dma("sync", deltap_out[j, 2 * hg + hl], S32[:, hl, :], reads=[bS])

        load_x()
        for layer in range(DEPTH):
            if layer % 2 == 0 and stage >= 3:
                arena_reset()
                norm_alloc()
                rmsnorm_h(layer)
                arena_reset()
                deltanet(layer // 2)
            if layer % 2 == 1 and stage >= 2:
                arena_reset()
                norm_alloc()
                rmsnorm_h(layer)
                arena_reset()
                swa(layer // 2)
            arena_reset()
            norm_alloc()
            rmsnorm_h(4 + layer)
            arena_reset()
            ffn(layer)
        arena_reset()
        norm_alloc()
        final_out()
        P.emit()
    return nc


def core_inputs(inputs, c):
    f = lambda a: np.ascontiguousarray(np.asarray(a, dtype=np.float32))
    sl = slice(NB * c, NB * (c + 1))
    return {
        "xp": f(inputs["x_prompt"][c]),
        "xs": f(inputs["x_sample"][sl].reshape(TS, D)),
        "norm_mix": f(inputs["norm_mix"]),
        "norm_ffn": f(inputs["norm_ffn"]),
        "norm_final": f(inputs["norm_final"]).reshape(1, D),
        "ffn_w_gu": f(inputs["ffn_w_gu"]),
        "ffn_w_down": f(inputs["ffn_w_down"]),
        "swa_w_qkv": f(inputs["swa_w_qkv"]),
        "swa_b_qkv": f(inputs["swa_b_qkv"]),
        "swa_sinks": f(inputs["swa_sinks"]),
        "swa_w_o": f(inputs["swa_w_o"]),
        "swa_b_o": f(inputs["swa_b_o"]),
        "dn_w_in": f(inputs["dn_w_in"]),
        "dn_conv_w": f(inputs["dn_conv_w"]),
        "dn_a_log": f(inputs["dn_a_log"]),
        "dn_dt_bias": f(inputs["dn_dt_bias"]),
        "dn_norm_w": f(inputs["dn_norm_w"]),
        "dn_w_out": f(inputs["dn_w_out"]),
        "state_conv": f(inputs["state_conv"][:, sl].reshape(2, NB * 3, 4096)),
        "state_delta": f(inputs["state_delta"][:, sl]),
        "cache_k": f(inputs["cache_k"][:, sl].reshape(2, NB, 128, 256)),
        "cache_v": f(inputs["cache_v"][:, sl].reshape(2, NB, 128, 256)),
    }


def kernel(**inputs):
    nc = build_program()
    in_maps = [core_inputs(inputs, c) for c in range(NCORES)]
    res = run_bass_kernel_spmd(nc, in_maps, core_ids=list(range(NCORES)))
    R = res.results
    f = lambda a: np.asarray(a, dtype=np.float32)
    y_prompt = np.stack([f(R[c]["yp"]) for c in range(NCORES)], 0)
    y_sample = np.concatenate([f(R[c]["ys"]).reshape(NB, 4, D) for c in range(NCORES)], 0)
    conv_p = np.stack([f(R[c]["convp"]) for c in range(NCORES)], 1)
    delta_p = np.stack([f(R[c]["deltap"]) for c in range(NCORES)], 1)
    k_p = np.stack([f(R[c]["kp"]).reshape(2, 128, 4, 64) for c in range(NCORES)], 1)
    v_p = np.stack([f(R[c]["vp"]).reshape(2, 128, 4, 64) for c in range(NCORES)], 1)
    conv_s = np.concatenate([f(R[c]["convs"]) for c in range(NCORES)], 1)
    delta_s = np.concatenate([f(R[c]["deltas"]) for c in range(NCORES)], 1)
    k_s = np.concatenate([f(R[c]["ks"]).reshape(2, NB, 4, 4, 64) for c in range(NCORES)], 1)
    v_s = np.concatenate([f(R[c]["vs"]).reshape(2, NB, 4, 4, 64) for c in range(NCORES)], 1)
    return (y_prompt, y_sample, conv_p, delta_p, k_p, v_p, conv_s, delta_s, k_s, v_s)
```
